# Optimizing a Trainium2 kernel written in Bass

```python
import math
import jax
import jax.numpy as jnp
from jax import lax
import numpy as np

D_MODEL = 1024
BATCH = 16
SEQ = 2048
DEPTH = 1

GRID_W = 64
CTX_LEN = 256
N_MOD = 9
D_FF = 2816
RMS_EPS = 1e-6

RWKV_HEADS = 8
RWKV_HEAD_DIM = 64
RWKV_WIDTH = RWKV_HEADS * RWKV_HEAD_DIM
DECAY_LORA = 64
AAA_LORA = 64
GATE_LORA = 128
RWKV_COLS = 3 * RWKV_WIDTH + 2 * DECAY_LORA + 2 * AAA_LORA + GATE_LORA
RWKV_SPLITS = [RWKV_WIDTH, 2 * RWKV_WIDTH, 3 * RWKV_WIDTH,
               3 * RWKV_WIDTH + 2 * DECAY_LORA,
               3 * RWKV_WIDTH + 2 * DECAY_LORA + 2 * AAA_LORA]
RWKV_GN_EPS = 64e-5
SHIFT_TAPS = 3

DIFF_HEADS = 4
DIFF_HEAD_DIM = 64
DIFF_V_DIM = 2 * DIFF_HEAD_DIM
DIFF_WIDTH = DIFF_HEADS * DIFF_V_DIM
DIFF_COLS = 3 * DIFF_WIDTH
Q_BLOCK = 128
ROPE_BASE = 10000.0
ROPE_FREQS = DIFF_HEAD_DIM // 4

N_BRANCH = 2
MIX_COLS = RWKV_COLS + DIFF_COLS + N_BRANCH * D_MODEL

kernel_name = "hybrid_rwkv7_diffattn_macaron_dit_layer"


def _rms(x, g):
    xf = x.astype(jnp.float32)
    y = xf * lax.rsqrt(jnp.mean(xf * xf, axis=-1, keepdims=True) + RMS_EPS)
    return (y * g).astype(x.dtype)


def _modulate(h, shift, scale):
    return h * (1.0 + scale) + shift


def _adaln(cond, w, b):
    m = jax.nn.silu(cond) @ w + b
    return m.reshape(m.shape[:-1] + (N_MOD, D_MODEL))


def _swiglu(h, w_in, w_out):
    gate, up = jnp.split(h @ w_in, 2, axis=-1)
    return (jax.nn.silu(gate) * up) @ w_out


def _ffn_half_step(x, m, pre_g, post_g, w_in, w_out):
    h = _modulate(_rms(x, pre_g), m[..., 0, :], m[..., 1, :])
    return x + 0.5 * m[..., 2, :] * _rms(_swiglu(h, w_in, w_out), post_g)


def _short_conv(u, w):
    up = jnp.pad(u, ((0, 0), (1, 1), (0, 0)))
    return up[:, :-2] * w[0] + up[:, 1:-1] * w[1] + up[:, 2:] * w[2]


def _axial_angles(rows):
    row = jnp.repeat(jnp.arange(rows, dtype=jnp.float32), GRID_W)
    col = jnp.tile(jnp.arange(GRID_W, dtype=jnp.float32), rows)
    freqs = ROPE_BASE ** (-jnp.arange(ROPE_FREQS, dtype=jnp.float32) / ROPE_FREQS)
    return row[:, None] * freqs, col[:, None] * freqs


def _rotate(z, ang):
    c = jnp.cos(ang)[None, :, None, None, :]
    s = jnp.sin(ang)[None, :, None, None, :]
    z1, z2 = jnp.split(z, 2, axis=-1)
    return jnp.concatenate([z1 * c - z2 * s, z2 * c + z1 * s], axis=-1)


def _rope_2d(z, ang_row, ang_col):
    zr, zc = jnp.split(z, 2, axis=-1)
    return jnp.concatenate([_rotate(zr, ang_row), _rotate(zc, ang_col)], axis=-1).astype(z.dtype)


def _rwkv_streams(u, shift_w, w0, w2, a0, a2, g2, k_k, k_a):
    u = _short_conv(u, shift_w)
    B, T, _ = u.shape
    r, k, v, wd, ad, gd = jnp.split(u, RWKV_SPLITS, axis=-1)
    wd = wd.reshape(B, T, 2, DECAY_LORA)
    ad = ad.reshape(B, T, 2, AAA_LORA)
    w_logit = (w0 + jnp.einsum("btdr,drc->btdc", jnp.tanh(wd), w2)).astype(jnp.float32)
    decay = jnp.exp(-jnp.exp(-jax.nn.softplus(-w_logit) - 0.5))
    a = jax.nn.sigmoid(a0 + jnp.einsum("btdr,drc->btdc", ad, a2))
    g = jax.nn.sigmoid(gd) @ g2
    kk = (k * k_k).astype(jnp.float32).reshape(B, T, RWKV_HEADS, RWKV_HEAD_DIM)
    kk = kk * lax.rsqrt(jnp.sum(kk * kk, axis=-1, keepdims=True) + 1e-12)
    k_dir = k[:, :, None, :] * (1.0 + (a - 1.0) * k_a)
    heads = lambda t: t.reshape(t.shape[:-1] + (RWKV_HEADS, RWKV_HEAD_DIM))
    return heads(r), heads(v), kk, heads(k_dir), heads(decay), heads(a), g


def _rwkv7_scan(s0, r, w, k, v, kk, a, reverse, collect):
    xs = tuple(jnp.moveaxis(t.astype(jnp.float32), 1, 0) for t in (r, w, k, v, kk, a))

    def step(s, inp):
        r_t, w_t, k_t, v_t, kk_t, a_t = inp
        sa = jnp.einsum("bhvk,bhk->bhv", s, kk_t)
        s = (s * w_t[:, :, None, :] - sa[..., None] * (kk_t * a_t)[:, :, None, :]
             + v_t[..., None] * k_t[:, :, None, :])
        y = jnp.einsum("bhvk,bhk->bhv", s, r_t) if collect else None
        return s, y

    s_fin, ys = lax.scan(step, s0, xs, reverse=reverse)
    return s_fin, (jnp.moveaxis(ys, 0, 1) if collect else None)


def _rwkv_readout(y, r, v, k_dir, g, r_k, ln_g, ln_b):
    B, T = y.shape[:2]
    mu = jnp.mean(y, axis=-1, keepdims=True)
    var = jnp.mean(jnp.square(y - mu), axis=-1, keepdims=True)
    yn = ((y - mu) * lax.rsqrt(var + RWKV_GN_EPS)).reshape(B, T, RWKV_WIDTH) * ln_g + ln_b
    bonus = jnp.sum(r[:, :, None] * k_dir * r_k, axis=(2, 4))[..., None] * v
    return ((yn + bonus.reshape(B, T, RWKV_WIDTH)) * g).astype(g.dtype)


def _rwkv_bidir(streams, s0, collect, r_k, ln_g, ln_b):
    r, v, kk, k_dir, decay, a, g = streams
    finals, ys = [], []
    for d, rev in enumerate((False, True)):
        s_fin, y = _rwkv7_scan(s0[d], r, decay[:, :, d], k_dir[:, :, d], v, kk, a[:, :, d], rev, collect)
        finals.append(s_fin)
        ys.append(y)
    if not collect:
        return finals, None
    return finals, _rwkv_readout(ys[0] + ys[1], r, v, k_dir, g, r_k, ln_g, ln_b)


def _diff_split(u):
    B, T, _ = u.shape
    q, k, v = jnp.split(u, 3, axis=-1)
    return (q.reshape(B, T, DIFF_HEADS, 2, DIFF_HEAD_DIM),
            k.reshape(B, T, DIFF_HEADS, 2, DIFF_HEAD_DIM),
            v.reshape(B, T, DIFF_HEADS, DIFF_V_DIM))


def _diff_attend(q, k, v, lam):
    s = jnp.einsum("bqhcd,bkhcd->bhcqk", q, k).astype(jnp.float32) * DIFF_HEAD_DIM ** -0.5
    p = jax.nn.softmax(s, axis=-1)
    attn = (p[:, :, 0] - lam * p[:, :, 1]).astype(v.dtype)
    return jnp.einsum("bhqk,bkhe->bqhe", attn, v)


def _diff_out(o, subln_g, lam_init):
    B, T = o.shape[:2]
    return (_rms(o, subln_g) * (1.0 - lam_init)).reshape(B, T, DIFF_WIDTH)


def _gated_merge(ya, yb, gate_cols, up_a, up_b, w_out):
    ga, gb = jnp.split(jax.nn.sigmoid(gate_cols), N_BRANCH, axis=-1)
    return (ga * (ya @ up_a) + gb * (yb @ up_b)) @ w_out


def setup_inputs(seed: int = 0) -> dict:
    key = jax.random.key(seed)
    ks = iter(jax.random.split(key, 32))
    f32 = jnp.float32
    L, D = DEPTH, D_MODEL

    def nrm(shape, s):
        return jax.random.normal(next(ks), shape, f32) * s

    side = jax.random.uniform(next(ks), (L, 2, RWKV_COLS), f32, 0.0, 0.5)
    shift_w = jnp.stack([side[:, 0], 1.0 - 0.5 * (side[:, 0] + side[:, 1]), side[:, 1]], axis=1)
    return {
        "x": nrm((BATCH, SEQ, D), 1.0),
        "c": nrm((BATCH, D), 1.0),
        "ctx": nrm((BATCH, CTX_LEN, D), 1.0),
        "c_ctx": nrm((D,), 1.0),
        "ada_w": nrm((L, D, N_MOD * D), 0.5 * D ** -0.5),
        "ada_b": nrm((L, N_MOD * D), 0.02),
        "pre_norm_g": 1.0 + nrm((L, 3, D), 0.1),
        "post_norm_g": 1.0 + nrm((L, 3, D), 0.1),
        "ffn1_w_in": nrm((L, D, 2 * D_FF), D ** -0.5),
        "ffn1_w_out": nrm((L, D_FF, D), D_FF ** -0.5),
        "mix_w_in": nrm((L, D, MIX_COLS), D ** -0.5),
        "rwkv_shift_w": shift_w,
        "rwkv_w0": jax.random.uniform(next(ks), (L, 2, RWKV_WIDTH), f32, -6.0, 1.0),
        "rwkv_w2": nrm((L, 2, DECAY_LORA, RWKV_WIDTH), 0.5 * DECAY_LORA ** -0.5),
        "rwkv_a0": nrm((L, 2, RWKV_WIDTH), 0.3),
        "rwkv_a2": nrm((L, 2, AAA_LORA, RWKV_WIDTH), 0.5 * AAA_LORA ** -0.5),
        "rwkv_g2": nrm((L, GATE_LORA, RWKV_WIDTH), GATE_LORA ** -0.5),
        "rwkv_k_k": 0.85 + nrm((L, RWKV_WIDTH), 0.05),
        "rwkv_k_a": 1.0 + nrm((L, RWKV_WIDTH), 0.05),
        "rwkv_r_k": nrm((L, RWKV_HEADS, RWKV_HEAD_DIM), 0.1),
        "rwkv_ln_g": 1.0 + nrm((L, RWKV_WIDTH), 0.1),
        "rwkv_ln_b": nrm((L, RWKV_WIDTH), 0.02),
        "diff_lambda": nrm((L, 4, DIFF_HEAD_DIM), 0.1),
        "diff_subln_g": 1.0 + nrm((L, DIFF_V_DIM), 0.1),
        "branch_up_a": nrm((L, RWKV_WIDTH, D), RWKV_WIDTH ** -0.5),
        "branch_up_b": nrm((L, DIFF_WIDTH, D), DIFF_WIDTH ** -0.5),
        "mix_w_out": nrm((L, D, D), D ** -0.5),
        "ffn2_w_in": nrm((L, D, 2 * D_FF), D ** -0.5),
        "ffn2_w_out": nrm((L, D_FF, D), D_FF ** -0.5),
    }


def reference(x, c, ctx, c_ctx, ada_w, ada_b, pre_norm_g, post_norm_g,
              ffn1_w_in, ffn1_w_out, mix_w_in, rwkv_shift_w, rwkv_w0, rwkv_w2,
              rwkv_a0, rwkv_a2, rwkv_g2, rwkv_k_k, rwkv_k_a, rwkv_r_k,
              rwkv_ln_g, rwkv_ln_b, diff_lambda, diff_subln_g,
              branch_up_a, branch_up_b, mix_w_out, ffn2_w_in, ffn2_w_out):
    B, T, _ = x.shape
    rows = T // GRID_W
    ang_row, ang_col = _axial_angles(rows)
    zero_state = jnp.zeros((B, RWKV_HEADS, RWKV_HEAD_DIM, RWKV_HEAD_DIM), jnp.float32)
    split_cols = [RWKV_COLS, RWKV_COLS + DIFF_COLS]
    n_blk = T // Q_BLOCK

    for l in range(DEPTH):
        last = l == DEPTH - 1
        lam_init = 0.8 - 0.6 * math.exp(-0.3 * l)
        m_x = _adaln(c, ada_w[l], ada_b[l])[:, None]
        m_c = _adaln(c_ctx, ada_w[l], ada_b[l])[None, None]

        x = _ffn_half_step(x, m_x[..., 0:3, :], pre_norm_g[l, 0], post_norm_g[l, 0],
                           ffn1_w_in[l], ffn1_w_out[l])
        ctx = _ffn_half_step(ctx, m_c[..., 0:3, :], pre_norm_g[l, 0], post_norm_g[l, 0],
                             ffn1_w_in[l], ffn1_w_out[l])

        hx = _modulate(_rms(x, pre_norm_g[l, 1]), m_x[..., 3, :], m_x[..., 4, :])
        hc = _modulate(_rms(ctx, pre_norm_g[l, 1]), m_c[..., 3, :], m_c[..., 4, :])
        rx, dx, gx = jnp.split(hx @ mix_w_in[l], split_cols, axis=-1)
        rc, dc, gc = jnp.split(hc @ mix_w_in[l], split_cols, axis=-1)

        rwkv_p = (rwkv_shift_w[l], rwkv_w0[l], rwkv_w2[l], rwkv_a0[l], rwkv_a2[l],
                  rwkv_g2[l], rwkv_k_k[l], rwkv_k_a[l])
        read_p = (rwkv_r_k[l], rwkv_ln_g[l], rwkv_ln_b[l])
        st_c = _rwkv_streams(rc, *rwkv_p)
        st_x = _rwkv_streams(rx, *rwkv_p)
        ctx_states, ya_c = _rwkv_bidir(st_c, (zero_state, zero_state), not last, *read_p)
        _, ya_x = _rwkv_bidir(st_x, ctx_states, True, *read_p)

        lq1, lk1, lq2, lk2 = diff_lambda[l].astype(jnp.float32)
        lam = jnp.exp(jnp.sum(lq1 * lk1)) - jnp.exp(jnp.sum(lq2 * lk2)) + lam_init
        qc, kc, vc = _diff_split(dc)
        qx, kx, vx = _diff_split(dx)
        qx = _rope_2d(qx, ang_row, ang_col)
        kx = _rope_2d(kx, ang_row, ang_col)
        k_all = jnp.concatenate([kc, kx], axis=1)
        v_all = jnp.concatenate([vc, vx], axis=1)
        q_blk = jnp.moveaxis(qx.reshape(B, n_blk, Q_BLOCK, DIFF_HEADS, 2, DIFF_HEAD_DIM), 1, 0)
        o_x = lax.map(lambda qb: _diff_attend(qb, k_all, v_all, lam), q_blk)
        o_x = jnp.moveaxis(o_x, 0, 1).reshape(B, T, DIFF_HEADS, DIFF_V_DIM)
        yb_x = _diff_out(o_x, diff_subln_g[l], lam_init)

        mix_x = _gated_merge(ya_x, yb_x, gx, branch_up_a[l], branch_up_b[l], mix_w_out[l])
        x = x + m_x[..., 5, :] * _rms(mix_x, post_norm_g[l, 1])

        if not last:
            yb_c = _diff_out(_diff_attend(qc, kc, vc, lam), diff_subln_g[l], lam_init)
            mix_c = _gated_merge(ya_c, yb_c, gc, branch_up_a[l], branch_up_b[l], mix_w_out[l])
            ctx = ctx + m_c[..., 5, :] * _rms(mix_c, post_norm_g[l, 1])
            ctx = _ffn_half_step(ctx, m_c[..., 6:9, :], pre_norm_g[l, 2], post_norm_g[l, 2],
                                 ffn2_w_in[l], ffn2_w_out[l])

        x = _ffn_half_step(x, m_x[..., 6:9, :], pre_norm_g[l, 2], post_norm_g[l, 2],
                           ffn2_w_in[l], ffn2_w_out[l])
    return x
```

```python
import types
import numpy as np
from contextlib import ExitStack
import concourse.bass as bass
import concourse.mybir as mybir
from concourse.bass_utils import run_bass_kernel_spmd

F32 = mybir.dt.float32
BF16 = mybir.dt.bfloat16
AF = mybir.ActivationFunctionType
ALU = mybir.AluOpType
AX = mybir.AxisListType

D = 1024
KD = 8
FF = 2816
KF = 22
NMOD = 9
MIXC = 5504
TT = 256
C0 = 0.6065306597126334

PV_ADAB, PV_PRE, PV_POST, PV_CONV, PV_W0, PV_A0, PV_KK, PV_KA, PV_RK, NPV = 0, 72, 96, 120, 165, 173, 181, 185, 189, 194
PV_SUBG = 193
BR_LNG, BR_LNB, BR_SUB, BR_LAM, NBR = 0, 512, 1024, 1152, 1408
CS_ID, CS_FS, CS_FI, CS_BS, CS_BI, CS_BONES, CS_IND, NCONST = 0, 128, 256, 384, 512, 640, 768, 770


def _freeze(fn):
    if fn is None or fn.__closure__ is None:
        return fn
    cells = []
    for c in fn.__closure__:
        try:
            cells.append(types.CellType(c.cell_contents))
        except ValueError:
            cells.append(c)
    return types.FunctionType(fn.__code__, fn.__globals__, fn.__name__, fn.__defaults__, tuple(cells))


class Sched:
    ENGS = ("pe", "act", "dve", "pool", "sp")
    LIM = 30000

    def __init__(self, nc, stack):
        self.nc, self.stack = nc, stack
        self.prog = {e: [] for e in self.ENGS}
        self.sems = {}
        self.cnt = {}
        self.known = {e: {} for e in self.ENGS}
        self.lw = {}
        self.lr = {}
        self.nins = 0

    def sem(self, key, ep):
        k = (key, ep)
        if k not in self.sems:
            self.sems[k] = self.stack.enter_context(self.nc.semaphore("s_%s_%d" % (key, ep)))
        return self.sems[k]

    def _next(self, key, amt, commit):
        ep, v = self.cnt.get(key, (0, 0))
        if v + amt > self.LIM:
            ep, v = ep + 1, 0
        v += amt
        if commit:
            self.cnt[key] = (ep, v)
        return (key, ep, v)

    def op(self, eng, fn, reads=(), writes=(), inc=True, dsem=None):
        fn = _freeze(fn)
        evs = []
        for r in reads:
            if r in self.lw:
                evs.append(self.lw[r])
        for w in writes:
            if w in self.lw:
                evs.append(self.lw[w])
            for k, (ep, v) in self.lr.get(w, {}).items():
                evs.append((k, ep, v))
        need = {}
        for (k, ep, v) in evs:
            if k == eng and eng == "pe":
                continue
            if self.known[eng].get(k, (-1, 0)) >= (ep, v):
                continue
            if need.get(k, (-1, 0)) < (ep, v):
                need[k] = (ep, v)
        for k, ev in need.items():
            self.known[eng][k] = ev
        if dsem is not None:
            my = self._next(dsem, 16, True)
        else:
            my = self._next(eng, 1, inc)
        self.prog[eng].append((fn, [(k, ep, v) for k, (ep, v) in need.items()],
                               my if (inc or dsem is not None) else None, dsem is not None))
        self.nins += 1
        for r in reads:
            d = self.lr.setdefault(r, {})
            if d.get(my[0], (-1, 0)) < my[1:]:
                d[my[0]] = my[1:]
        for w in writes:
            self.lw[w] = my
            self.lr[w] = {}
        return my

    def barrier(self):
        for eng in self.ENGS:
            need = []
            for k, (ep, v) in self.cnt.items():
                if v == 0:
                    continue
                if k == eng:
                    continue
                if self.known[eng].get(k, (-1, 0)) >= (ep, v):
                    continue
                self.known[eng][k] = (ep, v)
                need.append((k, ep, v))
            if need:
                self.prog[eng].append((None, need, None, False))

    def wait_all(self, eng):
        need = []
        for k, (ep, v) in self.cnt.items():
            if v == 0 or self.known[eng].get(k, (-1, 0)) >= (ep, v):
                continue
            self.known[eng][k] = (ep, v)
            need.append((k, ep, v))
        if need:
            self.prog[eng].append((None, need, None, False))

    def emit(self):
        for key, (ep, v) in list(self.cnt.items()):
            for e in range(ep + 1):
                self.sem(key, e)
        for eng in self.ENGS:
            for (_, need, my, _) in self.prog[eng]:
                for (k, ep, v) in need:
                    self.sem(k, ep)
        with self.nc.Block() as blk:
            names = {"pe": "tensor", "act": "scalar", "dve": "vector", "pool": "gpsimd", "sp": "sync"}
            for eng in self.ENGS:
                prog = self.prog[eng]

                def body(e, prog=prog):
                    for fn, need, my, isdma in prog:
                        for (k, ep, v) in need:
                            e.wait_ge(self.sems[(k, ep)], v)
                        if fn is None:
                            continue
                        ins = fn(e)
                        if my is not None:
                            ins.then_inc(self.sems[(my[0], my[1])], 16 if isdma else 1)
                getattr(blk, names[eng])(body)
        self.prog = {e: [] for e in self.ENGS}


def build(cfg):
    T, CTX, NB = cfg["T"], cfg["CTX"], cfg["NB"]
    stop = cfg.get("stop", "all")
    dbg = cfg.get("dbg", False)
    E = CTX + T
    NR = NB + 1
    assert T % TT == 0 and CTX % 128 == 0
    nc = bass.Bass("TRN2", target_bir_lowering=False)

    def din(name, shape, dt=F32):
        return nc.dram_tensor(name, list(shape), dt, kind="ExternalInput").ap()

    x_d = din("x", [NB, T, D])
    ctx_d = din("ctx", [NB, CTX, D])
    cond_d = din("condT", [128, KD, NR])
    pvec_d = din("pvec", [128, NPV])
    brow_d = din("brow", [1, NBR])
    consts_d = din("consts", [128, NCONST])
    cos_d = din("cosT", [128, T])
    sin_d = din("sinT", [128, T])
    ada_w_d = din("ada_w", [D, NMOD * D])
    f1_in_d = din("ffn1_w_in", [D, 2 * FF])
    f1_out_d = din("ffn1_w_out", [FF, D])
    f2_in_d = din("ffn2_w_in", [D, 2 * FF])
    f2_out_d = din("ffn2_w_out", [FF, D])
    mix_in_d = din("mix_w_in", [D, MIXC])
    qk_sw_d = din("w_qk_swap", [D, 1024])
    w2_d = din("rwkv_w2", [2, 64, 512])
    a2_d = din("rwkv_a2", [2, 64, 512])
    g2_d = din("rwkv_g2", [128, 512])
    upa_d = din("branch_up_a", [512, D])
    upb_d = din("branch_up_b", [512, D])
    mo_d = din("mix_w_out", [D, D])
    out_d = nc.dram_tensor("out", [NB, T, D], F32, kind="ExternalOutput").ap()

    okind = "ExternalOutput" if dbg else "Internal"
    x1T_d = nc.dram_tensor("x1T", [NB, KD, 128, E], F32, kind=okind).ap()
    x2T_d = nc.dram_tensor("x2T", [NB, KD, 128, T], F32, kind=okind).ap()
    hT_d = nc.dram_tensor("hT_s", [NB, KD, 128, E], BF16, kind=okind).ap()
    yaT_d = nc.dram_tensor("yaT_s", [NB, 4, 128, T], BF16, kind=okind).ap()
    ybT_d = nc.dram_tensor("ybT_s", [NB, 4, 128, T], BF16, kind=okind).ap()
    mod_dbg = nc.dram_tensor("mod_dbg", [128, 72 * NR], F32, kind=okind).ap()

    stack = ExitStack()
    with stack:
        S = Sched(nc, stack)

        uid = [0]

        def sb(name, shape, dt=F32, st=stack):
            uid[0] += 1
            return st.enter_context(nc.sbuf_tensor("t%d_%s" % (uid[0], name), list(shape), dt))

        pvec = sb("pvec", [128, NPV])
        brow = sb("brow", [128, NBR])
        cst = sb("cst", [128, NCONST])
        modT = sb("modT", [128, 72, NR])
        GE = sb("GE", [128, 3, NR, KD])
        CG = sb("CG", [128, 3, NR, KD])
        ident_bf = sb("ident_bf", [128, 128], BF16)
        onesD = sb("onesD", [128, 128], BF16)
        lam = sb("lam", [128, 4])
        epsc = sb("epsc", [128, 4])
        ones_f = sb("ones_f", [128, 128])
        psb = [stack.enter_context(nc.psum_tensor("ps%d" % i, [128, 512], F32)) for i in range(8)]
        pst = {"i": 0}

        pst["set"] = list(range(8))

        def bank():
            st_ = pst["set"]
            pst["i"] = (pst["i"] + 1) % len(st_)
            return st_[pst["i"]]

        ident32 = cst[:, CS_ID:CS_ID + 128]

        with ExitStack() as ph:
            condT = sb("condT", [128, KD, NR], st=ph)
            siluT = sb("siluT", [128, KD, NR], st=ph)
            wblk = [sb("wblk%d" % i, [128, KD, 1152], st=ph) for i in range(2)]
            S.op("sp", lambda e: e.dma_start(out=pvec[:], in_=pvec_d), writes=["pvec"], dsem="d_pvec")
            S.op("sp", lambda e: e.dma_start(out=brow[:], in_=brow_d.partition_broadcast(128)[:, 0, :]),
                 writes=["brow"], dsem="d_brow")
            S.op("sp", lambda e: e.dma_start(out=cst[:], in_=consts_d), writes=["cst"], dsem="d_cst")
            S.op("sp", lambda e: e.dma_start(out=condT[:], in_=cond_d), writes=["condT"], dsem="d_cond")
            S.op("act", lambda e: e.activation(out=siluT[:], in_=condT[:], func=AF.Silu),
                 reads=["condT"], writes=["siluT"])
            S.op("dve", lambda e: e.tensor_copy(ident_bf[:], ident32), reads=["cst"], writes=["ident_bf"])
            S.op("dve", lambda e: e.memset(onesD[:], 1.0 / D), writes=["onesD"])
            S.op("dve", lambda e: e.memset(epsc[:, 0:1], 1e-6), writes=["epsc"])
            S.op("dve", lambda e: e.memset(ones_f[:], 1.0), writes=["ones_f"])
            S.op("dve", lambda e: e.memset(epsc[:, 1:2], 64e-5), writes=["epsc"])
            S.op("dve", lambda e: e.memset(epsc[:, 2:3], 1e-12), writes=["epsc"])
            aw = ada_w_d.rearrange("(k p) n -> p k n", p=128)
            pm = bank()
            for cb in range(8):
                wb = wblk[cb % 2]
                S.op("sp", lambda e, wb=wb, cb=cb: e.dma_start(out=wb[:], in_=aw[:, :, cb * 1152:(cb + 1) * 1152]),
                     writes=[("wblk", cb % 2)], dsem="d_wblk%d" % (cb % 2))
                for jj in range(9):
                    j = cb * 9 + jj
                    for kc in range(KD):
                        S.op("pe", lambda e, wb=wb, jj=jj, j=j, kc=kc: e.matmul(
                            psb[pm][:, j * NR:(j + 1) * NR], wb[:, kc, jj * 128:(jj + 1) * 128], siluT[:, kc, :],
                            start=(kc == 0), stop=(kc == KD - 1)),
                            reads=[("wblk", cb % 2), "siluT"], writes=[("ps", pm)],
                            inc=(kc == KD - 1 and jj == 8))
            S.op("dve", lambda e: e.tensor_tensor(
                out=modT[:], in0=psb[pm][:, 0:72 * NR].rearrange("p (j r) -> p j r", r=NR),
                in1=pvec[:, PV_ADAB:PV_ADAB + 72].unsqueeze(2).to_broadcast([128, 72, NR]), op=ALU.add),
                reads=[("ps", pm), "pvec"], writes=["modT"])
            for s in range(3):
                half = 1.0 if s == 1 else 0.5
                for r in range(NR):
                    S.op("dve", lambda e, s=s, r=r: e.scalar_tensor_tensor(
                        out=GE[:, s, r, :], in0=modT[:, (3 * s + 1) * 8:(3 * s + 2) * 8, r], scalar=1.0,
                        in1=pvec[:, PV_PRE + s * 8:PV_PRE + s * 8 + 8], op0=ALU.add, op1=ALU.mult),
                        reads=["modT", "pvec"], writes=["GE"])
                    S.op("dve", lambda e, s=s, r=r, half=half: e.scalar_tensor_tensor(
                        out=CG[:, s, r, :], in0=modT[:, (3 * s + 2) * 8:(3 * s + 3) * 8, r], scalar=half,
                        in1=pvec[:, PV_POST + s * 8:PV_POST + s * 8 + 8], op0=ALU.mult, op1=ALU.mult),
                        reads=["modT", "pvec"], writes=["CG"])
            if dbg:
                S.op("sp", lambda e: e.dma_start(out=mod_dbg, in_=modT[:].rearrange("p j r -> p (j r)")),
                     reads=["modT"], writes=["mod_dbg"], dsem="d_dbg")
            S.barrier()
            S.emit()

        def ffn_phase(sub, w_in_d, w_out_d, first):
            with ExitStack() as ph:
                win = sb("win", [128, KD, 2 * FF], BF16, st=ph)
                wout = sb("wout", [128, KF, D], BF16, st=ph)
                xT = [sb("xT%d" % i, [128, KD, TT], st=ph) for i in range(2)]
                tmp = sb("tmp", [128, KD, TT], st=ph)
                hT = [sb("hT%d" % i, [128, KD, TT], BF16, st=ph) for i in range(2)]
                actT = sb("actT", [128, KF, TT], BF16, st=ph)
                sq = sb("sq", [128, KD, TT], BF16, st=ph)
                rstd = sb("rstd", [128, TT], st=ph)
                sg = [sb("sg%d" % i, [128, TT], st=ph) for i in range(2)]
                xio = sb("xio", [128, TT // 128, D], st=ph)
                wi = w_in_d.rearrange("(k p) n -> p k n", p=128)
                wo = w_out_d.rearrange("(k p) n -> p k n", p=128)
                for kc in range(KD):
                    S.op("pool", lambda e, kc=kc: e.dma_start(out=win[:, kc, :], in_=wi[:, kc, :]),
                         writes=(["W"] if kc == KD - 1 else []), dsem="d_w")
                for q in range(2):
                    S.op("pool", lambda e, q=q: e.dma_start(out=wout[:, q * 11:(q + 1) * 11, :], in_=wo[:, q * 11:(q + 1) * 11, :]),
                         writes=(["W2"] if q == 1 else []), dsem="d_w2")
                tiles = []
                for b in range(NB):
                    if first:
                        for s0 in range(0, CTX, TT):
                            tiles.append((b, NB, "ctx", s0, s0, min(TT, CTX - s0)))
                    for s0 in range(0, T, TT):
                        tiles.append((b, b, "x", s0, CTX + s0, TT))

                def load(i):
                    b, r, kind, s0, e0, n = tiles[i]
                    buf = i % 2
                    if first:
                        src = (ctx_d if kind == "ctx" else x_d)[b, s0:s0 + n, :].rearrange("(s p) d -> p s d", p=128)
                        S.op("sp", lambda e: e.dma_start(out=xio[:, 0:n // 128, :], in_=src),
                             writes=["xio"], dsem="d_xio")
                        for kp in range(KD // 2):
                            pb = bank()
                            for k2 in range(2):
                                kc = kp * 2 + k2
                                for s in range(n // 128):
                                    S.op("pe", lambda e, pb=pb, k2=k2, kc=kc, s=s: e.transpose(
                                        psb[pb][:, k2 * TT + s * 128:k2 * TT + (s + 1) * 128],
                                        xio[:, s, kc * 128:(kc + 1) * 128], ident32),
                                        reads=["xio", "cst"], writes=[("ps", pb)],
                                        inc=(k2 == 1 and s == n // 128 - 1))
                            eng = "act" if kp % 2 == 0 else "dve"
                            if eng == "act":
                                S.op("act", lambda e, pb=pb, kp=kp, buf=buf: e.activation(
                                    out=xT[buf][:, 2 * kp:2 * kp + 2, 0:n],
                                    in_=psb[pb][:].rearrange("p (k t) -> p k t", k=2)[:, :, 0:n], func=AF.Copy),
                                    reads=[("ps", pb)], writes=[("xT", buf)])
                            else:
                                S.op("dve", lambda e, pb=pb, kp=kp, buf=buf: e.tensor_copy(
                                    xT[buf][:, 2 * kp:2 * kp + 2, 0:n],
                                    psb[pb][:].rearrange("p (k t) -> p k t", k=2)[:, :, 0:n]),
                                    reads=[("ps", pb)], writes=[("xT", buf)])
                    else:
                        S.op("sp", lambda e: e.dma_start(
                            out=xT[buf][:, :, 0:n], in_=x2T_d[b, :, :, s0:s0 + n].rearrange("k p t -> p k t")),
                            reads=[("x2T", b, s0)], writes=[("xT", buf)], dsem="d_xT%d" % buf)

                def rms_stats(src_tile, n):
                    S.op("act", lambda e: e.activation(out=sq[:, :, 0:n], in_=src_tile[:, :, 0:n], func=AF.Square),
                         reads=[src_tile.name_res], writes=["sq"])
                    pb = bank()
                    for kc in range(KD):
                        S.op("pe", lambda e, kc=kc, pb=pb: e.matmul(psb[pb][:, 0:n], onesD[:], sq[:, kc, 0:n],
                                                                     start=(kc == 0), stop=(kc == KD - 1)),
                             reads=["sq", "onesD"], writes=[("ps", pb)], inc=(kc == KD - 1))
                    S.op("act", lambda e, pb=pb: e.activation(out=rstd[:, 0:n], in_=psb[pb][:, 0:n], func=AF.Sqrt,
                                                              bias=epsc[:, 0:1], scale=1.0),
                         reads=[("ps", pb), "epsc"], writes=["rstd"])
                    S.op("dve", lambda e: e.reciprocal(rstd[:, 0:n], rstd[:, 0:n]), reads=["rstd"], writes=["rstd"])

                class TV:
                    def __init__(self, t, res):
                        self.t, self.name_res = t, res

                    def __getitem__(self, k):
                        return self.t[k]

                def prenorm(i):
                    b, r, kind, s0, e0, n = tiles[i]
                    buf = i % 2
                    xb = TV(xT[buf], ("xT", buf))
                    rms_stats(xb, n)
                    S.op("dve", lambda e, buf=buf: e.tensor_tensor(
                        out=tmp[:, :, 0:n], in0=xT[buf][:, :, 0:n],
                        in1=rstd[:, 0:n].unsqueeze(1).to_broadcast([128, KD, n]), op=ALU.mult),
                        reads=[("xT", buf), "rstd"], writes=["tmp"])
                    for kc in range(KD):
                        S.op("act", lambda e, kc=kc, r=r: e.activation(
                            out=hT[buf][:, kc, 0:n], in_=tmp[:, kc, 0:n], func=AF.Identity,
                            bias=modT[:, (3 * sub) * 8 + kc, r:r + 1], scale=GE[:, sub, r, kc:kc + 1]),
                            reads=["tmp", "modT", "GE"], writes=[("hT", buf)])

                def gateup(i):
                    b, r, kind, s0, e0, n = tiles[i]
                    buf = i % 2
                    for fc in range(KF):
                        pg, pu = bank(), bank()
                        for which, pb in ((0, pg), (1, pu)):
                            for kc in range(KD):
                                S.op("pe", lambda e, which=which, pb=pb, kc=kc, fc=fc: e.matmul(
                                    psb[pb][:, 0:n], win[:, kc, which * FF + fc * 128:which * FF + (fc + 1) * 128],
                                    hT[buf][:, kc, 0:n], start=(kc == 0), stop=(kc == KD - 1)),
                                    reads=[("hT", buf), "W"], writes=[("ps", pb)], inc=(kc == KD - 1))
                        sgb = fc % 2
                        S.op("act", lambda e, pg=pg, sgb=sgb: e.activation(out=sg[sgb][:, 0:n], in_=psb[pg][:, 0:n], func=AF.Silu),
                             reads=[("ps", pg)], writes=[("sg", sgb)])
                        S.op("dve", lambda e, pu=pu, sgb=sgb, fc=fc: e.tensor_tensor(
                            out=actT[:, fc, 0:n], in0=psb[pu][:, 0:n], in1=sg[sgb][:, 0:n], op=ALU.mult),
                            reads=[("ps", pu), ("sg", sgb)], writes=["actT"])

                def outproj(i):
                    b, r, kind, s0, e0, n = tiles[i]
                    buf = i % 2
                    for dc in range(KD):
                        pb = bank()
                        for fc in range(KF):
                            S.op("pe", lambda e, pb=pb, fc=fc, dc=dc: e.matmul(
                                psb[pb][:, 0:n], wout[:, fc, dc * 128:(dc + 1) * 128], actT[:, fc, 0:n],
                                start=(fc == 0), stop=(fc == KF - 1)),
                                reads=["actT", "W2"], writes=[("ps", pb)], inc=(fc == KF - 1))
                        S.op("act", lambda e, pb=pb, dc=dc: e.activation(out=tmp[:, dc, 0:n], in_=psb[pb][:, 0:n], func=AF.Copy),
                             reads=[("ps", pb)], writes=["tmp"])
                    rms_stats(TV(tmp, "tmp"), n)
                    S.op("dve", lambda e: e.tensor_tensor(
                        out=tmp[:, :, 0:n], in0=tmp[:, :, 0:n],
                        in1=rstd[:, 0:n].unsqueeze(1).to_broadcast([128, KD, n]), op=ALU.mult),
                        reads=["tmp", "rstd"], writes=["tmp"])
                    for dc in range(KD):
                        S.op("dve", lambda e, dc=dc, buf=buf, r=r: e.scalar_tensor_tensor(
                            out=xT[buf][:, dc, 0:n], in0=tmp[:, dc, 0:n], scalar=CG[:, sub, r, dc:dc + 1],
                            in1=xT[buf][:, dc, 0:n], op0=ALU.mult, op1=ALU.add),
                            reads=["tmp", "CG", ("xT", buf)], writes=[("xT", buf)])
                    if first:
                        S.op("sp", lambda e, b=b, e0=e0, buf=buf: e.dma_start(
                            out=x1T_d[b, :, :, e0:e0 + n].rearrange("k p t -> p k t"), in_=xT[buf][:, :, 0:n]),
                            reads=[("xT", buf)], writes=[("x1T", b, e0)], dsem="d_st%d" % buf)
                    else:
                        for s in range(n // 128):
                            for hf in range(2):
                                pb = bank()
                                for k4 in range(4):
                                    kc = hf * 4 + k4
                                    S.op("pe", lambda e, pb=pb, k4=k4, kc=kc, s=s, buf=buf: e.transpose(
                                        psb[pb][:, k4 * 128:(k4 + 1) * 128], xT[buf][:, kc, s * 128:(s + 1) * 128], ident32),
                                        reads=[("xT", buf), "cst"], writes=[("ps", pb)], inc=(k4 == 3))
                                if hf == 0:
                                    S.op("act", lambda e, pb=pb, s=s: e.activation(out=xio[:, s, 0:512], in_=psb[pb][:], func=AF.Copy),
                                         reads=[("ps", pb)], writes=["xio"])
                                else:
                                    S.op("dve", lambda e, pb=pb, s=s: e.tensor_copy(xio[:, s, 512:1024], psb[pb][:]),
                                         reads=[("ps", pb)], writes=["xio"])
                        S.op("sp", lambda e, b=b, s0=s0: e.dma_start(
                            out=out_d[b, s0:s0 + n, :].rearrange("(s p) d -> p s d", p=128), in_=xio[:, 0:n // 128, :]),
                            reads=["xio"], writes=[("out", b, s0)], dsem="d_out")
                load(0)
                prenorm(0)
                for i in range(len(tiles)):
                    if i + 1 < len(tiles):
                        load(i + 1)
                    gateup(i)
                    if i + 1 < len(tiles):
                        prenorm(i + 1)
                    outproj(i)
                S.barrier()
                S.emit()


        def tile_list(with_ctx):
            tiles = []
            for b in range(NB):
                if with_ctx:
                    for s0 in range(0, CTX, TT):
                        tiles.append((b, NB, s0, min(TT, CTX - s0)))
                for s0 in range(0, T, TT):
                    tiles.append((b, b, CTX + s0, TT))
            return tiles

        def m1_phase():
            with ExitStack() as ph:
                xT = [sb("xT%d" % i, [128, KD, TT], st=ph) for i in range(2)]
                tmp = sb("tmp", [128, KD, TT], st=ph)
                hT = [sb("hT%d" % i, [128, KD, TT], BF16, st=ph) for i in range(2)]
                sq = sb("sq", [128, KD, TT], BF16, st=ph)
                rstd = sb("rstd", [128, TT], st=ph)
                tiles = tile_list(True)

                def load(i):
                    b, r, e0, n = tiles[i]
                    buf = i % 2
                    S.op("sp", lambda e: e.dma_start(
                        out=xT[buf][:, :, 0:n], in_=x1T_d[b, :, :, e0:e0 + n].rearrange("k p t -> p k t")),
                        reads=[("x1T", b, e0)], writes=[("xT", buf)], dsem="d_xT%d" % buf)

                def do_tile(i):
                    b, r, e0, n = tiles[i]
                    buf = i % 2
                    if i + 1 < len(tiles):
                        load(i + 1)
                    S.op("act", lambda e: e.activation(out=sq[:, :, 0:n], in_=xT[buf][:, :, 0:n], func=AF.Square),
                         reads=[("xT", buf)], writes=["sq"])
                    pb = bank()
                    for kc in range(KD):
                        S.op("pe", lambda e, kc=kc: e.matmul(psb[pb][:, 0:n], onesD[:], sq[:, kc, 0:n],
                                                             start=(kc == 0), stop=(kc == KD - 1)),
                             reads=["sq", "onesD"], writes=[("ps", pb)], inc=(kc == KD - 1))
                    S.op("act", lambda e: e.activation(out=rstd[:, 0:n], in_=psb[pb][:, 0:n], func=AF.Sqrt,
                                                       bias=epsc[:, 0:1], scale=1.0),
                         reads=[("ps", pb), "epsc"], writes=["rstd"])
                    S.op("dve", lambda e: e.reciprocal(rstd[:, 0:n], rstd[:, 0:n]), reads=["rstd"], writes=["rstd"])
                    S.op("dve", lambda e: e.tensor_tensor(
                        out=tmp[:, :, 0:n], in0=xT[buf][:, :, 0:n],
                        in1=rstd[:, 0:n].unsqueeze(1).to_broadcast([128, KD, n]), op=ALU.mult),
                        reads=[("xT", buf), "rstd"], writes=["tmp"])
                    for kc in range(KD):
                        S.op("act", lambda e, kc=kc: e.activation(
                            out=hT[buf][:, kc, 0:n], in_=tmp[:, kc, 0:n], func=AF.Identity,
                            bias=modT[:, 3 * 8 + kc, r:r + 1], scale=GE[:, 1, r, kc:kc + 1]),
                            reads=["tmp", "modT", "GE"], writes=[("hT", buf)])
                    S.op("sp", lambda e: e.dma_start(
                        out=hT_d[b, :, :, e0:e0 + n].rearrange("k p t -> p k t"), in_=hT[buf][:, :, 0:n]),
                        reads=[("hT", buf)], writes=[("hTd", b)], dsem="d_hst%d" % buf)
                load(0)
                for i in range(len(tiles)):
                    do_tile(i)
                S.barrier()
                S.emit()

        def m4_phase():
            with ExitStack() as ph:
                wg = sb("wg", [128, KD, 2048], BF16, st=ph)
                wua = sb("wua", [128, 4, D], BF16, st=ph)
                wub = sb("wub", [128, 4, D], BF16, st=ph)
                wmo = sb("wmo", [128, KD, D], BF16, st=ph)
                xT = [sb("xT%d" % i, [128, KD, TT], st=ph) for i in range(2)]
                hT = [sb("hT%d" % i, [128, KD, TT], BF16, st=ph) for i in range(2)]
                ya = [sb("ya%d" % i, [128, 4, TT], BF16, st=ph) for i in range(2)]
                yb = [sb("yb%d" % i, [128, 4, TT], BF16, st=ph) for i in range(2)]
                tmp = sb("tmp", [128, KD, TT], st=ph)
                mT = sb("mT", [128, KD, TT], BF16, st=ph)
                sq = sb("sq", [128, KD, TT], BF16, st=ph)
                rstd = sb("rstd", [128, TT], st=ph)
                sga = [sb("sga%d" % i, [128, TT], st=ph) for i in range(2)]
                sgb = [sb("sgb%d" % i, [128, TT], st=ph) for i in range(2)]
                t1 = [sb("t1%d" % i, [128, TT], st=ph) for i in range(2)]
                mi = mix_in_d.rearrange("(k p) n -> p k n", p=128)
                for kc in range(KD):
                    S.op("pool", lambda e, kc=kc: e.dma_start(out=wg[:, kc, :], in_=mi[:, kc, 3456:5504]),
                         writes=(["W"] if kc == KD - 1 else []), dsem="d_w")
                S.op("pool", lambda e: e.dma_start(out=wua[:], in_=upa_d.rearrange("(k p) n -> p k n", p=128)), dsem="d_w2")
                S.op("pool", lambda e: e.dma_start(out=wub[:], in_=upb_d.rearrange("(k p) n -> p k n", p=128)), dsem="d_w2")
                S.op("pool", lambda e: e.dma_start(out=wmo[:], in_=mo_d.rearrange("(k p) n -> p k n", p=128)),
                     writes=["W2"], dsem="d_w2")
                tiles = tile_list(False)

                def load(i):
                    b, r, e0, n = tiles[i]
                    s0 = e0 - CTX
                    buf = i % 2
                    S.op("sp", lambda e: e.dma_start(
                        out=xT[buf][:, :, 0:n], in_=x1T_d[b, :, :, e0:e0 + n].rearrange("k p t -> p k t")),
                        reads=[("x1T", b, e0)], writes=[("xT", buf)], dsem="d_xT%d" % buf)
                    S.op("sp", lambda e: e.dma_start(
                        out=hT[buf][:, :, 0:n], in_=hT_d[b, :, :, e0:e0 + n].rearrange("k p t -> p k t")),
                        reads=[("hTd", b)], writes=[("hT", buf)], dsem="d_hT%d" % buf)
                    S.op("sp", lambda e: e.dma_start(
                        out=ya[buf][:, :, 0:n], in_=yaT_d[b, :, :, s0:s0 + n].rearrange("k p t -> p k t")),
                        reads=[("yaT", b)], writes=[("ya", buf)], dsem="d_ya%d" % buf)
                    S.op("sp", lambda e: e.dma_start(
                        out=yb[buf][:, :, 0:n], in_=ybT_d[b, :, :, s0:s0 + n].rearrange("k p t -> p k t")),
                        reads=[("ybT", b)], writes=[("yb", buf)], dsem="d_yb%d" % buf)

                def do_tile(i):
                    b, r, e0, n = tiles[i]
                    s0 = e0 - CTX
                    buf = i % 2
                    if i + 1 < len(tiles):
                        load(i + 1)
                    for ncx in range(KD):
                        pa, pbb, pga, pgb = bank(), bank(), bank(), bank()
                        for k4 in range(4):
                            S.op("pe", lambda e, k4=k4: e.matmul(psb[pa][:, 0:n], wua[:, k4, ncx * 128:(ncx + 1) * 128],
                                                                 ya[buf][:, k4, 0:n], start=(k4 == 0), stop=(k4 == 3)),
                                 reads=[("ya", buf), "W2"], writes=[("ps", pa)], inc=(k4 == 3))
                        for k4 in range(4):
                            S.op("pe", lambda e, k4=k4: e.matmul(psb[pbb][:, 0:n], wub[:, k4, ncx * 128:(ncx + 1) * 128],
                                                                 yb[buf][:, k4, 0:n], start=(k4 == 0), stop=(k4 == 3)),
                                 reads=[("yb", buf), "W2"], writes=[("ps", pbb)], inc=(k4 == 3))
                        for gi, pg in ((0, pga), (1, pgb)):
                            for kc in range(KD):
                                S.op("pe", lambda e, kc=kc, gi=gi, pg=pg: e.matmul(
                                    psb[pg][:, 0:n], wg[:, kc, gi * 1024 + ncx * 128:gi * 1024 + (ncx + 1) * 128],
                                    hT[buf][:, kc, 0:n], start=(kc == 0), stop=(kc == KD - 1)),
                                    reads=[("hT", buf), "W"], writes=[("ps", pg)], inc=(kc == KD - 1))
                        j = ncx % 2
                        S.op("act", lambda e: e.activation(out=sga[j][:, 0:n], in_=psb[pga][:, 0:n], func=AF.Sigmoid),
                             reads=[("ps", pga)], writes=[("sga", j)])
                        S.op("act", lambda e: e.activation(out=sgb[j][:, 0:n], in_=psb[pgb][:, 0:n], func=AF.Sigmoid),
                             reads=[("ps", pgb)], writes=[("sgb", j)])
                        S.op("dve", lambda e: e.tensor_tensor(out=t1[j][:, 0:n], in0=psb[pa][:, 0:n], in1=sga[j][:, 0:n], op=ALU.mult),
                             reads=[("ps", pa), ("sga", j)], writes=[("t1", j)])
                        S.op("dve", lambda e: e.tensor_tensor(out=sgb[j][:, 0:n], in0=psb[pbb][:, 0:n], in1=sgb[j][:, 0:n], op=ALU.mult),
                             reads=[("ps", pbb), ("sgb", j)], writes=[("sgb", j)])
                        S.op("pool", lambda e: e.tensor_tensor(out=mT[:, ncx, 0:n], in0=t1[j][:, 0:n], in1=sgb[j][:, 0:n], op=ALU.add),
                             reads=[("t1", j), ("sgb", j)], writes=["mT"])
                    for dc in range(KD):
                        pb = bank()
                        for kc in range(KD):
                            S.op("pe", lambda e, kc=kc: e.matmul(psb[pb][:, 0:n], wmo[:, kc, dc * 128:(dc + 1) * 128],
                                                                 mT[:, kc, 0:n], start=(kc == 0), stop=(kc == KD - 1)),
                                 reads=["mT", "W2"], writes=[("ps", pb)], inc=(kc == KD - 1))
                        S.op("act", lambda e: e.activation(out=tmp[:, dc, 0:n], in_=psb[pb][:, 0:n], func=AF.Copy),
                             reads=[("ps", pb)], writes=["tmp"])
                    S.op("act", lambda e: e.activation(out=sq[:, :, 0:n], in_=tmp[:, :, 0:n], func=AF.Square),
                         reads=["tmp"], writes=["sq"])
                    pb = bank()
                    for kc in range(KD):
                        S.op("pe", lambda e, kc=kc: e.matmul(psb[pb][:, 0:n], onesD[:], sq[:, kc, 0:n],
                                                             start=(kc == 0), stop=(kc == KD - 1)),
                             reads=["sq", "onesD"], writes=[("ps", pb)], inc=(kc == KD - 1))
                    S.op("act", lambda e: e.activation(out=rstd[:, 0:n], in_=psb[pb][:, 0:n], func=AF.Sqrt,
                                                       bias=epsc[:, 0:1], scale=1.0),
                         reads=[("ps", pb), "epsc"], writes=["rstd"])
                    S.op("dve", lambda e: e.reciprocal(rstd[:, 0:n], rstd[:, 0:n]), reads=["rstd"], writes=["rstd"])
                    S.op("dve", lambda e: e.tensor_tensor(
                        out=tmp[:, :, 0:n], in0=tmp[:, :, 0:n],
                        in1=rstd[:, 0:n].unsqueeze(1).to_broadcast([128, KD, n]), op=ALU.mult),
                        reads=["tmp", "rstd"], writes=["tmp"])
                    for dc in range(KD):
                        S.op("dve", lambda e, dc=dc: e.scalar_tensor_tensor(
                            out=xT[buf][:, dc, 0:n], in0=tmp[:, dc, 0:n], scalar=CG[:, 1, r, dc:dc + 1],
                            in1=xT[buf][:, dc, 0:n], op0=ALU.mult, op1=ALU.add),
                            reads=["tmp", "CG", ("xT", buf)], writes=[("xT", buf)])
                    S.op("sp", lambda e: e.dma_start(
                        out=x2T_d[b, :, :, s0:s0 + n].rearrange("k p t -> p k t"), in_=xT[buf][:, :, 0:n]),
                        reads=[("xT", buf)], writes=[("x2T", b, s0)], dsem="d_st%d" % buf)
                load(0)
                for i in range(len(tiles)):
                    do_tile(i)
                S.barrier()
                S.emit()


        def m3_phase():
            NKB = E // 128
            NQT = T // 512 if T >= 512 else 1
            QT = min(512, T)
            with ExitStack() as ph:
                wqk = sb("wqk", [128, KD, 1024], BF16, st=ph)
                wqs = sb("wqs", [128, KD, 1024], BF16, st=ph)
                wv = sb("wv", [128, KD, 512], BF16, st=ph)
                hA = sb("hA", [128, KD, E], BF16, st=ph)
                cosT = sb("cosT", [128, T], st=ph)
                sinT = sb("sinT", [128, T], st=ph)
                Vaug = sb("Vaug", [128, NKB, 4, 129], BF16, st=ph)
                QTt_ = [sb("QTt%d" % i, [128, T], BF16, st=ph) for i in range(2)]
                QTc_ = [[sb("QTc%d_%d" % (k_, i), [128, T], BF16, st=ph) for i in range(2)] for k_ in range(2)]
                KTt_ = [sb("KTt%d" % i, [128, E], BF16, st=ph) for i in range(2)]
                sel = sb("sel", [128, 2, 128], BF16, st=ph)
                r1 = [sb("r1%d" % i, [128, QT], st=ph) for i in range(2)]
                r2 = [sb("r2%d" % i, [128, QT], st=ph) for i in range(2)]
                qsq = [sb("qsq%d" % i, [128, QT], BF16, st=ph) for i in range(2)]
                mx_ = [sb("mx%d" % i, [128, 2, 2, 8], st=ph) for i in range(2)]
                mxr_ = [sb("mxr%d" % i, [128, 2, 2], st=ph) for i in range(2)]
                negc_ = [sb("negc%d" % i, [128, 2], st=ph) for i in range(2)]
                PT = [sb("PT%d" % i, [128, QT], BF16, st=ph) for i in range(3)]
                on1 = sb("on1", [128, QT], st=ph)
                oo = sb("oo", [128, QT], st=ph)
                osq = sb("osq", [128, QT], st=ph)
                Pacc = [sb("Pacc%d" % i, [128, QT], st=ph) for i in range(2)]
                rcp = [sb("rcp%d" % i, [128, QT], st=ph) for i in range(2)]
                subgp = sb("subgp", [128, 1], st=ph)
                ybT = [sb("ybT%d" % i, [128, QT], BF16, st=ph) for i in range(2)]
                NQS = QT // 128
                mi = mix_in_d.rearrange("(k p) n -> p k n", p=128)
                qs_ = qk_sw_d.rearrange("(k p) n -> p k n", p=128)
                for kc in range(KD):
                    S.op("pool", lambda e, kc=kc: e.dma_start(out=wqk[:, kc, :], in_=mi[:, kc, 1920:2944]), dsem="d_w")
                    S.op("pool", lambda e, kc=kc: e.dma_start(out=wqs[:, kc, :], in_=qs_[:, kc, :]), dsem="d_w")
                    S.op("pool", lambda e, kc=kc: e.dma_start(out=wv[:, kc, :], in_=mi[:, kc, 2944:3456]),
                         writes=(["W"] if kc == KD - 1 else []), dsem="d_w")
                S.op("sp", lambda e: e.dma_start(out=cosT[:], in_=cos_d), writes=["cosT"], dsem="d_cos")
                S.op("sp", lambda e: e.dma_start(out=sinT[:], in_=sin_d), writes=["sinT"], dsem="d_sin")
                for c in range(2):
                    S.op("dve", lambda e, c=c: e.tensor_copy(sel[:, c, :], cst[:, CS_IND + c:CS_IND + c + 1].to_broadcast([128, 128])),
                         reads=["cst"], writes=["sel"])
                S.op("dve", lambda e: e.tensor_scalar(subgp[:], pvec[:, PV_SUBG:PV_SUBG + 1], 0.8, None, ALU.mult),
                     reads=["pvec"], writes=["subgp"])
                S.op("dve", lambda e: e.memset(Vaug[:, :, :, 128:129], 1.0), writes=["Vaug1"])
                for k_ in range(2):
                    S.op("pool", lambda e: e.memset(QTc_[k_][0][64:128, :], 0.0), writes=[("QTc", k_)])
                    S.op("pool", lambda e: e.memset(QTc_[k_][1][0:64, :], 0.0), writes=[("QTc", k_)])
                S.op("dve", lambda e: e.tensor_tensor(out=osq[:, 0:64], in0=brow[:, BR_LAM:BR_LAM + 64], in1=brow[:, BR_LAM + 64:BR_LAM + 128], op=ALU.mult),
                     reads=["brow"], writes=["osq"])
                S.op("dve", lambda e: e.tensor_tensor(out=osq[:, 64:128], in0=brow[:, BR_LAM + 128:BR_LAM + 192], in1=brow[:, BR_LAM + 192:BR_LAM + 256], op=ALU.mult),
                     reads=["brow"], writes=["osq"])
                S.op("dve", lambda e: e.tensor_reduce(out=lam[:, 2:4], in_=osq[:, 0:128].rearrange("p (a x) -> p a x", a=2), axis=AX.X, op=ALU.add),
                     reads=["osq"], writes=["lam"])
                S.op("act", lambda e: e.activation(out=lam[:, 2:4], in_=lam[:, 2:4], func=AF.Exp), reads=["lam"], writes=["lam"])
                S.op("dve", lambda e: e.tensor_tensor(out=lam[:, 0:1], in0=lam[:, 2:3], in1=lam[:, 3:4], op=ALU.subtract),
                     reads=["lam"], writes=["lam"])
                S.op("dve", lambda e: e.tensor_scalar(lam[:, 0:1], lam[:, 0:1], 0.2, None, ALU.add), reads=["lam"], writes=["lam"])
                S.op("dve", lambda e: e.tensor_scalar(lam[:, 1:2], lam[:, 0:1], -1.0, None, ALU.mult), reads=["lam"], writes=["lam"])

                def norm_tile(qk, j, n, slot, hq):
                    for c in range(2):
                        pb = bank()
                        S.op("pe", lambda e, pb=pb, c=c: e.matmul(psb[pb][:, 0:n], sel[:, c, :], qsq[j][:, 0:n], start=True, stop=True),
                             reads=[("qsq", j), "sel"], writes=[("ps", pb)])
                        S.op("dve", lambda e, pb=pb, c=c: e.tensor_reduce(out=mx_[hq][:, qk, c, slot:slot + 1], in_=psb[pb][:, 0:n], axis=AX.X, op=ALU.max),
                             reads=[("ps", pb)], writes=[("mx", hq)])

                for b in range(NB):
                    S.op("sp", lambda e, b=b: e.dma_start(out=hA[:], in_=hT_d[b].rearrange("k p t -> p k t")),
                         reads=[("hTd", b)], writes=["hA"], dsem="d_hA")
                    pst["set"] = list(range(8))
                    for kb in range(NKB):
                        pb = bank()
                        for kc in range(KD):
                            S.op("pe", lambda e, kb=kb, kc=kc, pb=pb: e.matmul(
                                psb[pb][:, 0:512], hA[:, kc, kb * 128:(kb + 1) * 128], wv[:, kc, :],
                                start=(kc == 0), stop=(kc == KD - 1)),
                                reads=["hA", "W"], writes=[("ps", pb)], inc=(kc == KD - 1))
                        if kb % 2 == 0:
                            S.op("act", lambda e, kb=kb, pb=pb: e.activation(
                                out=Vaug[:, kb, :, 0:128], in_=psb[pb][:, 0:512].rearrange("p (h e) -> p h e", h=4), func=AF.Copy),
                                reads=[("ps", pb)], writes=["Vaug"])
                        else:
                            S.op("dve", lambda e, kb=kb, pb=pb: e.tensor_copy(
                                Vaug[:, kb, :, 0:128], psb[pb][:, 0:512].rearrange("p (h e) -> p h e", h=4)),
                                reads=[("ps", pb)], writes=["Vaug"])
                    def prep_head(h):
                        hq = (b * 4 + h) % 2
                        pst["set"] = list(range(8))
                        for e0 in range(0, CTX, 512):
                            n = min(512, CTX - e0)
                            pb = bank()
                            for kc in range(KD):
                                S.op("pe", lambda e, kc=kc, pb=pb, e0=e0, n=n: e.matmul(
                                    psb[pb][:, 0:n], wqk[:, kc, 512 + h * 128:512 + (h + 1) * 128], hA[:, kc, e0:e0 + n],
                                    start=(kc == 0), stop=(kc == KD - 1)),
                                    reads=["hA", "W"], writes=[("ps", pb)], inc=(kc == KD - 1))
                            S.op("act", lambda e, pb=pb, e0=e0, n=n: e.activation(out=KTt_[hq][:, e0:e0 + n], in_=psb[pb][:, 0:n], func=AF.Copy),
                                 reads=[("ps", pb)], writes=[("KTt", hq)])
                            S.op("pool", lambda e, e0=e0, n=n, i=(e0 // 512) % 2: e.tensor_tensor(
                                out=qsq[i][:, 0:n], in0=KTt_[hq][:, e0:e0 + n], in1=KTt_[hq][:, e0:e0 + n], op=ALU.mult),
                                reads=[("KTt", hq)], writes=[("qsq", (e0 // 512) % 2)])
                            norm_tile(1, (e0 // 512) % 2, n, e0 // 512, hq)
                        for ti in range(NQT):
                            t0 = ti * QT
                            for qk in range(2):
                                pm, psw = bank(), bank()
                                for kc in range(KD):
                                    S.op("pe", lambda e, kc=kc, pm=pm, qk=qk, t0=t0: e.matmul(
                                        psb[pm][:, 0:QT], wqk[:, kc, qk * 512 + h * 128:qk * 512 + (h + 1) * 128],
                                        hA[:, kc, CTX + t0:CTX + t0 + QT], start=(kc == 0), stop=(kc == KD - 1)),
                                        reads=["hA", "W"], writes=[("ps", pm)], inc=(kc == KD - 1))
                                for kc in range(KD):
                                    S.op("pe", lambda e, kc=kc, psw=psw, qk=qk, t0=t0: e.matmul(
                                        psb[psw][:, 0:QT], wqs[:, kc, qk * 512 + h * 128:qk * 512 + (h + 1) * 128],
                                        hA[:, kc, CTX + t0:CTX + t0 + QT], start=(kc == 0), stop=(kc == KD - 1)),
                                        reads=["hA", "W"], writes=[("ps", psw)], inc=(kc == KD - 1))
                                j = (ti * 2 + qk) % 2
                                S.op("dve", lambda e, pm=pm, j=j, t0=t0: e.tensor_tensor(
                                    out=r1[j][:], in0=psb[pm][:, 0:QT], in1=cosT[:, t0:t0 + QT], op=ALU.mult),
                                    reads=[("ps", pm), "cosT"], writes=[("r1", j)])
                                S.op("dve", lambda e, psw=psw, j=j, t0=t0: e.tensor_tensor(
                                    out=r2[j][:], in0=psb[psw][:, 0:QT], in1=sinT[:, t0:t0 + QT], op=ALU.mult),
                                    reads=[("ps", psw), "sinT"], writes=[("r2", j)])
                                dst = QTt_[hq][:, t0:t0 + QT] if qk == 0 else KTt_[hq][:, CTX + t0:CTX + t0 + QT]
                                dres = ("QTt", hq) if qk == 0 else ("KTt", hq)
                                S.op("pool", lambda e, j=j, dst=dst: e.tensor_tensor(out=dst, in0=r1[j][:], in1=r2[j][:], op=ALU.add),
                                     reads=[("r1", j), ("r2", j)], writes=[dres])
                                S.op("pool", lambda e, j=j, dst=dst: e.tensor_tensor(out=qsq[j][:], in0=dst, in1=dst, op=ALU.mult),
                                     reads=[dres], writes=[("qsq", j)])
                                if qk == 0:
                                    S.op("pool", lambda e: e.tensor_copy(QTc_[hq][0][0:64, t0:t0 + QT], QTt_[hq][0:64, t0:t0 + QT]), reads=[("QTt", hq)], writes=[("QTc", hq)])
                                    S.op("pool", lambda e: e.tensor_copy(QTc_[hq][1][64:128, t0:t0 + QT], QTt_[hq][64:128, t0:t0 + QT]), reads=[("QTt", hq)], writes=[("QTc", hq)])
                                norm_tile(qk, j, QT, ti + (CTX + 511) // 512 if qk == 1 else ti, hq)
                        nkt = NQT + (CTX + 511) // 512
                        S.op("dve", lambda e: e.tensor_reduce(out=mxr_[hq][:, 0, :], in_=mx_[hq][:, 0, :, 0:NQT], axis=AX.X, op=ALU.max),
                             reads=[("mx", hq)], writes=[("mxr", hq)])
                        S.op("dve", lambda e, nkt=nkt: e.tensor_reduce(out=mxr_[hq][:, 1, :], in_=mx_[hq][:, 1, :, 0:nkt], axis=AX.X, op=ALU.max),
                             reads=[("mx", hq)], writes=[("mxr", hq)])
                        S.op("dve", lambda e: e.tensor_tensor(out=negc_[hq][:], in0=mxr_[hq][:, 0, :], in1=mxr_[hq][:, 1, :], op=ALU.mult),
                             reads=[("mxr", hq)], writes=[("negc", hq)])
                        S.op("act", lambda e: e.activation(out=negc_[hq][:], in_=negc_[hq][:], func=AF.Sqrt, scale=1.0 / 64.0),
                             reads=[("negc", hq)], writes=[("negc", hq)])
                        S.op("dve", lambda e: e.tensor_scalar(negc_[hq][:], negc_[hq][:], -1.0, None, ALU.mult), reads=[("negc", hq)], writes=[("negc", hq)])
                    def attend(h):
                        hq = (b * 4 + h) % 2
                        pst["set"] = [2, 3, 4, 5, 6, 7]
                        for ti in range(NQT):
                            t0 = ti * QT
                            for c in range(2):
                                accb = c
                                cs = slice(c * 64, (c + 1) * 64)

                                def score(kb, c=c, cs=cs, t0=t0):
                                    pb = bank()
                                    S.op("pe", lambda e: e.matmul(psb[pb][:, 0:QT], KTt_[hq][:, kb * 128:(kb + 1) * 128], QTc_[hq][c][:, t0:t0 + QT],
                                                                  start=True, stop=True),
                                         reads=[("KTt", hq), ("QTc", hq)], writes=[("ps", pb)])
                                    return pb

                                pbs = score(0)
                                for kb in range(NKB):
                                    pbn = score(kb + 1) if kb + 1 < NKB else None
                                    pi = kb % 3
                                    S.op("act", lambda e, pbs=pbs, pi=pi, c=c: e.activation(
                                        out=PT[pi][:], in_=psb[pbs][:, 0:QT], func=AF.Exp, bias=negc_[hq][:, c:c + 1], scale=0.125),
                                        reads=[("ps", pbs), ("negc", hq)], writes=[("PT", pi)])
                                    S.op("pe", lambda e, pi=pi, kb=kb: e.matmul(
                                        psb[accb][:, 0:QT], Vaug[:, kb, h, 0:128], PT[pi][:], start=(kb == 0), stop=(kb == NKB - 1)),
                                        reads=[("PT", pi), "Vaug"], writes=[("ps", accb)])
                                    if kb == 0:
                                        S.op("dve", lambda e, pi=pi: e.tensor_copy(Pacc[c][:], PT[pi][:]), reads=[("PT", pi)], writes=[("Pacc", c)])
                                    else:
                                        S.op("dve", lambda e, pi=pi: e.tensor_tensor(out=Pacc[c][:], in0=Pacc[c][:], in1=PT[pi][:], op=ALU.add),
                                             reads=[("PT", pi), ("Pacc", c)], writes=[("Pacc", c)])
                                    pbs = pbn
                                pr = bank()
                                S.op("pe", lambda e: e.matmul(psb[pr][:, 0:QT], ones_f[:], Pacc[c][:], start=True, stop=True),
                                     reads=[("Pacc", c), "ones_f"], writes=[("ps", pr)])
                                S.op("act", lambda e: e.activation(out=rcp[c][:], in_=psb[pr][:, 0:QT], func=AF.Ln), reads=[("ps", pr)], writes=[("rcp", c)])
                                S.op("act", lambda e: e.activation(out=rcp[c][:], in_=rcp[c][:], func=AF.Exp, scale=-1.0), reads=[("rcp", c)], writes=[("rcp", c)])
                                if c == 0:
                                    S.op("dve", lambda e: e.tensor_tensor(out=on1[:], in0=psb[accb][:, 0:QT], in1=rcp[0][:], op=ALU.mult),
                                         reads=[("ps", accb), ("rcp", 0)], writes=["on1"])
                                else:
                                    S.op("dve", lambda e: e.tensor_scalar(rcp[1][:], rcp[1][:], lam[:, 1:2], None, ALU.mult),
                                         reads=[("rcp", 1), "lam"], writes=[("rcp", 1)])
                                    S.op("dve", lambda e: e.tensor_tensor(out=oo[:], in0=psb[accb][:, 0:QT], in1=rcp[1][:], op=ALU.mult),
                                         reads=[("ps", accb), ("rcp", 1)], writes=["oo"])
                                    S.op("pool", lambda e: e.tensor_tensor(out=oo[:], in0=oo[:], in1=on1[:], op=ALU.add),
                                         reads=["oo", "on1"], writes=["oo"])
                            S.op("pool", lambda e: e.tensor_tensor(out=osq[:], in0=oo[:], in1=oo[:], op=ALU.mult), reads=["oo"], writes=["osq"])
                            pm_ = bank()
                            S.op("pe", lambda e: e.matmul(psb[pm_][:, 0:QT], ones_f[:], osq[:], start=True, stop=True),
                                 reads=["osq", "ones_f"], writes=[("ps", pm_)])
                            S.op("act", lambda e: e.activation(out=osq[:], in_=psb[pm_][:, 0:QT], func=AF.Ln, bias=epsc[:, 0:1], scale=1.0 / 128.0),
                                 reads=[("ps", pm_), "epsc"], writes=["osq"])
                            S.op("act", lambda e: e.activation(out=osq[:], in_=osq[:], func=AF.Exp, scale=-0.5), reads=["osq"], writes=["osq"])
                            yi = ti % 2
                            S.op("dve", lambda e: e.scalar_tensor_tensor(out=ybT[yi][:], in0=oo[:], scalar=subgp[:, 0:1], in1=osq[:],
                                                                         op0=ALU.mult, op1=ALU.mult),
                                 reads=["oo", "osq", "subgp"], writes=[("ybT", yi)])
                            S.op("sp", lambda e, yi=yi, t0=t0, b=b: e.dma_start(out=ybT_d[b, h, :, t0:t0 + QT], in_=ybT[yi][:]),
                                 reads=[("ybT", yi)], writes=[("ybT", b)], dsem="d_yb%d" % yi)
                    prep_head(0)
                    for h in range(4):
                        if h + 1 < 4:
                            prep_head(h + 1)
                        attend(h)
                pst["set"] = list(range(8))
                S.barrier()
                S.emit()


        def m2_phase():
            NCH = E // 128
            NCC = CTX // 128
            NXC = T // 128
            NSL = 2 * NCH
            G32 = cfg.get('m2f32', 'BC')
            gdt = lambda g: F32 if g in G32 else BF16
            gid = lambda g: ident32 if g in G32 else ident_bf[:]
            gbc = lambda g, ap: (ap if g in G32 else ap.bitcast(BF16))
            TPB = 4 if 'R' in G32 else 8
            with ExitStack() as ph:
                wsub = [sb("wsub%d" % i, [128, KD, 384], BF16, st=ph) for i in range(2)]
                wcnt = [0]
                w2t = sb("w2t", [128, 512], gdt('E'), st=ph)
                a2t = sb("a2t", [128, 512], gdt('E'), st=ph)
                g2t = sb("g2t", [128, 512], gdt('R'), st=ph)
                bones = sb("bones", [128, 128], gdt('A'), st=ph)
                ind2 = sb("ind2", [128, 2], gdt('R'), st=ph)
                omka = sb("omka", [128, 4], st=ph)
                hTt = [sb("hTt%d" % i, [128, KD, TT + 2], BF16, st=ph) for i in range(2)]
                lw = sb("lw", [128, E], gdt('E'), st=ph)
                la = sb("la", [128, E], gdt('E'), st=ph)
                lg = sb("lg", [128, E], gdt('R'), st=ph)
                MTs = sb("MTs", [128, NSL, 128], gdt('D'), st=ph)
                GyTs = sb("GyTs", [128, NSL, 128], gdt('D'), st=ph)
                Sadds = sb("Sadds", [128, NSL, 64], st=ph)
                WCs = sb("WCs", [128, NSL], st=ph)
                Yacc = sb("Yacc", [128, NXC, 128], st=ph)
                Vtm = sb("Vtm", [128, NCH, 128], gdt('C'), st=ph)
                BS = sb("BS", [128, NXC, 2], st=ph)
                S32 = sb("S32", [128, 2, 64], st=ph)
                Sbf = sb("Sbf", [128, 2, 64], gdt('D'), st=ph)
                Sbd = sb("Sbd", [128, 2, 128], gdt('D'), st=ph)
                ub = [sb("ub%d" % i, [128, TT + 2], st=ph) for i in range(2)]
                NF = 22
                ft = [sb("ft%d" % i, [128, TT], st=ph) for i in range(NF)]
                (rT, kT, vT, kk, t0_, t1_, Lin, Lex, E1, E2, E3, E4) = ft[0:12]
                sgw = ft[12:14]
                a_d = ft[14:16]
                kdir = ft[16:18]
                b_d = ft[18:20]
                Pp = ft[20]
                WCt = sb("WCt", [128, 2], st=ph)
                QRl = [[sb("QR%d_%d" % (t_, i), [128, TT // 128, 256], gdt('A'), st=ph) for i in range(2)] for t_ in range(2)]
                Khl = [[sb("Kh%d_%d" % (t_, i), [128, TT], gdt('A'), st=ph) for i in range(2)] for t_ in range(2)]
                Bhl = [[sb("Bh%d_%d" % (t_, i), [128, TT], gdt('A'), st=ph) for i in range(2)] for t_ in range(2)]
                Kdl = [[sb("Kd%d_%d" % (t_, i), [128, TT], gdt('C'), st=ph) for i in range(2)] for t_ in range(2)]
                Bdl = [[sb("Bd%d_%d" % (t_, i), [128, TT], gdt('C'), st=ph) for i in range(2)] for t_ in range(2)]
                vb = sb("vb", [128, TT], gdt('C'), st=ph)
                prb = sb("prb", [128, TT], gdt('R'), st=ph)
                ksq = sb("ksq", [128, TT], gdt('A'), st=ph)
                NCHN = cfg.get("nchain", 4)
                SAl = [sb("SA%d" % i, [128, 2, 256], gdt('B'), st=ph) for i in range(NCHN)]
                SBl = [sb("SB%d" % i, [128, 2, 256], gdt('B'), st=ph) for i in range(NCHN)]
                XTal = [None] * NCHN
                Xal = [[sb("Xa%d_%d" % (i, k), [128, 2, 128], gdt('B'), st=ph) for k in range(2)] for i in range(NCHN)]
                Zal = [[sb("Za%d_%d" % (i, k), [128, 2, 256], gdt('B'), st=ph) for k in range(2)] for i in range(NCHN)]
                KdBdl = [sb("KdBd%d" % i, [128, 2, 2, 128], gdt('C'), st=ph) for i in range(NCHN)]
                Gpadl = [sb("Gpad%d" % i, [128, 2, 128], gdt('B'), st=ph) for i in range(NCHN)]
                yc = sb("yc", [128, (NXC + 1) // 2, 128], st=ph)
                ysq = sb("ysq", [128, (NXC + 1) // 2, 128], st=ph)
                st1 = sb("st1", [128, NXC * 2], st=ph)
                st2 = sb("st2", [128, NXC * 2], st=ph)
                yab = sb("yab", [128, NXC, 128], gdt('R'), st=ph)
                yaT = sb("yaT", [128, T], BF16, st=ph)
                mi = mix_in_d.rearrange("(k p) n -> p k n", p=128)
                def load_w(hp_):
                    k = wcnt[0] % 2
                    wcnt[0] += 1
                    if hp_ is None:
                        S.op("pool", lambda e: e.dma_start(out=wsub[k][:], in_=mi[:, :, 1536:1920]), writes=[("wsub", k, 0), ("wsub", k, 1), ("wsub", k, 2)], dsem="d_wsub%d_0" % k)
                    else:
                        for j in range(3):
                            c0_ = j * 512 + hp_ * 128
                            S.op("pool", lambda e: e.dma_start(out=wsub[k][:, :, j * 128:(j + 1) * 128], in_=mi[:, :, c0_:c0_ + 128]),
                                 writes=[("wsub", k, j)], dsem="d_wsub%d_%d" % (k, j))
                    return k

                S.op("pool", lambda e: e.dma_start(out=w2t[:], in_=w2_d.rearrange("d r c -> (d r) c")), dsem="d_w")
                S.op("pool", lambda e: e.dma_start(out=a2t[:], in_=a2_d.rearrange("d r c -> (d r) c")), dsem="d_w")
                S.op("pool", lambda e: e.dma_start(out=g2t[:], in_=g2_d), writes=["W"], dsem="d_w")
                S.op("dve", lambda e: e.tensor_copy(bones[:], cst[:, CS_BONES:CS_BONES + 128]), reads=["cst"], writes=["bones"])
                S.op("dve", lambda e: e.tensor_copy(ind2[:], cst[:, CS_IND:CS_IND + 2]), reads=["cst"], writes=["ind2"])
                for ci in range(NCHN):
                    S.op("pool", lambda e: e.memset(KdBdl[ci][:], 0.0), writes=[("KdBd", ci)])
                    S.op("pool", lambda e: e.memset(Gpadl[ci][:], 0.0), writes=[("Gpad", ci)])
                S.op("dve", lambda e: e.tensor_scalar(omka[:], pvec[:, PV_KA:PV_KA + 4], -1.0, 1.0, ALU.mult, ALU.add),
                     reads=["pvec"], writes=["omka"])
                fsm = cst[:, CS_FS:CS_FS + 256]
                bsm = cst[:, CS_BS:CS_BS + 256]
                hcnt = [0]

                def load_h(b, e0, n):
                    r0, r1 = (0, CTX) if e0 < CTX else (CTX, E)
                    lo, hi = max(e0 - 1, r0), min(e0 + n + 1, r1)
                    off = lo - (e0 - 1)
                    buf = hcnt[0] % 2
                    hcnt[0] += 1
                    S.op("sp", lambda e: e.dma_start(out=hTt[buf][:, :, off:off + hi - lo],
                                                     in_=hT_d[b, :, :, lo:hi].rearrange("k p t -> p k t")),
                         reads=[("hTd", b)], writes=[("hTt", buf)], dsem="d_hTt%d" % buf)
                    return buf, lo, hi, off

                ucnt = [0]

                def proj_conv(hb, lo, hi, off, e0, n, chunk, dst, wk, wj):
                    N = hi - lo
                    pb = bank()
                    u = ub[ucnt[0] % 2]
                    ur = ("ub", ucnt[0] % 2)
                    ucnt[0] += 1
                    for kc in range(KD):
                        S.op("pe", lambda e, kc=kc: e.matmul(psb[pb][:, 0:N], wsub[wk][:, kc, wj * 128:(wj + 1) * 128],
                                                             hTt[hb][:, kc, off:off + N], start=(kc == 0), stop=(kc == KD - 1)),
                             reads=[("hTt", hb), ("wsub", wk, wj)], writes=[("ps", pb)], inc=(kc == KD - 1))
                    if off == 1:
                        S.op("pool", lambda e: e.memset(u[:, 0:1], 0.0), writes=[ur])
                    if hi < e0 + n + 1:
                        S.op("pool", lambda e: e.memset(u[:, n + 1:n + 2], 0.0), writes=[ur])
                    S.op("act", lambda e: e.activation(out=u[:, off:off + N], in_=psb[pb][:, 0:N], func=AF.Copy),
                         reads=[("ps", pb)], writes=[ur])
                    cw = lambda tap: pvec[:, PV_CONV + tap * 15 + chunk:PV_CONV + tap * 15 + chunk + 1]
                    S.op("act", lambda e: e.activation(out=dst, in_=u[:, 0:n], func=AF.Identity, scale=cw(0)),
                         reads=[ur, "pvec"], writes=[dst.tensor.name])
                    S.op("dve", lambda e: e.scalar_tensor_tensor(out=dst, in0=u[:, 1:n + 1], scalar=cw(1), in1=dst,
                                                                  op0=ALU.mult, op1=ALU.add),
                         reads=[ur, "pvec", dst.tensor.name], writes=[dst.tensor.name])
                    S.op("dve", lambda e: e.scalar_tensor_tensor(out=dst, in0=u[:, 2:n + 2], scalar=cw(2), in1=dst,
                                                                  op0=ALU.mult, op1=ALU.add),
                         reads=[ur, "pvec", dst.tensor.name], writes=[dst.tensor.name])

                def rn(t):
                    return t.tensor.name if hasattr(t, "tensor") else t.name

                def chunk_pre(b, hp, cidx, cl, d, ci, tp):
                    slot = d * NCH + cidx
                    QR, Kh, Bh, Kd, Bd = QRl[tp], Khl[tp], Bhl[tp], Kdl[tp], Bdl[tp]
                    SA, SB, XTa, Xa, Za, KdBd, Gpad = SAl[ci], SBl[ci], XTal[ci], Xal[ci], Zal[ci], KdBdl[ci], Gpadl[ci]
                    cs = slice(cl * 128, (cl + 1) * 128)
                    m2 = (fsm if d == 0 else bsm).unsqueeze(1).to_broadcast([128, 2, 256])
                    mT = (cst[:, CS_BS:CS_BS + 128] if d == 0 else cst[:, CS_FS:CS_FS + 128]).unsqueeze(1).to_broadcast([128, 2, 128])
                    pq = bank()
                    pqv = gbc('A', psb[pq][:])
                    S.op("pe", lambda e: e.transpose(pqv[:, 0:128], QR[d][:, cl, 0:128], gid('A')),
                         reads=[("QR", tp, d), "ident_bf", "cst"], writes=[("ps", pq)])
                    pt = bank()
                    ptv = gbc('C', psb[pt][:])
                    S.op("pe", lambda e: e.transpose(ptv[:, 128:256], Kd[d][:, cs], gid('C')),
                         reads=[("Kd", tp, d), "ident_bf", "cst"], writes=[("ps", pt)], inc=False)
                    S.op("pe", lambda e: e.transpose(ptv[:, 256:384], Bd[d][:, cs], gid('C')),
                         reads=[("Bd", tp, d), "ident_bf", "cst"], writes=[("ps", pt)])
                    S.op("act", lambda e: e.activation(out=Za[0][:, :, 64:128], in_=pqv[:, 0:128].rearrange("p (h k) -> p h k", h=2), func=AF.Copy),
                         reads=[("ps", pq)], writes=[("Z", ci, 0)])
                    for h in range(2):
                        S.op("act", lambda e, h=h: e.activation(out=KdBd[:, :, h, h * 64:(h + 1) * 64],
                                                                in_=ptv[:, 128:384].rearrange("p (a x) -> p a x", a=2)[:, :, h * 64:(h + 1) * 64], func=AF.Copy),
                             reads=[("ps", pt)], writes=[("KdBd", ci)])
                    if cfg.get('m2cut', 99) <= 3.01:
                        return
                    yield
                    for h in range(2):
                        hs = slice(h * 64, (h + 1) * 64)
                        pA, pB, pC = bank(), bank(), bank()
                        S.op("pe", lambda e: e.matmul(psb[pA][:, 0:256], Kh[d][hs, cs], QR[d][hs, cl, :], start=True, stop=True),
                             reads=[("Kh", tp, d), ("QR", tp, d)], writes=[("ps", pA)])
                        S.op("pe", lambda e: e.matmul(psb[pB][:, 0:256], Bh[d][hs, cs], QR[d][hs, cl, :], start=True, stop=True),
                             reads=[("Bh", tp, d), ("QR", tp, d)], writes=[("ps", pB)])
                        S.op("pe", lambda e: e.matmul(psb[pC][:, 0:128], QR[d][hs, cl, 0:128], Bh[d][hs, cs], start=True, stop=True),
                             reads=[("Bh", tp, d), ("QR", tp, d)], writes=[("ps", pC)])
                        if cfg.get('m2cut', 99) <= 3.02:
                            continue
                        S.op("dve", lambda e: e.tensor_tensor(out=SA[:, h, :], in0=psb[pA][:, 0:256], in1=(fsm if d == 0 else bsm), op=ALU.mult),
                             reads=[("ps", pA), "cst"], writes=[("SA", ci)])
                        S.op("dve", lambda e: e.tensor_tensor(out=SB[:, h, :], in0=psb[pB][:, 0:256], in1=(fsm if d == 0 else bsm), op=ALU.mult),
                             reads=[("ps", pB), "cst"], writes=[("SB", ci)])
                        S.op("dve", lambda e: e.tensor_tensor(out=Za[0][:, h, 128:256], in0=psb[pC][:, 0:128],
                                                              in1=(cst[:, CS_BS:CS_BS + 128] if d == 0 else cst[:, CS_FS:CS_FS + 128]), op=ALU.mult),
                             reads=[("ps", pC), "cst"], writes=[("Z", ci, 0)])
                    if cfg.get('m2cut', 99) <= 3.02:
                        return
                    if cfg.get('m2cut', 99) <= 3.03:
                        return
                    if cfg.get('m2cut', 99) <= 3.1:
                        return
                    yield
                    pP = bank()
                    for h in range(2):
                        S.op("pe", lambda e, h=h: e.matmul(psb[pP][:, h * 64:(h + 1) * 64], SA[:, h, 0:128], Vtm[:, cidx, h * 64:(h + 1) * 64],
                                                           start=(h == 0), stop=(h == 1)),
                             reads=[("SA", ci), "Vtm"], writes=[("ps", pP)], inc=(h == 1))
                    S.op("act", lambda e: e.activation(out=Za[0][:, :, 0:64], in_=psb[pP][:, 0:128].rearrange("p (h v) -> p h v", h=2),
                                                       func=AF.Identity, scale=-1.0),
                         reads=[("ps", pP)], writes=[("Z", ci, 0)])
                    if cfg.get('m2cut', 99) <= 3.2:
                        return
                    yield
                    for j in range(7):
                        zi, zo = j % 2, (j + 1) % 2
                        Xj = (lambda h: SB[:, h, 0:128]) if j == 0 else (lambda h, t=Xa[j % 2]: t[:, h, :])
                        xres = ("SB", ci) if j == 0 else ("X", ci, j % 2)
                        NW = 256 if j < 5 else 128
                        pZ = bank()
                        for h in range(2):
                            S.op("pe", lambda e, h=h: e.matmul(psb[pZ][:, h * 256:h * 256 + NW], Xj(h), Za[zi][:, h, 0:NW],
                                                               start=(h == 0), stop=(h == 1)),
                                 reads=[xres, ("Z", ci, zi)], writes=[("ps", pZ)], inc=(h == 1))
                        if j < 6:
                            pX = bank()
                            for h in range(2):
                                S.op("pe", lambda e, h=h: e.matmul(psb[pX][:, h * 128:(h + 1) * 128], Za[zi][:, h, 128:256], Xj(h),
                                                                   start=(h == 0), stop=(h == 1)),
                                     reads=[xres, ("Z", ci, zi)], writes=[("ps", pX)], inc=(h == 1))
                        yield
                        sign = -1.0 if j == 0 else 1.0
                        pz3 = psb[pZ][:].rearrange("p (h x) -> p h x", h=2)
                        S.op("dve", lambda e: e.scalar_tensor_tensor(
                            out=Za[zo][:, :, 0:128], in0=pz3[:, :, 0:128], scalar=sign, in1=Za[zi][:, :, 0:128],
                            op0=ALU.mult, op1=ALU.add),
                            reads=[("ps", pZ), ("Z", ci, zi)], writes=[("Z", ci, zo)])
                        if j < 5:
                            S.op("act", lambda e: e.activation(out=Za[zo][:, :, 128:256], in_=pz3[:, :, 128:256], func=AF.Copy),
                                 reads=[("ps", pZ)], writes=[("Z", ci, zo)])
                        if j < 6:
                            S.op("act", lambda e: e.activation(out=Xa[(j + 1) % 2][:], in_=psb[pX][:, 0:256].rearrange("p (h x) -> p h x", h=2), func=AF.Copy),
                                 reads=[("ps", pX)], writes=[("X", ci, (j + 1) % 2)])
                    if cfg.get('m2cut', 99) <= 3.3:
                        return
                    yield
                    Z7 = Za[1]
                    zr = ("Z", ci, 1)
                    for h in range(2):
                        hs = slice(h * 64, (h + 1) * 64)
                        S.op("dve", lambda e: e.tensor_copy(Gpad[:, h, hs], Z7[:, h, 64:128]), reads=[zr], writes=[("Gpad", ci)])
                    yield
                    p5 = bank()
                    for h in range(2):
                        hs = slice(h * 64, (h + 1) * 64)
                        S.op("pe", lambda e: e.matmul(psb[p5][:, 0:128], Gpad[:, h, :], SB[:, h, 128:256], start=(h == 0), stop=(h == 1)),
                             reads=[("Gpad", ci), ("SB", ci)], writes=[("ps", p5)], inc=(h == 1))
                    p5b = bank()
                    for h in range(2):
                        hs = slice(h * 64, (h + 1) * 64)
                        S.op("pe", lambda e: e.matmul(psb[p5b][:, h * 64:(h + 1) * 64], Gpad[:, h, :], KdBd[:, 1, h, hs],
                                                      start=(h == 0), stop=(h == 1)),
                             reads=[("Gpad", ci), ("KdBd", ci)], writes=[("ps", p5b)], inc=(h == 1))
                    yield
                    S.op("dve", lambda e: e.scalar_tensor_tensor(out=GyTs[:, slot, :], in0=psb[p5][:, 0:128], scalar=-1.0, in1=QR[d][:, cl, 128:256],
                                                                 op0=ALU.mult, op1=ALU.add),
                         reads=[("ps", p5), ("QR", tp, d)], writes=[("GyTs", slot)])
                    S.op("act", lambda e: e.activation(out=MTs[:, slot, :], in_=psb[p5b][:, 0:128], func=AF.Identity, scale=-1.0),
                         reads=[("ps", p5b)], writes=[("MTs", slot)])
                    if cfg.get('m2cut', 99) <= 3.4:
                        return
                    yield
                    if cidx >= NCC:
                        xc = cidx - NCC
                        p6 = bank()
                        for h in range(2):
                            S.op("pe", lambda e, h=h: e.matmul(psb[p6][:, h * 64:(h + 1) * 64], SA[:, h, 128:256], Vtm[:, cidx, h * 64:(h + 1) * 64],
                                                               start=(h == 0), stop=False),
                                 reads=[("SA", ci), "Vtm"], writes=[("ps", p6)], inc=False)
                            S.op("pe", lambda e, h=h: e.matmul(psb[p6][:, h * 64:(h + 1) * 64], SB[:, h, 128:256], Z7[:, h, 0:64],
                                                               start=False, stop=(h == 1)),
                                 reads=[("SB", ci), zr], writes=[("ps", p6)], inc=(h == 1))
                        if d == 0:
                            S.op("act", lambda e: e.activation(out=Yacc[:, xc, :], in_=psb[p6][:, 0:128], func=AF.Copy),
                                 reads=[("ps", p6)], writes=[("Yacc", xc)])
                        else:
                            S.op("dve", lambda e: e.tensor_tensor(out=Yacc[:, xc, :], in0=psb[p6][:, 0:128], in1=Yacc[:, xc, :], op=ALU.add),
                                 reads=[("ps", p6), ("Yacc", xc)], writes=[("Yacc", xc)])
                    if cfg.get('m2cut', 99) <= 3.5:
                        return
                    yield
                    p7 = bank()
                    for h in range(2):
                        hs = slice(h * 64, (h + 1) * 64)
                        S.op("pe", lambda e: e.matmul(psb[p7][:, 0:64], KdBd[:, 0, h, :], Vtm[:, cidx, hs], start=(h == 0), stop=False),
                             reads=[("KdBd", ci), "Vtm"], writes=[("ps", p7)], inc=False)
                        S.op("pe", lambda e: e.matmul(psb[p7][:, 0:64], KdBd[:, 1, h, :], Z7[:, h, 0:64], start=False, stop=(h == 1)),
                             reads=[("KdBd", ci), zr], writes=[("ps", p7)], inc=(h == 1))
                    S.op("act", lambda e: e.activation(out=Sadds[:, slot, :], in_=psb[p7][:, 0:64], func=AF.Copy),
                         reads=[("ps", p7)], writes=[("Sadds", slot)])

                def tile_prep(b, hp, hb, lo, hi, off, e0, n, wk, tp):
                    nch = n // 128
                    c0 = e0 // 128
                    QR, Kh, Bh, Kd, Bd = QRl[tp], Khl[tp], Bhl[tp], Kdl[tp], Bdl[tp]
                    proj_conv(hb, lo, hi, off, e0, n, hp, rT[:, 0:n], wk, 0)
                    proj_conv(hb, lo, hi, off, e0, n, 4 + hp, kT[:, 0:n], wk, 1)
                    proj_conv(hb, lo, hi, off, e0, n, 8 + hp, vT[:, 0:n], wk, 2)
                    hpc = slice(hp * 128, (hp + 1) * 128)
                    for d in range(2):
                        ds_ = slice(d * 64, (d + 1) * 64)
                        pb = bank()
                        S.op("pe", lambda e: e.matmul(psb[pb][:, 0:n], w2t[ds_, hpc], lw[ds_, e0:e0 + n], start=True, stop=True),
                             reads=["W", "lw"], writes=[("ps", pb)])
                        S.op("act", lambda e: e.activation(out=sgw[d][:, 0:n], in_=psb[pb][:, 0:n], func=AF.Sigmoid,
                                                           bias=pvec[:, PV_W0 + d * 4 + hp:PV_W0 + d * 4 + hp + 1], scale=1.0),
                             reads=[("ps", pb), "pvec"], writes=[rn(sgw[d])])
                        pb2 = bank()
                        S.op("pe", lambda e: e.matmul(psb[pb2][:, 0:n], a2t[ds_, hpc], la[ds_, e0:e0 + n], start=True, stop=True),
                             reads=["W", "la"], writes=[("ps", pb2)])
                        S.op("act", lambda e: e.activation(out=a_d[d][:, 0:n], in_=psb[pb2][:, 0:n], func=AF.Sigmoid,
                                                           bias=pvec[:, PV_A0 + d * 4 + hp:PV_A0 + d * 4 + hp + 1], scale=1.0),
                             reads=[("ps", pb2), "pvec"], writes=[rn(a_d[d])])
                    S.op("dve", lambda e: e.tensor_scalar(kk[:, 0:n], kT[:, 0:n], pvec[:, PV_KK + hp:PV_KK + hp + 1], None, ALU.mult),
                         reads=[rn(kT), "pvec"], writes=[rn(kk)])
                    S.op("dve", lambda e: e.tensor_tensor(out=ksq[:, 0:n], in0=kk[:, 0:n], in1=kk[:, 0:n], op=ALU.mult),
                         reads=[rn(kk)], writes=["ksq"])
                    pb = bank()
                    S.op("pe", lambda e: e.matmul(psb[pb][:, 0:n], bones[:], ksq[:, 0:n], start=True, stop=True),
                         reads=["ksq", "bones"], writes=[("ps", pb)])
                    S.op("act", lambda e: e.activation(out=t0_[:, 0:n], in_=psb[pb][:, 0:n], func=AF.Sqrt, bias=epsc[:, 2:3], scale=1.0),
                         reads=[("ps", pb), "epsc"], writes=[rn(t0_)])
                    S.op("dve", lambda e: e.reciprocal(t0_[:, 0:n], t0_[:, 0:n]), reads=[rn(t0_)], writes=[rn(t0_)])
                    S.op("dve", lambda e: e.tensor_tensor(out=kk[:, 0:n], in0=kk[:, 0:n], in1=t0_[:, 0:n], op=ALU.mult),
                         reads=[rn(kk), rn(t0_)], writes=[rn(kk)])
                    for d in range(2):
                        S.op("dve", lambda e: e.tensor_scalar(t1_[:, 0:n], a_d[d][:, 0:n], pvec[:, PV_KA + hp:PV_KA + hp + 1],
                                                              omka[:, hp:hp + 1], ALU.mult, ALU.add),
                             reads=[rn(a_d[d]), "pvec", "omka"], writes=[rn(t1_)])
                        S.op("dve", lambda e: e.tensor_tensor(out=kdir[d][:, 0:n], in0=kT[:, 0:n], in1=t1_[:, 0:n], op=ALU.mult),
                             reads=[rn(kT), rn(t1_)], writes=[rn(kdir[d])])
                        S.op("dve", lambda e: e.tensor_tensor(out=b_d[d][:, 0:n], in0=kk[:, 0:n], in1=a_d[d][:, 0:n], op=ALU.mult),
                             reads=[rn(kk), rn(a_d[d])], writes=[rn(b_d[d])])
                    if e0 >= CTX:
                        S.op("dve", lambda e: e.tensor_tensor(out=t1_[:, 0:n], in0=kdir[0][:, 0:n], in1=kdir[1][:, 0:n], op=ALU.add),
                             reads=[rn(kdir[0]), rn(kdir[1])], writes=[rn(t1_)])
                        S.op("dve", lambda e: e.scalar_tensor_tensor(out=prb[:, 0:n], in0=rT[:, 0:n], scalar=pvec[:, PV_RK + hp:PV_RK + hp + 1],
                                                                     in1=t1_[:, 0:n], op0=ALU.mult, op1=ALU.mult),
                             reads=[rn(rT), rn(t1_), "pvec"], writes=["prb"])
                        pb = bank()
                        for cl in range(nch):
                            S.op("pe", lambda e, cl=cl: e.matmul(psb[pb][:, cl * 2:cl * 2 + 2], prb[:, cl * 128:(cl + 1) * 128], ind2[:],
                                                                 start=(cl == 0), stop=(cl == nch - 1)),
                                 reads=["prb", "ind2"], writes=[("ps", pb)], inc=(cl == nch - 1))
                        xc0 = c0 - NCC
                        S.op("act", lambda e: e.activation(out=BS[:, xc0:xc0 + nch, :], in_=psb[pb][:, 0:nch * 2].rearrange("p (c h) -> p c h", h=2), func=AF.Copy),
                             reads=[("ps", pb)], writes=["BS"])
                    S.op("act", lambda e: e.activation(out=vb[:, 0:n], in_=vT[:, 0:n], func=AF.Copy), reads=[rn(vT)], writes=["vb"])
                    pv_ = bank()
                    pvv = gbc('C', psb[pv_][:])
                    for cl in range(nch):
                        S.op("pe", lambda e, cl=cl: e.transpose(pvv[:, cl * 128:(cl + 1) * 128], vb[:, cl * 128:(cl + 1) * 128], gid('C')),
                             reads=["vb", "ident_bf", "cst"], writes=[("ps", pv_)], inc=(cl == nch - 1))
                    S.op("act", lambda e: e.activation(out=Vtm[:, c0:c0 + nch, :], in_=pvv[:, 0:nch * 128].rearrange("p (c x) -> p c x", c=nch), func=AF.Copy),
                         reads=[("ps", pv_)], writes=["Vtm"])
                    for d in range(2):
                        for cl in range(nch):
                            S.op("dve", lambda e, cl=cl: e.tensor_tensor_scan(Pp[:, cl * 128:(cl + 1) * 128], ones_f[:, 0:128],
                                                                             sgw[d][:, cl * 128:(cl + 1) * 128], 0.0, ALU.mult, ALU.add),
                                 reads=[rn(sgw[d]), "ones_f"], writes=[rn(Pp)])
                        P3 = Pp[:, 0:n].rearrange("p (c t) -> p c t", c=nch)
                        tot = P3[:, :, 127:128].to_broadcast([128, nch, 128])
                        L3i = Lin[:, 0:n].rearrange("p (c t) -> p c t", c=nch)
                        L3e = Lex[:, 0:n].rearrange("p (c t) -> p c t", c=nch)
                        if d == 0:
                            S.op("dve", lambda e: e.tensor_copy(Lin[:, 0:n], Pp[:, 0:n]), reads=[rn(Pp)], writes=[rn(Lin)])
                            S.op("dve", lambda e: e.tensor_tensor(out=Lex[:, 0:n], in0=Pp[:, 0:n], in1=sgw[d][:, 0:n], op=ALU.subtract),
                                 reads=[rn(Pp), rn(sgw[d])], writes=[rn(Lex)])
                        else:
                            S.op("dve", lambda e: e.tensor_tensor(out=L3e, in0=tot, in1=P3, op=ALU.subtract),
                                 reads=[rn(Pp)], writes=[rn(Lex)])
                            S.op("dve", lambda e: e.tensor_tensor(out=Lin[:, 0:n], in0=Lex[:, 0:n], in1=sgw[d][:, 0:n], op=ALU.add),
                                 reads=[rn(Lex), rn(sgw[d])], writes=[rn(Lin)])
                        S.op("act", lambda e: e.activation(out=E2[:, 0:n], in_=Lin[:, 0:n], func=AF.Exp, scale=C0), reads=[rn(Lin)], writes=[rn(E2)])
                        S.op("act", lambda e: e.activation(out=E1[:, 0:n], in_=Lin[:, 0:n], func=AF.Exp, scale=-C0), reads=[rn(Lin)], writes=[rn(E1)])
                        S.op("act", lambda e: e.activation(out=E3[:, 0:n], in_=Lex[:, 0:n], func=AF.Exp, scale=-C0), reads=[rn(Lex)], writes=[rn(E3)])
                        S.op("act", lambda e: e.activation(out=WCt[:, 0:nch], in_=P3[:, :, 127], func=AF.Exp, scale=-C0), reads=[rn(Pp)], writes=["WCt"])
                        S.op("dve", lambda e: e.tensor_copy(WCs[:, d * NCH + c0:d * NCH + c0 + nch], WCt[:, 0:nch]),
                             reads=["WCt"], writes=["WCs"])
                        S.op("dve", lambda e: e.tensor_tensor(out=E4[:, 0:n].rearrange("p (c t) -> p c t", c=nch),
                                                              in0=E2[:, 0:n].rearrange("p (c t) -> p c t", c=nch),
                                                              in1=WCt[:, 0:nch].unsqueeze(2).to_broadcast([128, nch, 128]), op=ALU.mult),
                             reads=[rn(E2), "WCt"], writes=[rn(E4)])
                        S.op("dve", lambda e: e.tensor_tensor(out=QR[d][:, 0:nch, 0:128], in0=kk[:, 0:n].rearrange("p (c t) -> p c t", c=nch),
                                                              in1=E3[:, 0:n].rearrange("p (c t) -> p c t", c=nch), op=ALU.mult),
                             reads=[rn(kk), rn(E3)], writes=[("QR", tp, d)])
                        S.op("dve", lambda e: e.tensor_tensor(out=QR[d][:, 0:nch, 128:256], in0=rT[:, 0:n].rearrange("p (c t) -> p c t", c=nch),
                                                               in1=E1[:, 0:n].rearrange("p (c t) -> p c t", c=nch), op=ALU.mult),
                             reads=[rn(rT), rn(E1)], writes=[("QR", tp, d)])
                        S.op("dve", lambda e: e.tensor_tensor(out=Kh[d][:, 0:n], in0=kdir[d][:, 0:n], in1=E2[:, 0:n], op=ALU.mult),
                             reads=[rn(kdir[d]), rn(E2)], writes=[("Kh", tp, d)])
                        S.op("dve", lambda e: e.tensor_tensor(out=Bh[d][:, 0:n], in0=b_d[d][:, 0:n], in1=E2[:, 0:n], op=ALU.mult),
                             reads=[rn(b_d[d]), rn(E2)], writes=[("Bh", tp, d)])
                        S.op("dve", lambda e: e.tensor_tensor(out=Kd[d][:, 0:n], in0=kdir[d][:, 0:n], in1=E4[:, 0:n], op=ALU.mult),
                             reads=[rn(kdir[d]), rn(E4)], writes=[("Kd", tp, d)])
                        S.op("dve", lambda e: e.tensor_tensor(out=Bd[d][:, 0:n], in0=b_d[d][:, 0:n], in1=E4[:, 0:n], op=ALU.mult),
                             reads=[rn(b_d[d]), rn(E4)], writes=[("Bd", tp, d)])

                def run_chains(b, hp, e0, n, tp):
                    nch = n // 128
                    c0 = e0 // 128
                    work = [(cl, d) for cl in range(nch) for d in range(2)]
                    for w0 in range(0, len(work), NCHN):
                        gens = [chunk_pre(b, hp, c0 + cl, cl, d, k, tp) for k, (cl, d) in enumerate(work[w0:w0 + NCHN])]
                        while gens:
                            for g in list(gens):
                                try:
                                    next(g)
                                except StopIteration:
                                    gens.remove(g)

                def seq_step(d, cidx, last):
                    slot = d * NCH + cidx
                    if cidx >= NCC:
                        xc = cidx - NCC
                        pY = bank()
                        S.op("pe", lambda e: e.matmul(psb[pY][:, 0:128], GyTs[:, slot, :], Sbd[:, d, :], start=True, stop=True),
                             reads=[("GyTs", slot), ("Sbd", d)], writes=[("ps", pY)])
                        S.op("dve", lambda e: e.tensor_tensor(out=Yacc[:, xc, :], in0=psb[pY][:, 0:128], in1=Yacc[:, xc, :], op=ALU.add),
                             reads=[("ps", pY), ("Yacc", xc)], writes=[("Yacc", xc)])
                    if last:
                        return
                    pS = bank()
                    S.op("pe", lambda e: e.matmul(psb[pS][:, 0:64], MTs[:, slot, :], Sbf[:, d, :], start=True, stop=True),
                         reads=[("MTs", slot), ("Sbf", d)], writes=[("ps", pS)])
                    S.op("dve", lambda e: e.scalar_tensor_tensor(out=S32[:, d, :], in0=S32[:, d, :], scalar=WCs[:, slot:slot + 1],
                                                                 in1=Sadds[:, slot, :], op0=ALU.mult, op1=ALU.add),
                         reads=[("S32", d), "WCs", ("Sadds", slot)], writes=[("S32", d)])
                    S.op("dve", lambda e: e.tensor_tensor(out=S32[:, d, :], in0=psb[pS][:, 0:64], in1=S32[:, d, :], op=ALU.add),
                         reads=[("ps", pS), ("S32", d)], writes=[("S32", d)])
                    S.op("act", lambda e: e.activation(out=Sbf[:, d, :], in_=S32[:, d, :], func=AF.Copy),
                         reads=[("S32", d)], writes=[("Sbf", d)])
                    for h in range(2):
                        hs = slice(h * 64, (h + 1) * 64)
                        S.op("act", lambda e: e.activation(out=Sbd[hs, d, hs], in_=S32[hs, d, :], func=AF.Copy),
                             reads=[("S32", d)], writes=[("Sbd", d)])

                def readout(b, hp):
                    NH = (NXC + 1) // 2
                    for xh in range(0, NXC, NH):
                        readout_half(b, hp, xh, min(NH, NXC - xh))
                    S.op("sp", lambda e: e.dma_start(out=yaT_d[b, hp, :, :], in_=yaT[:]), reads=["yaT"], writes=[("yaT", b)], dsem="d_yaT")

                def readout_half(b, hp, xh, NXH):
                    n2 = NXH * 2
                    Y3 = Yacc[:, xh:xh + NXH, :].rearrange("p c (h v) -> p (c h) v", h=2)
                    yc3 = yc[:, 0:NXH, :].rearrange("p c (h v) -> p (c h) v", h=2)
                    sq3 = ysq[:, 0:NXH, :].rearrange("p c (h v) -> p (c h) v", h=2)
                    S.op("dve", lambda e: e.tensor_reduce(out=st1[:, 0:n2], in_=Y3, axis=AX.X, op=ALU.add),
                         reads=[("Yacc", x) for x in range(xh, xh + NXH)], writes=["st1"])
                    S.op("dve", lambda e: e.tensor_scalar(st1[:, 0:n2], st1[:, 0:n2], 1.0 / 64.0, None, ALU.mult), reads=["st1"], writes=["st1"])
                    S.op("dve", lambda e: e.tensor_tensor(out=yc3, in0=Y3, in1=st1[:, 0:n2].unsqueeze(2).to_broadcast([128, n2, 64]), op=ALU.subtract),
                         reads=[("Yacc", x) for x in range(xh, xh + NXH)] + ["st1"], writes=["yc"])
                    S.op("pool", lambda e: e.tensor_tensor(out=ysq[:, 0:NXH, :], in0=yc[:, 0:NXH, :], in1=yc[:, 0:NXH, :], op=ALU.mult), reads=["yc"], writes=["ysq"])
                    S.op("dve", lambda e: e.tensor_reduce(out=st2[:, 0:n2], in_=sq3, axis=AX.X, op=ALU.add), reads=["ysq"], writes=["st2"])
                    S.op("act", lambda e: e.activation(out=st2[:, 0:n2], in_=st2[:, 0:n2], func=AF.Sqrt, bias=epsc[:, 1:2], scale=1.0 / 64.0),
                         reads=["st2", "epsc"], writes=["st2"])
                    S.op("dve", lambda e: e.reciprocal(st2[:, 0:n2], st2[:, 0:n2]), reads=["st2"], writes=["st2"])
                    S.op("dve", lambda e: e.tensor_tensor(out=yc3, in0=yc3, in1=st2[:, 0:n2].unsqueeze(2).to_broadcast([128, n2, 64]), op=ALU.mult),
                         reads=["yc", "st2"], writes=["yc"])
                    lg_b = brow[:, BR_LNG + hp * 128:BR_LNG + (hp + 1) * 128].unsqueeze(1).to_broadcast([128, NXH, 128])
                    lb_b = brow[:, BR_LNB + hp * 128:BR_LNB + (hp + 1) * 128].unsqueeze(1).to_broadcast([128, NXH, 128])
                    S.op("pool", lambda e: e.tensor_tensor(out=yc[:, 0:NXH, :], in0=yc[:, 0:NXH, :], in1=lg_b, op=ALU.mult), reads=["yc", "brow"], writes=["yc"])
                    S.op("pool", lambda e: e.tensor_tensor(out=yc[:, 0:NXH, :], in0=yc[:, 0:NXH, :], in1=lb_b, op=ALU.add), reads=["yc", "brow"], writes=["yc"])
                    V3 = Vtm[:, NCC + xh:NCC + xh + NXH, :].rearrange("p c (h v) -> p (c h) v", h=2)
                    S.op("dve", lambda e: e.tensor_tensor(out=sq3, in0=V3,
                                                          in1=BS[:, xh:xh + NXH, :].rearrange("p c h -> p (c h)").unsqueeze(2).to_broadcast([128, n2, 64]), op=ALU.mult),
                         reads=["Vtm", "BS"], writes=["ysq"])
                    S.op("pool", lambda e: e.tensor_tensor(out=yc[:, 0:NXH, :], in0=yc[:, 0:NXH, :], in1=ysq[:, 0:NXH, :], op=ALU.add), reads=["yc", "ysq"], writes=["yc"])
                    for x0 in range(xh, xh + NXH, 4):
                        pg = bank()
                        nx = min(4, xh + NXH - x0)
                        for xi in range(nx):
                            xc = x0 + xi
                            S.op("pe", lambda e, xi=xi, xc=xc: e.matmul(psb[pg][:, xi * 128:(xi + 1) * 128],
                                                                        lg[:, CTX + xc * 128:CTX + (xc + 1) * 128], g2t[:, hp * 128:(hp + 1) * 128],
                                                                        start=(xi == 0), stop=(xi == nx - 1)),
                                 reads=["lg", "W"], writes=[("ps", pg)], inc=(xi == nx - 1))
                        S.op("dve", lambda e: e.tensor_tensor(out=yab[:, x0:x0 + nx, :], in0=psb[pg][:, 0:nx * 128].rearrange("p (c x) -> p c x", c=nx),
                                                              in1=yc[:, x0 - xh:x0 - xh + nx, :], op=ALU.mult),
                             reads=[("ps", pg), "yc"], writes=["yab"])
                    for x0 in range(xh, xh + NXH, TPB):
                        pt = bank()
                        ptv = gbc('R', psb[pt][:])
                        nx = min(TPB, xh + NXH - x0)
                        for xi in range(nx):
                            S.op("pe", lambda e, xi=xi: e.transpose(ptv[:, xi * 128:(xi + 1) * 128], yab[:, x0 + xi, :], gid('R')),
                                 reads=["yab", "ident_bf", "cst"], writes=[("ps", pt)], inc=(xi == nx - 1))
                        S.op("act", lambda e: e.activation(out=yaT[:, x0 * 128:(x0 + nx) * 128], in_=ptv[:, 0:nx * 128], func=AF.Copy),
                             reads=[("ps", pt)], writes=["yaT"])

                tl = tile_list(True)
                wk_next = load_w(None)
                for b in range(NB):
                    btiles = [t for t in tl if t[0] == b]
                    wk_l = wk_next
                    wk_next = load_w(0)
                    for (_, r, e0, n) in btiles:
                        hb, lo, hi, off = load_h(b, e0, n)
                        proj_conv(hb, lo, hi, off, e0, n, 12, t0_[:, 0:n], wk_l, 0)
                        S.op("act", lambda e: e.activation(out=lw[:, e0:e0 + n], in_=t0_[:, 0:n], func=AF.Tanh), reads=[rn(t0_)], writes=["lw"])
                        proj_conv(hb, lo, hi, off, e0, n, 13, t1_[:, 0:n], wk_l, 1)
                        S.op("act", lambda e: e.activation(out=la[:, e0:e0 + n], in_=t1_[:, 0:n], func=AF.Copy), reads=[rn(t1_)], writes=["la"])
                        proj_conv(hb, lo, hi, off, e0, n, 14, t0_[:, 0:n], wk_l, 2)
                        S.op("act", lambda e: e.activation(out=lg[:, e0:e0 + n], in_=t0_[:, 0:n], func=AF.Sigmoid), reads=[rn(t0_)], writes=["lg"])
                    cut = cfg.get("m2cut", 99)
                    for hp in range(4):
                        if cut <= 1:
                            break
                        wk_h = wk_next
                        if hp < 3:
                            wk_next = load_w(hp + 1)
                        elif b + 1 < NB:
                            wk_next = load_w(None)
                        def prep_t(ti_):
                            (_, r_, e0_, n_) = btiles[ti_]
                            hb, lo, hi, off = load_h(b, e0_, n_)
                            tile_prep(b, hp, hb, lo, hi, off, e0_, n_, wk_h, ti_ % 2)

                        prep_t(0)
                        for ti_ in range(len(btiles)):
                            if ti_ + 1 < len(btiles):
                                prep_t(ti_ + 1)
                            if cut > 2:
                                run_chains(b, hp, btiles[ti_][2], btiles[ti_][3], ti_ % 2)
                        if cut < 4:
                            continue
                        S.op("dve", lambda e: e.memset(S32[:], 0.0), writes=[("S32", 0), ("S32", 1)])
                        S.op("dve", lambda e: e.memset(Sbf[:], 0.0), writes=[("Sbf", 0), ("Sbf", 1)])
                        S.op("pool", lambda e: e.memset(Sbd[:], 0.0), writes=[("Sbd", 0), ("Sbd", 1)])
                        fwd = list(range(NCH))
                        bwd = list(range(NCC - 1, -1, -1)) + list(range(NCH - 1, NCC - 1, -1))
                        for i in range(NCH):
                            seq_step(0, fwd[i], i == NCH - 1)
                            seq_step(1, bwd[i], i == NCH - 1)
                        if cut <= 4:
                            continue
                        readout(b, hp)
                S.barrier()
                S.emit()

        def finish():
            S.wait_all("sp")
            S.emit()
            return nc

        ffn_phase(0, f1_in_d, f1_out_d, True)
        if stop == "ffn1":
            return finish()
        m1_phase()
        if stop == "m1":
            return finish()
        if cfg.get("skip_m2") is None:
            m2_phase()
        if stop == "m2":
            return finish()
        m3_phase()
        if stop == "m3":
            return finish()
        m4_phase()
        if stop == "m4":
            return finish()
        ffn_phase(2, f2_in_d, f2_out_d, False)
        return finish()
    return nc


def _consts():
    c = np.zeros((128, NCONST), np.float32)
    i = np.arange(128)[:, None]
    t = np.arange(128)[None, :]
    c[:, CS_ID:CS_ID + 128] = (i == t)
    c[:, CS_FS:CS_FS + 128] = (i < t)
    c[:, CS_FI:CS_FI + 128] = (i <= t)
    c[:, CS_BS:CS_BS + 128] = (i > t)
    c[:, CS_BI:CS_BI + 128] = (i >= t)
    c[:, CS_BONES:CS_BONES + 128] = ((i // 64) == (t // 64))
    c[:, CS_IND + 0] = (np.arange(128) < 64)
    c[:, CS_IND + 1] = (np.arange(128) >= 64)
    return c


def _rope_tables(T):
    pos = np.arange(T)
    row = (pos // 64).astype(np.float32)
    col = (pos % 64).astype(np.float32)
    freqs = (np.float32(10000.0) ** (-np.arange(16, dtype=np.float32) / np.float32(16))).astype(np.float32)
    cosT = np.zeros((128, T), np.float32)
    sinT = np.zeros((128, T), np.float32)
    for p in range(128):
        d = p % 64
        half, j = d // 32, d % 32
        f = j % 16
        ang = ((row if half == 0 else col) * freqs[f]).astype(np.float32)
        cosT[p] = np.cos(ang)
        sinT[p] = (-np.sin(ang) if j < 16 else np.sin(ang))
    return cosT, sinT


def host_prep(inputs, cfg, core):
    T, CTX, NB = cfg["T"], cfg["CTX"], cfg["NB"]
    f = lambda a: np.ascontiguousarray(np.asarray(a, dtype=np.float32))
    b0 = core * NB
    m = {}
    m["x"] = f(inputs["x"][b0:b0 + NB])
    m["ctx"] = f(inputs["ctx"][b0:b0 + NB])
    cond = np.concatenate([np.asarray(inputs["c"])[b0:b0 + NB], np.asarray(inputs["c_ctx"])[None, :]], 0)
    m["condT"] = f(cond.reshape(NB + 1, KD, 128).transpose(2, 1, 0))
    pv = np.zeros((128, NPV), np.float32)
    pv[:, PV_ADAB:PV_ADAB + 72] = np.asarray(inputs["ada_b"])[0].reshape(72, 128).T
    pv[:, PV_PRE:PV_PRE + 24] = np.asarray(inputs["pre_norm_g"])[0].reshape(3, 8, 128).transpose(2, 0, 1).reshape(128, 24)
    pv[:, PV_POST:PV_POST + 24] = np.asarray(inputs["post_norm_g"])[0].reshape(3, 8, 128).transpose(2, 0, 1).reshape(128, 24)
    pv[:, PV_CONV:PV_CONV + 45] = np.asarray(inputs["rwkv_shift_w"])[0].reshape(3, 15, 128).transpose(2, 0, 1).reshape(128, 45)
    pv[:, PV_W0:PV_W0 + 8] = np.asarray(inputs["rwkv_w0"])[0].reshape(2, 4, 128).transpose(2, 0, 1).reshape(128, 8)
    pv[:, PV_A0:PV_A0 + 8] = np.asarray(inputs["rwkv_a0"])[0].reshape(2, 4, 128).transpose(2, 0, 1).reshape(128, 8)
    pv[:, PV_KK:PV_KK + 4] = np.asarray(inputs["rwkv_k_k"])[0].reshape(4, 128).T
    pv[:, PV_KA:PV_KA + 4] = np.asarray(inputs["rwkv_k_a"])[0].reshape(4, 128).T
    pv[:, PV_RK:PV_RK + 4] = np.asarray(inputs["rwkv_r_k"])[0].reshape(4, 128).T
    pv[:, PV_SUBG] = np.asarray(inputs["diff_subln_g"])[0]
    m["pvec"] = pv
    br = np.zeros((1, NBR), np.float32)
    br[0, BR_LNG:BR_LNG + 512] = np.asarray(inputs["rwkv_ln_g"])[0]
    br[0, BR_LNB:BR_LNB + 512] = np.asarray(inputs["rwkv_ln_b"])[0]
    br[0, BR_SUB:BR_SUB + 128] = np.asarray(inputs["diff_subln_g"])[0]
    br[0, BR_LAM:BR_LAM + 256] = np.asarray(inputs["diff_lambda"])[0].reshape(256)
    m["brow"] = br
    m["consts"] = _consts()
    m["cosT"], m["sinT"] = _rope_tables(T)
    for k in ("ada_w", "ffn1_w_in", "ffn1_w_out", "ffn2_w_in", "ffn2_w_out", "mix_w_in", "rwkv_w2", "rwkv_a2",
              "rwkv_g2", "branch_up_a", "branch_up_b", "mix_w_out"):
        m[k] = f(np.asarray(inputs[k])[0])
    perm = np.arange(1024)
    j = perm % 32
    perm = perm - j + np.where(j < 16, j + 16, j - 16)
    m["w_qk_swap"] = f(m["mix_w_in"][:, 1920 + perm])
    return m


FULL = {"T": 2048, "CTX": 256, "NB": 2}
_NC_CACHE = {}


def kernel(**inputs):
    cfg = FULL
    if "nc" not in _NC_CACHE:
        _NC_CACHE["nc"] = build(cfg)
    nc = _NC_CACHE["nc"]
    ncores = 8
    in_maps = [host_prep(inputs, cfg, c) for c in range(ncores)]
    res = run_bass_kernel_spmd(nc, in_maps, core_ids=list(range(ncores)))
    return np.concatenate([np.asarray(r["out"]) for r in res.results], axis=0).astype(np.float32)
```

```python
import types
import numpy as np
from contextlib import ExitStack
import concourse.bass as bass
import concourse.mybir as mybir
from concourse.bass_utils import run_bass_kernel_spmd

F32 = mybir.dt.float32
BF16 = mybir.dt.bfloat16
AF = mybir.ActivationFunctionType
ALU = mybir.AluOpType
AX = mybir.AxisListType

D = 1024
KD = 8
FF = 2816
KF = 22
NMOD = 9
MIXC = 5504
TT = 256
C0 = 0.6065306597126334

PV_ADAB, PV_PRE, PV_POST, PV_CONV, PV_W0, PV_A0, PV_KK, PV_KA, PV_RK, NPV = 0, 72, 96, 120, 165, 173, 181, 185, 189, 194
PV_SUBG = 193
BR_LNG, BR_LNB, BR_SUB, BR_LAM, NBR = 0, 512, 1024, 1152, 1408
CS_ID, CS_FS, CS_FI, CS_BS, CS_BI, CS_BONES, CS_IND, NCONST = 0, 128, 256, 384, 512, 640, 768, 770


def _freeze(fn):
    if fn is None or fn.__closure__ is None:
        return fn
    cells = []
    for c in fn.__closure__:
        try:
            cells.append(types.CellType(c.cell_contents))
        except ValueError:
            cells.append(c)
    return types.FunctionType(fn.__code__, fn.__globals__, fn.__name__, fn.__defaults__, tuple(cells))


class Sched:
    ENGS = ("pe", "act", "dve", "pool", "sp")
    LIM = 30000

    def __init__(self, nc, stack):
        self.nc, self.stack = nc, stack
        self.prog = {e: [] for e in self.ENGS}
        self.sems = {}
        self.cnt = {}
        self.known = {e: {} for e in self.ENGS}
        self.lw = {}
        self.lr = {}
        self.nins = 0

    def sem(self, key, ep):
        k = (key, ep)
        if k not in self.sems:
            self.sems[k] = self.stack.enter_context(self.nc.semaphore("s_%s_%d" % (key, ep)))
        return self.sems[k]

    def _next(self, key, amt, commit):
        ep, v = self.cnt.get(key, (0, 0))
        if v + amt > self.LIM:
            ep, v = ep + 1, 0
        v += amt
        if commit:
            self.cnt[key] = (ep, v)
        return (key, ep, v)

    def op(self, eng, fn, reads=(), writes=(), inc=True, dsem=None):
        fn = _freeze(fn)
        evs = []
        for r in reads:
            if r in self.lw:
                evs.append(self.lw[r])
        for w in writes:
            if w in self.lw:
                evs.append(self.lw[w])
            for k, (ep, v) in self.lr.get(w, {}).items():
                evs.append((k, ep, v))
        need = {}
        for (k, ep, v) in evs:
            if k == eng and eng == "pe":
                continue
            if self.known[eng].get(k, (-1, 0)) >= (ep, v):
                continue
            if need.get(k, (-1, 0)) < (ep, v):
                need[k] = (ep, v)
        for k, ev in need.items():
            self.known[eng][k] = ev
        if dsem is not None:
            my = self._next(dsem, 16, True)
        else:
            my = self._next(eng, 1, inc)
        self.prog[eng].append((fn, [(k, ep, v) for k, (ep, v) in need.items()],
                               my if (inc or dsem is not None) else None, dsem is not None))
        self.nins += 1
        for r in reads:
            d = self.lr.setdefault(r, {})
            if d.get(my[0], (-1, 0)) < my[1:]:
                d[my[0]] = my[1:]
        for w in writes:
            self.lw[w] = my
            self.lr[w] = {}
        return my

    def barrier(self):
        for eng in self.ENGS:
            need = []
            for k, (ep, v) in self.cnt.items():
                if v == 0:
                    continue
                if k == eng:
                    continue
                if self.known[eng].get(k, (-1, 0)) >= (ep, v):
                    continue
                self.known[eng][k] = (ep, v)
                need.append((k, ep, v))
            if need:
                self.prog[eng].append((None, need, None, False))

    def wait_all(self, eng):
        need = []
        for k, (ep, v) in self.cnt.items():
            if v == 0 or self.known[eng].get(k, (-1, 0)) >= (ep, v):
                continue
            self.known[eng][k] = (ep, v)
            need.append((k, ep, v))
        if need:
            self.prog[eng].append((None, need, None, False))

    def emit(self):
        for key, (ep, v) in list(self.cnt.items()):
            for e in range(ep + 1):
                self.sem(key, e)
        for eng in self.ENGS:
            for (_, need, my, _) in self.prog[eng]:
                for (k, ep, v) in need:
                    self.sem(k, ep)
        with self.nc.Block() as blk:
            names = {"pe": "tensor", "act": "scalar", "dve": "vector", "pool": "gpsimd", "sp": "sync"}
            for eng in self.ENGS:
                prog = self.prog[eng]

                def body(e, prog=prog):
                    for fn, need, my, isdma in prog:
                        for (k, ep, v) in need:
                            e.wait_ge(self.sems[(k, ep)], v)
                        if fn is None:
                            continue
                        ins = fn(e)
                        if my is not None:
                            ins.then_inc(self.sems[(my[0], my[1])], 16 if isdma else 1)
                getattr(blk, names[eng])(body)
        self.prog = {e: [] for e in self.ENGS}


def build(cfg):
    T, CTX, NB = cfg["T"], cfg["CTX"], cfg["NB"]
    stop = cfg.get("stop", "all")
    dbg = cfg.get("dbg", False)
    E = CTX + T
    NR = NB + 1
    assert T % TT == 0 and CTX % 128 == 0
    nc = bass.Bass("TRN2", target_bir_lowering=False)

    def din(name, shape, dt=F32):
        return nc.dram_tensor(name, list(shape), dt, kind="ExternalInput").ap()

    x_d = din("x", [NB, T, D])
    ctx_d = din("ctx", [NB, CTX, D])
    cond_d = din("condT", [128, KD, NR])
    pvec_d = din("pvec", [128, NPV])
    brow_d = din("brow", [1, NBR])
    consts_d = din("consts", [128, NCONST])
    cos_d = din("cosT", [128, T])
    sin_d = din("sinT", [128, T])
    ada_w_d = din("ada_w", [D, NMOD * D])
    f1_in_d = din("ffn1_w_in", [D, 2 * FF])
    f1_out_d = din("ffn1_w_out", [FF, D])
    f2_in_d = din("ffn2_w_in", [D, 2 * FF])
    f2_out_d = din("ffn2_w_out", [FF, D])
    mix_in_d = din("mix_w_in", [D, MIXC])
    qk_sw_d = din("w_qk_swap", [D, 1024])
    w2_d = din("rwkv_w2", [2, 64, 512])
    a2_d = din("rwkv_a2", [2, 64, 512])
    g2_d = din("rwkv_g2", [128, 512])
    upa_d = din("branch_up_a", [512, D])
    upb_d = din("branch_up_b", [512, D])
    mo_d = din("mix_w_out", [D, D])
    out_d = nc.dram_tensor("out", [NB, T, D], F32, kind="ExternalOutput").ap()

    okind = "ExternalOutput" if dbg else "Internal"
    x1T_d = nc.dram_tensor("x1T", [NB, KD, 128, E], F32, kind=okind).ap()
    x2T_d = nc.dram_tensor("x2T", [NB, KD, 128, T], F32, kind=okind).ap()
    hT_d = nc.dram_tensor("hT_s", [NB, KD, 128, E], BF16, kind=okind).ap()
    yaT_d = nc.dram_tensor("yaT_s", [NB, 4, 128, T], BF16, kind=okind).ap()
    ybT_d = nc.dram_tensor("ybT_s", [NB, 4, 128, T], BF16, kind=okind).ap()
    mod_dbg = nc.dram_tensor("mod_dbg", [128, 72 * NR], F32, kind=okind).ap()

    stack = ExitStack()
    with stack:
        S = Sched(nc, stack)

        uid = [0]

        def sb(name, shape, dt=F32, st=stack):
            uid[0] += 1
            return st.enter_context(nc.sbuf_tensor("t%d_%s" % (uid[0], name), list(shape), dt))

        pvec = sb("pvec", [128, NPV])
        brow = sb("brow", [128, NBR])
        cst = sb("cst", [128, NCONST])
        modT = sb("modT", [128, 72, NR])
        GE = sb("GE", [128, 3, NR, KD])
        CG = sb("CG", [128, 3, NR, KD])
        ident_bf = sb("ident_bf", [128, 128], BF16)
        onesD = sb("onesD", [128, 128], BF16)
        lam = sb("lam", [128, 4])
        epsc = sb("epsc", [128, 4])
        ones_f = sb("ones_f", [128, 128])
        psb = [stack.enter_context(nc.psum_tensor("ps%d" % i, [128, 512], F32)) for i in range(8)]
        pst = {"i": 0}

        pst["set"] = list(range(8))

        def bank():
            st_ = pst["set"]
            pst["i"] = (pst["i"] + 1) % len(st_)
            return st_[pst["i"]]

        ident32 = cst[:, CS_ID:CS_ID + 128]

        with ExitStack() as ph:
            condT = sb("condT", [128, KD, NR], st=ph)
            siluT = sb("siluT", [128, KD, NR], st=ph)
            wblk = [sb("wblk%d" % i, [128, KD, 1152], st=ph) for i in range(2)]
            S.op("sp", lambda e: e.dma_start(out=pvec[:], in_=pvec_d), writes=["pvec"], dsem="d_pvec")
            S.op("sp", lambda e: e.dma_start(out=brow[:], in_=brow_d.partition_broadcast(128)[:, 0, :]),
                 writes=["brow"], dsem="d_brow")
            S.op("sp", lambda e: e.dma_start(out=cst[:], in_=consts_d), writes=["cst"], dsem="d_cst")
            S.op("sp", lambda e: e.dma_start(out=condT[:], in_=cond_d), writes=["condT"], dsem="d_cond")
            S.op("act", lambda e: e.activation(out=siluT[:], in_=condT[:], func=AF.Silu),
                 reads=["condT"], writes=["siluT"])
            S.op("dve", lambda e: e.tensor_copy(ident_bf[:], ident32), reads=["cst"], writes=["ident_bf"])
            S.op("dve", lambda e: e.memset(onesD[:], 1.0 / D), writes=["onesD"])
            S.op("dve", lambda e: e.memset(epsc[:, 0:1], 1e-6), writes=["epsc"])
            S.op("dve", lambda e: e.memset(ones_f[:], 1.0), writes=["ones_f"])
            S.op("dve", lambda e: e.memset(epsc[:, 1:2], 64e-5), writes=["epsc"])
            S.op("dve", lambda e: e.memset(epsc[:, 2:3], 1e-12), writes=["epsc"])
            aw = ada_w_d.rearrange("(k p) n -> p k n", p=128)
            pm = bank()
            for cb in range(8):
                wb = wblk[cb % 2]
                S.op("sp", lambda e, wb=wb, cb=cb: e.dma_start(out=wb[:], in_=aw[:, :, cb * 1152:(cb + 1) * 1152]),
                     writes=[("wblk", cb % 2)], dsem="d_wblk%d" % (cb % 2))
                for jj in range(9):
                    j = cb * 9 + jj
                    for kc in range(KD):
                        S.op("pe", lambda e, wb=wb, jj=jj, j=j, kc=kc: e.matmul(
                            psb[pm][:, j * NR:(j + 1) * NR], wb[:, kc, jj * 128:(jj + 1) * 128], siluT[:, kc, :],
                            start=(kc == 0), stop=(kc == KD - 1)),
                            reads=[("wblk", cb % 2), "siluT"], writes=[("ps", pm)],
                            inc=(kc == KD - 1 and jj == 8))
            S.op("dve", lambda e: e.tensor_tensor(
                out=modT[:], in0=psb[pm][:, 0:72 * NR].rearrange("p (j r) -> p j r", r=NR),
                in1=pvec[:, PV_ADAB:PV_ADAB + 72].unsqueeze(2).to_broadcast([128, 72, NR]), op=ALU.add),
                reads=[("ps", pm), "pvec"], writes=["modT"])
            for s in range(3):
                half = 1.0 if s == 1 else 0.5
                for r in range(NR):
                    S.op("dve", lambda e, s=s, r=r: e.scalar_tensor_tensor(
                        out=GE[:, s, r, :], in0=modT[:, (3 * s + 1) * 8:(3 * s + 2) * 8, r], scalar=1.0,
                        in1=pvec[:, PV_PRE + s * 8:PV_PRE + s * 8 + 8], op0=ALU.add, op1=ALU.mult),
                        reads=["modT", "pvec"], writes=["GE"])
                    S.op("dve", lambda e, s=s, r=r, half=half: e.scalar_tensor_tensor(
                        out=CG[:, s, r, :], in0=modT[:, (3 * s + 2) * 8:(3 * s + 3) * 8, r], scalar=half,
                        in1=pvec[:, PV_POST + s * 8:PV_POST + s * 8 + 8], op0=ALU.mult, op1=ALU.mult),
                        reads=["modT", "pvec"], writes=["CG"])
            if dbg:
                S.op("sp", lambda e: e.dma_start(out=mod_dbg, in_=modT[:].rearrange("p j r -> p (j r)")),
                     reads=["modT"], writes=["mod_dbg"], dsem="d_dbg")
            S.barrier()
            S.emit()

        def ffn_phase(sub, w_in_d, w_out_d, first):
            with ExitStack() as ph:
                win = sb("win", [128, KD, 2 * FF], BF16, st=ph)
                wout = sb("wout", [128, KF, D], BF16, st=ph)
                xT = [sb("xT%d" % i, [128, KD, TT], st=ph) for i in range(2)]
                tmp = sb("tmp", [128, KD, TT], st=ph)
                hT = [sb("hT%d" % i, [128, KD, TT], BF16, st=ph) for i in range(2)]
                actT = sb("actT", [128, KF, TT], BF16, st=ph)
                sq = sb("sq", [128, KD, TT], BF16, st=ph)
                rstd = sb("rstd", [128, TT], st=ph)
                sg = [sb("sg%d" % i, [128, TT], st=ph) for i in range(2)]
                xio = sb("xio", [128, TT // 128, D], st=ph)
                wi = w_in_d.rearrange("(k p) n -> p k n", p=128)
                wo = w_out_d.rearrange("(k p) n -> p k n", p=128)
                for kc in range(KD):
                    S.op("pool", lambda e, kc=kc: e.dma_start(out=win[:, kc, :], in_=wi[:, kc, :]),
                         writes=(["W"] if kc == KD - 1 else []), dsem="d_w")
                for q in range(2):
                    S.op("pool", lambda e, q=q: e.dma_start(out=wout[:, q * 11:(q + 1) * 11, :], in_=wo[:, q * 11:(q + 1) * 11, :]),
                         writes=(["W2"] if q == 1 else []), dsem="d_w2")
                tiles = []
                for b in range(NB):
                    if first:
                        for s0 in range(0, CTX, TT):
                            tiles.append((b, NB, "ctx", s0, s0, min(TT, CTX - s0)))
                    for s0 in range(0, T, TT):
                        tiles.append((b, b, "x", s0, CTX + s0, TT))

                def load(i):
                    b, r, kind, s0, e0, n = tiles[i]
                    buf = i % 2
                    if first:
                        src = (ctx_d if kind == "ctx" else x_d)[b, s0:s0 + n, :].rearrange("(s p) d -> p s d", p=128)
                        S.op("sp", lambda e: e.dma_start(out=xio[:, 0:n // 128, :], in_=src),
                             writes=["xio"], dsem="d_xio")
                        for kp in range(KD // 2):
                            pb = bank()
                            for k2 in range(2):
                                kc = kp * 2 + k2
                                for s in range(n // 128):
                                    S.op("pe", lambda e, pb=pb, k2=k2, kc=kc, s=s: e.transpose(
                                        psb[pb][:, k2 * TT + s * 128:k2 * TT + (s + 1) * 128],
                                        xio[:, s, kc * 128:(kc + 1) * 128], ident32),
                                        reads=["xio", "cst"], writes=[("ps", pb)],
                                        inc=(k2 == 1 and s == n // 128 - 1))
                            eng = "act" if kp % 2 == 0 else "dve"
                            if eng == "act":
                                S.op("act", lambda e, pb=pb, kp=kp, buf=buf: e.activation(
                                    out=xT[buf][:, 2 * kp:2 * kp + 2, 0:n],
                                    in_=psb[pb][:].rearrange("p (k t) -> p k t", k=2)[:, :, 0:n], func=AF.Copy),
                                    reads=[("ps", pb)], writes=[("xT", buf)])
                            else:
                                S.op("dve", lambda e, pb=pb, kp=kp, buf=buf: e.tensor_copy(
                                    xT[buf][:, 2 * kp:2 * kp + 2, 0:n],
                                    psb[pb][:].rearrange("p (k t) -> p k t", k=2)[:, :, 0:n]),
                                    reads=[("ps", pb)], writes=[("xT", buf)])
                    else:
                        S.op("sp", lambda e: e.dma_start(
                            out=xT[buf][:, :, 0:n], in_=x2T_d[b, :, :, s0:s0 + n].rearrange("k p t -> p k t")),
                            reads=[("x2T", b, s0)], writes=[("xT", buf)], dsem="d_xT%d" % buf)

                def rms_stats(src_tile, n):
                    S.op("act", lambda e: e.activation(out=sq[:, :, 0:n], in_=src_tile[:, :, 0:n], func=AF.Square),
                         reads=[src_tile.name_res], writes=["sq"])
                    pb = bank()
                    for kc in range(KD):
                        S.op("pe", lambda e, kc=kc, pb=pb: e.matmul(psb[pb][:, 0:n], onesD[:], sq[:, kc, 0:n],
                                                                     start=(kc == 0), stop=(kc == KD - 1)),
                             reads=["sq", "onesD"], writes=[("ps", pb)], inc=(kc == KD - 1))
                    S.op("act", lambda e, pb=pb: e.activation(out=rstd[:, 0:n], in_=psb[pb][:, 0:n], func=AF.Sqrt,
                                                              bias=epsc[:, 0:1], scale=1.0),
                         reads=[("ps", pb), "epsc"], writes=["rstd"])
                    S.op("dve", lambda e: e.reciprocal(rstd[:, 0:n], rstd[:, 0:n]), reads=["rstd"], writes=["rstd"])

                class TV:
                    def __init__(self, t, res):
                        self.t, self.name_res = t, res

                    def __getitem__(self, k):
                        return self.t[k]

                def prenorm(i):
                    b, r, kind, s0, e0, n = tiles[i]
                    buf = i % 2
                    xb = TV(xT[buf], ("xT", buf))
                    rms_stats(xb, n)
                    S.op("dve", lambda e, buf=buf: e.tensor_tensor(
                        out=tmp[:, :, 0:n], in0=xT[buf][:, :, 0:n],
                        in1=rstd[:, 0:n].unsqueeze(1).to_broadcast([128, KD, n]), op=ALU.mult),
                        reads=[("xT", buf), "rstd"], writes=["tmp"])
                    for kc in range(KD):
                        S.op("act", lambda e, kc=kc, r=r: e.activation(
                            out=hT[buf][:, kc, 0:n], in_=tmp[:, kc, 0:n], func=AF.Identity,
                            bias=modT[:, (3 * sub) * 8 + kc, r:r + 1], scale=GE[:, sub, r, kc:kc + 1]),
                            reads=["tmp", "modT", "GE"], writes=[("hT", buf)])

                def gateup(i):
                    b, r, kind, s0, e0, n = tiles[i]
                    buf = i % 2
                    for fc in range(KF):
                        pg, pu = bank(), bank()
                        for which, pb in ((0, pg), (1, pu)):
                            for kc in range(KD):
                                S.op("pe", lambda e, which=which, pb=pb, kc=kc, fc=fc: e.matmul(
                                    psb[pb][:, 0:n], win[:, kc, which * FF + fc * 128:which * FF + (fc + 1) * 128],
                                    hT[buf][:, kc, 0:n], start=(kc == 0), stop=(kc == KD - 1)),
                                    reads=[("hT", buf), "W"], writes=[("ps", pb)], inc=(kc == KD - 1))
                        sgb = fc % 2
                        S.op("act", lambda e, pg=pg, sgb=sgb: e.activation(out=sg[sgb][:, 0:n], in_=psb[pg][:, 0:n], func=AF.Silu),
                             reads=[("ps", pg)], writes=[("sg", sgb)])
                        S.op("dve", lambda e, pu=pu, sgb=sgb, fc=fc: e.tensor_tensor(
                            out=actT[:, fc, 0:n], in0=psb[pu][:, 0:n], in1=sg[sgb][:, 0:n], op=ALU.mult),
                            reads=[("ps", pu), ("sg", sgb)], writes=["actT"])

                def outproj(i):
                    b, r, kind, s0, e0, n = tiles[i]
                    buf = i % 2
                    for dc in range(KD):
                        pb = bank()
                        for fc in range(KF):
                            S.op("pe", lambda e, pb=pb, fc=fc, dc=dc: e.matmul(
                                psb[pb][:, 0:n], wout[:, fc, dc * 128:(dc + 1) * 128], actT[:, fc, 0:n],
                                start=(fc == 0), stop=(fc == KF - 1)),
                                reads=["actT", "W2"], writes=[("ps", pb)], inc=(fc == KF - 1))
                        S.op("act", lambda e, pb=pb, dc=dc: e.activation(out=tmp[:, dc, 0:n], in_=psb[pb][:, 0:n], func=AF.Copy),
                             reads=[("ps", pb)], writes=["tmp"])
                    rms_stats(TV(tmp, "tmp"), n)
                    S.op("dve", lambda e: e.tensor_tensor(
                        out=tmp[:, :, 0:n], in0=tmp[:, :, 0:n],
                        in1=rstd[:, 0:n].unsqueeze(1).to_broadcast([128, KD, n]), op=ALU.mult),
                        reads=["tmp", "rstd"], writes=["tmp"])
                    for dc in range(KD):
                        S.op("dve", lambda e, dc=dc, buf=buf, r=r: e.scalar_tensor_tensor(
                            out=xT[buf][:, dc, 0:n], in0=tmp[:, dc, 0:n], scalar=CG[:, sub, r, dc:dc + 1],
                            in1=xT[buf][:, dc, 0:n], op0=ALU.mult, op1=ALU.add),
                            reads=["tmp", "CG", ("xT", buf)], writes=[("xT", buf)])
                    if first:
                        S.op("sp", lambda e, b=b, e0=e0, buf=buf: e.dma_start(
                            out=x1T_d[b, :, :, e0:e0 + n].rearrange("k p t -> p k t"), in_=xT[buf][:, :, 0:n]),
                            reads=[("xT", buf)], writes=[("x1T", b, e0)], dsem="d_st%d" % buf)
                        rms_stats(TV(xT[buf], ("xT", buf)), n)
                        S.op("dve", lambda e, buf=buf: e.tensor_tensor(
                            out=tmp[:, :, 0:n], in0=xT[buf][:, :, 0:n],
                            in1=rstd[:, 0:n].unsqueeze(1).to_broadcast([128, KD, n]), op=ALU.mult),
                            reads=[("xT", buf), "rstd"], writes=["tmp"])
                        for kc in range(KD):
                            S.op("act", lambda e, kc=kc, r=r: e.activation(
                                out=hT[buf][:, kc, 0:n], in_=tmp[:, kc, 0:n], func=AF.Identity,
                                bias=modT[:, 3 * 8 + kc, r:r + 1], scale=GE[:, 1, r, kc:kc + 1]),
                                reads=["tmp", "modT", "GE"], writes=[("hT", buf)])
                        S.op("sp", lambda e, b=b, e0=e0, buf=buf: e.dma_start(
                            out=hT_d[b, :, :, e0:e0 + n].rearrange("k p t -> p k t"), in_=hT[buf][:, :, 0:n]),
                            reads=[("hT", buf)], writes=[("hTd", b)], dsem="d_hst%d" % buf)
                    else:
                        for s in range(n // 128):
                            for hf in range(2):
                                pb = bank()
                                for k4 in range(4):
                                    kc = hf * 4 + k4
                                    S.op("pe", lambda e, pb=pb, k4=k4, kc=kc, s=s, buf=buf: e.transpose(
                                        psb[pb][:, k4 * 128:(k4 + 1) * 128], xT[buf][:, kc, s * 128:(s + 1) * 128], ident32),
                                        reads=[("xT", buf), "cst"], writes=[("ps", pb)], inc=(k4 == 3))
                                if hf == 0:
                                    S.op("act", lambda e, pb=pb, s=s: e.activation(out=xio[:, s, 0:512], in_=psb[pb][:], func=AF.Copy),
                                         reads=[("ps", pb)], writes=["xio"])
                                else:
                                    S.op("dve", lambda e, pb=pb, s=s: e.tensor_copy(xio[:, s, 512:1024], psb[pb][:]),
                                         reads=[("ps", pb)], writes=["xio"])
                        S.op("sp", lambda e, b=b, s0=s0: e.dma_start(
                            out=out_d[b, s0:s0 + n, :].rearrange("(s p) d -> p s d", p=128), in_=xio[:, 0:n // 128, :]),
                            reads=["xio"], writes=[("out", b, s0)], dsem="d_out")
                load(0)
                prenorm(0)
                for i in range(len(tiles)):
                    if i + 1 < len(tiles):
                        load(i + 1)
                    gateup(i)
                    if i + 1 < len(tiles):
                        prenorm(i + 1)
                    outproj(i)
                S.barrier()
                S.emit()


        def tile_list(with_ctx):
            tiles = []
            for b in range(NB):
                if with_ctx:
                    for s0 in range(0, CTX, TT):
                        tiles.append((b, NB, s0, min(TT, CTX - s0)))
                for s0 in range(0, T, TT):
                    tiles.append((b, b, CTX + s0, TT))
            return tiles

        def m1_phase():
            with ExitStack() as ph:
                xT = [sb("xT%d" % i, [128, KD, TT], st=ph) for i in range(2)]
                tmp = sb("tmp", [128, KD, TT], st=ph)
                hT = [sb("hT%d" % i, [128, KD, TT], BF16, st=ph) for i in range(2)]
                sq = sb("sq", [128, KD, TT], BF16, st=ph)
                rstd = sb("rstd", [128, TT], st=ph)
                tiles = tile_list(True)

                def load(i):
                    b, r, e0, n = tiles[i]
                    buf = i % 2
                    S.op("sp", lambda e: e.dma_start(
                        out=xT[buf][:, :, 0:n], in_=x1T_d[b, :, :, e0:e0 + n].rearrange("k p t -> p k t")),
                        reads=[("x1T", b, e0)], writes=[("xT", buf)], dsem="d_xT%d" % buf)

                def do_tile(i):
                    b, r, e0, n = tiles[i]
                    buf = i % 2
                    if i + 1 < len(tiles):
                        load(i + 1)
                    S.op("act", lambda e: e.activation(out=sq[:, :, 0:n], in_=xT[buf][:, :, 0:n], func=AF.Square),
                         reads=[("xT", buf)], writes=["sq"])
                    pb = bank()
                    for kc in range(KD):
                        S.op("pe", lambda e, kc=kc: e.matmul(psb[pb][:, 0:n], onesD[:], sq[:, kc, 0:n],
                                                             start=(kc == 0), stop=(kc == KD - 1)),
                             reads=["sq", "onesD"], writes=[("ps", pb)], inc=(kc == KD - 1))
                    S.op("act", lambda e: e.activation(out=rstd[:, 0:n], in_=psb[pb][:, 0:n], func=AF.Sqrt,
                                                       bias=epsc[:, 0:1], scale=1.0),
                         reads=[("ps", pb), "epsc"], writes=["rstd"])
                    S.op("dve", lambda e: e.reciprocal(rstd[:, 0:n], rstd[:, 0:n]), reads=["rstd"], writes=["rstd"])
                    S.op("dve", lambda e: e.tensor_tensor(
                        out=tmp[:, :, 0:n], in0=xT[buf][:, :, 0:n],
                        in1=rstd[:, 0:n].unsqueeze(1).to_broadcast([128, KD, n]), op=ALU.mult),
                        reads=[("xT", buf), "rstd"], writes=["tmp"])
                    for kc in range(KD):
                        S.op("act", lambda e, kc=kc: e.activation(
                            out=hT[buf][:, kc, 0:n], in_=tmp[:, kc, 0:n], func=AF.Identity,
                            bias=modT[:, 3 * 8 + kc, r:r + 1], scale=GE[:, 1, r, kc:kc + 1]),
                            reads=["tmp", "modT", "GE"], writes=[("hT", buf)])
                    S.op("sp", lambda e: e.dma_start(
                        out=hT_d[b, :, :, e0:e0 + n].rearrange("k p t -> p k t"), in_=hT[buf][:, :, 0:n]),
                        reads=[("hT", buf)], writes=[("hTd", b)], dsem="d_hst%d" % buf)
                load(0)
                for i in range(len(tiles)):
                    do_tile(i)
                S.barrier()
                S.emit()

        def m4_phase():
            with ExitStack() as ph:
                wg = sb("wg", [128, KD, 2048], BF16, st=ph)
                wua = sb("wua", [128, 4, D], BF16, st=ph)
                wub = sb("wub", [128, 4, D], BF16, st=ph)
                wmo = sb("wmo", [128, KD, D], BF16, st=ph)
                xT = [sb("xT%d" % i, [128, KD, TT], st=ph) for i in range(2)]
                hT = [sb("hT%d" % i, [128, KD, TT], BF16, st=ph) for i in range(2)]
                ya = [sb("ya%d" % i, [128, 4, TT], BF16, st=ph) for i in range(2)]
                yb = [sb("yb%d" % i, [128, 4, TT], BF16, st=ph) for i in range(2)]
                tmp = sb("tmp", [128, KD, TT], st=ph)
                mT = sb("mT", [128, KD, TT], BF16, st=ph)
                sq = sb("sq", [128, KD, TT], BF16, st=ph)
                rstd = sb("rstd", [128, TT], st=ph)
                sga = [sb("sga%d" % i, [128, TT], st=ph) for i in range(2)]
                sgb = [sb("sgb%d" % i, [128, TT], st=ph) for i in range(2)]
                t1 = [sb("t1%d" % i, [128, TT], st=ph) for i in range(2)]
                mi = mix_in_d.rearrange("(k p) n -> p k n", p=128)
                for kc in range(KD):
                    S.op("pool", lambda e, kc=kc: e.dma_start(out=wg[:, kc, :], in_=mi[:, kc, 3456:5504]),
                         writes=(["W"] if kc == KD - 1 else []), dsem="d_w")
                S.op("pool", lambda e: e.dma_start(out=wua[:], in_=upa_d.rearrange("(k p) n -> p k n", p=128)), dsem="d_w2")
                S.op("pool", lambda e: e.dma_start(out=wub[:], in_=upb_d.rearrange("(k p) n -> p k n", p=128)), dsem="d_w2")
                S.op("pool", lambda e: e.dma_start(out=wmo[:], in_=mo_d.rearrange("(k p) n -> p k n", p=128)),
                     writes=["W2"], dsem="d_w2")
                tiles = tile_list(False)

                def load(i):
                    b, r, e0, n = tiles[i]
                    s0 = e0 - CTX
                    buf = i % 2
                    S.op("sp", lambda e: e.dma_start(
                        out=xT[buf][:, :, 0:n], in_=x1T_d[b, :, :, e0:e0 + n].rearrange("k p t -> p k t")),
                        reads=[("x1T", b, e0)], writes=[("xT", buf)], dsem="d_xT%d" % buf)
                    S.op("sp", lambda e: e.dma_start(
                        out=hT[buf][:, :, 0:n], in_=hT_d[b, :, :, e0:e0 + n].rearrange("k p t -> p k t")),
                        reads=[("hTd", b)], writes=[("hT", buf)], dsem="d_hT%d" % buf)
                    S.op("sp", lambda e: e.dma_start(
                        out=ya[buf][:, :, 0:n], in_=yaT_d[b, :, :, s0:s0 + n].rearrange("k p t -> p k t")),
                        reads=[("yaT", b)], writes=[("ya", buf)], dsem="d_ya%d" % buf)
                    S.op("sp", lambda e: e.dma_start(
                        out=yb[buf][:, :, 0:n], in_=ybT_d[b, :, :, s0:s0 + n].rearrange("k p t -> p k t")),
                        reads=[("ybT", b)], writes=[("yb", buf)], dsem="d_yb%d" % buf)

                def do_tile(i):
                    b, r, e0, n = tiles[i]
                    s0 = e0 - CTX
                    buf = i % 2
                    if i + 1 < len(tiles):
                        load(i + 1)
                    for ncx in range(KD):
                        pa, pbb, pga, pgb = bank(), bank(), bank(), bank()
                        for k4 in range(4):
                            S.op("pe", lambda e, k4=k4: e.matmul(psb[pa][:, 0:n], wua[:, k4, ncx * 128:(ncx + 1) * 128],
                                                                 ya[buf][:, k4, 0:n], start=(k4 == 0), stop=(k4 == 3)),
                                 reads=[("ya", buf), "W2"], writes=[("ps", pa)], inc=(k4 == 3))
                        for k4 in range(4):
                            S.op("pe", lambda e, k4=k4: e.matmul(psb[pbb][:, 0:n], wub[:, k4, ncx * 128:(ncx + 1) * 128],
                                                                 yb[buf][:, k4, 0:n], start=(k4 == 0), stop=(k4 == 3)),
                                 reads=[("yb", buf), "W2"], writes=[("ps", pbb)], inc=(k4 == 3))
                        for gi, pg in ((0, pga), (1, pgb)):
                            for kc in range(KD):
                                S.op("pe", lambda e, kc=kc, gi=gi, pg=pg: e.matmul(
                                    psb[pg][:, 0:n], wg[:, kc, gi * 1024 + ncx * 128:gi * 1024 + (ncx + 1) * 128],
                                    hT[buf][:, kc, 0:n], start=(kc == 0), stop=(kc == KD - 1)),
                                    reads=[("hT", buf), "W"], writes=[("ps", pg)], inc=(kc == KD - 1))
                        j = ncx % 2
                        S.op("act", lambda e: e.activation(out=sga[j][:, 0:n], in_=psb[pga][:, 0:n], func=AF.Sigmoid),
                             reads=[("ps", pga)], writes=[("sga", j)])
                        S.op("act", lambda e: e.activation(out=sgb[j][:, 0:n], in_=psb[pgb][:, 0:n], func=AF.Sigmoid),
                             reads=[("ps", pgb)], writes=[("sgb", j)])
                        S.op("dve", lambda e: e.tensor_tensor(out=t1[j][:, 0:n], in0=psb[pa][:, 0:n], in1=sga[j][:, 0:n], op=ALU.mult),
                             reads=[("ps", pa), ("sga", j)], writes=[("t1", j)])
                        S.op("dve", lambda e: e.tensor_tensor(out=sgb[j][:, 0:n], in0=psb[pbb][:, 0:n], in1=sgb[j][:, 0:n], op=ALU.mult),
                             reads=[("ps", pbb), ("sgb", j)], writes=[("sgb", j)])
                        S.op("pool", lambda e: e.tensor_tensor(out=mT[:, ncx, 0:n], in0=t1[j][:, 0:n], in1=sgb[j][:, 0:n], op=ALU.add),
                             reads=[("t1", j), ("sgb", j)], writes=["mT"])
                    for dc in range(KD):
                        pb = bank()
                        for kc in range(KD):
                            S.op("pe", lambda e, kc=kc: e.matmul(psb[pb][:, 0:n], wmo[:, kc, dc * 128:(dc + 1) * 128],
                                                                 mT[:, kc, 0:n], start=(kc == 0), stop=(kc == KD - 1)),
                                 reads=["mT", "W2"], writes=[("ps", pb)], inc=(kc == KD - 1))
                        S.op("act", lambda e: e.activation(out=tmp[:, dc, 0:n], in_=psb[pb][:, 0:n], func=AF.Copy),
                             reads=[("ps", pb)], writes=["tmp"])
                    S.op("act", lambda e: e.activation(out=sq[:, :, 0:n], in_=tmp[:, :, 0:n], func=AF.Square),
                         reads=["tmp"], writes=["sq"])
                    pb = bank()
                    for kc in range(KD):
                        S.op("pe", lambda e, kc=kc: e.matmul(psb[pb][:, 0:n], onesD[:], sq[:, kc, 0:n],
                                                             start=(kc == 0), stop=(kc == KD - 1)),
                             reads=["sq", "onesD"], writes=[("ps", pb)], inc=(kc == KD - 1))
                    S.op("act", lambda e: e.activation(out=rstd[:, 0:n], in_=psb[pb][:, 0:n], func=AF.Sqrt,
                                                       bias=epsc[:, 0:1], scale=1.0),
                         reads=[("ps", pb), "epsc"], writes=["rstd"])
                    S.op("dve", lambda e: e.reciprocal(rstd[:, 0:n], rstd[:, 0:n]), reads=["rstd"], writes=["rstd"])
                    S.op("dve", lambda e: e.tensor_tensor(
                        out=tmp[:, :, 0:n], in0=tmp[:, :, 0:n],
                        in1=rstd[:, 0:n].unsqueeze(1).to_broadcast([128, KD, n]), op=ALU.mult),
                        reads=["tmp", "rstd"], writes=["tmp"])
                    for dc in range(KD):
                        S.op("dve", lambda e, dc=dc: e.scalar_tensor_tensor(
                            out=xT[buf][:, dc, 0:n], in0=tmp[:, dc, 0:n], scalar=CG[:, 1, r, dc:dc + 1],
                            in1=xT[buf][:, dc, 0:n], op0=ALU.mult, op1=ALU.add),
                            reads=["tmp", "CG", ("xT", buf)], writes=[("xT", buf)])
                    S.op("sp", lambda e: e.dma_start(
                        out=x2T_d[b, :, :, s0:s0 + n].rearrange("k p t -> p k t"), in_=xT[buf][:, :, 0:n]),
                        reads=[("xT", buf)], writes=[("x2T", b, s0)], dsem="d_st%d" % buf)
                load(0)
                for i in range(len(tiles)):
                    do_tile(i)
                S.barrier()
                S.emit()


        def m3_phase():
            NKB = E // 128
            NQT = T // 512 if T >= 512 else 1
            QT = min(512, T)
            with ExitStack() as ph:
                wqk = sb("wqk", [128, KD, 1024], BF16, st=ph)
                wqs = sb("wqs", [128, KD, 1024], BF16, st=ph)
                wv = sb("wv", [128, KD, 512], BF16, st=ph)
                hA = sb("hA", [128, KD, E], BF16, st=ph)
                cosT = sb("cosT", [128, T], st=ph)
                sinT = sb("sinT", [128, T], st=ph)
                Vaug = sb("Vaug", [128, NKB, 4, 129], BF16, st=ph)
                QTt = sb("QTt", [128, T], BF16, st=ph)
                QTc = [sb("QTc%d" % i, [128, T], BF16, st=ph) for i in range(2)]
                KTt = sb("KTt", [128, E], BF16, st=ph)
                sel = sb("sel", [128, 2, 128], BF16, st=ph)
                r1 = [sb("r1%d" % i, [128, QT], st=ph) for i in range(2)]
                r2 = [sb("r2%d" % i, [128, QT], st=ph) for i in range(2)]
                qsq = [sb("qsq%d" % i, [128, QT], BF16, st=ph) for i in range(2)]
                mx = sb("mx", [128, 2, 2, 8], st=ph)
                mxr = sb("mxr", [128, 2, 2], st=ph)
                negc = sb("negc", [128, 2], st=ph)
                PT = [sb("PT%d" % i, [128, QT], BF16, st=ph) for i in range(3)]
                on1 = sb("on1", [128, QT], st=ph)
                oo = sb("oo", [128, QT], st=ph)
                osq = sb("osq", [128, QT], st=ph)
                Pacc = [sb("Pacc%d" % i, [128, QT], st=ph) for i in range(2)]
                rcp = [sb("rcp%d" % i, [128, QT], st=ph) for i in range(2)]
                subgp = sb("subgp", [128, 1], st=ph)
                ybT = [sb("ybT%d" % i, [128, QT], BF16, st=ph) for i in range(2)]
                NQS = QT // 128
                mi = mix_in_d.rearrange("(k p) n -> p k n", p=128)
                qs_ = qk_sw_d.rearrange("(k p) n -> p k n", p=128)
                for kc in range(KD):
                    S.op("pool", lambda e, kc=kc: e.dma_start(out=wqk[:, kc, :], in_=mi[:, kc, 1920:2944]), dsem="d_w")
                    S.op("pool", lambda e, kc=kc: e.dma_start(out=wqs[:, kc, :], in_=qs_[:, kc, :]), dsem="d_w")
                    S.op("pool", lambda e, kc=kc: e.dma_start(out=wv[:, kc, :], in_=mi[:, kc, 2944:3456]),
                         writes=(["W"] if kc == KD - 1 else []), dsem="d_w")
                S.op("sp", lambda e: e.dma_start(out=cosT[:], in_=cos_d), writes=["cosT"], dsem="d_cos")
                S.op("sp", lambda e: e.dma_start(out=sinT[:], in_=sin_d), writes=["sinT"], dsem="d_sin")
                for c in range(2):
                    S.op("dve", lambda e, c=c: e.tensor_copy(sel[:, c, :], cst[:, CS_IND + c:CS_IND + c + 1].to_broadcast([128, 128])),
                         reads=["cst"], writes=["sel"])
                S.op("dve", lambda e: e.tensor_scalar(subgp[:], pvec[:, PV_SUBG:PV_SUBG + 1], 0.8, None, ALU.mult),
                     reads=["pvec"], writes=["subgp"])
                S.op("dve", lambda e: e.memset(Vaug[:, :, :, 128:129], 1.0), writes=["Vaug1"])
                S.op("pool", lambda e: e.memset(QTc[0][64:128, :], 0.0), writes=["QTc"])
                S.op("pool", lambda e: e.memset(QTc[1][0:64, :], 0.0), writes=["QTc"])
                S.op("dve", lambda e: e.tensor_tensor(out=osq[:, 0:64], in0=brow[:, BR_LAM:BR_LAM + 64], in1=brow[:, BR_LAM + 64:BR_LAM + 128], op=ALU.mult),
                     reads=["brow"], writes=["osq"])
                S.op("dve", lambda e: e.tensor_tensor(out=osq[:, 64:128], in0=brow[:, BR_LAM + 128:BR_LAM + 192], in1=brow[:, BR_LAM + 192:BR_LAM + 256], op=ALU.mult),
                     reads=["brow"], writes=["osq"])
                S.op("dve", lambda e: e.tensor_reduce(out=lam[:, 2:4], in_=osq[:, 0:128].rearrange("p (a x) -> p a x", a=2), axis=AX.X, op=ALU.add),
                     reads=["osq"], writes=["lam"])
                S.op("act", lambda e: e.activation(out=lam[:, 2:4], in_=lam[:, 2:4], func=AF.Exp), reads=["lam"], writes=["lam"])
                S.op("dve", lambda e: e.tensor_tensor(out=lam[:, 0:1], in0=lam[:, 2:3], in1=lam[:, 3:4], op=ALU.subtract),
                     reads=["lam"], writes=["lam"])
                S.op("dve", lambda e: e.tensor_scalar(lam[:, 0:1], lam[:, 0:1], 0.2, None, ALU.add), reads=["lam"], writes=["lam"])
                S.op("dve", lambda e: e.tensor_scalar(lam[:, 1:2], lam[:, 0:1], -1.0, None, ALU.mult), reads=["lam"], writes=["lam"])

                def norm_tile(qk, j, n, slot):
                    for c in range(2):
                        pb = bank()
                        S.op("pe", lambda e, pb=pb, c=c: e.matmul(psb[pb][:, 0:n], sel[:, c, :], qsq[j][:, 0:n], start=True, stop=True),
                             reads=[("qsq", j), "sel"], writes=[("ps", pb)])
                        S.op("dve", lambda e, pb=pb, c=c: e.tensor_reduce(out=mx[:, qk, c, slot:slot + 1], in_=psb[pb][:, 0:n], axis=AX.X, op=ALU.max),
                             reads=[("ps", pb)], writes=["mx"])

                for b in range(NB):
                    S.op("sp", lambda e, b=b: e.dma_start(out=hA[:], in_=hT_d[b].rearrange("k p t -> p k t")),
                         reads=[("hTd", b)], writes=["hA"], dsem="d_hA")
                    pst["set"] = list(range(8))
                    for kb in range(NKB):
                        pb = bank()
                        for kc in range(KD):
                            S.op("pe", lambda e, kb=kb, kc=kc, pb=pb: e.matmul(
                                psb[pb][:, 0:512], hA[:, kc, kb * 128:(kb + 1) * 128], wv[:, kc, :],
                                start=(kc == 0), stop=(kc == KD - 1)),
                                reads=["hA", "W"], writes=[("ps", pb)], inc=(kc == KD - 1))
                        if kb % 2 == 0:
                            S.op("act", lambda e, kb=kb, pb=pb: e.activation(
                                out=Vaug[:, kb, :, 0:128], in_=psb[pb][:, 0:512].rearrange("p (h e) -> p h e", h=4), func=AF.Copy),
                                reads=[("ps", pb)], writes=["Vaug"])
                        else:
                            S.op("dve", lambda e, kb=kb, pb=pb: e.tensor_copy(
                                Vaug[:, kb, :, 0:128], psb[pb][:, 0:512].rearrange("p (h e) -> p h e", h=4)),
                                reads=[("ps", pb)], writes=["Vaug"])
                    for h in range(4):
                        pst["set"] = list(range(8))
                        for e0 in range(0, CTX, 512):
                            n = min(512, CTX - e0)
                            pb = bank()
                            for kc in range(KD):
                                S.op("pe", lambda e, kc=kc, pb=pb, e0=e0, n=n: e.matmul(
                                    psb[pb][:, 0:n], wqk[:, kc, 512 + h * 128:512 + (h + 1) * 128], hA[:, kc, e0:e0 + n],
                                    start=(kc == 0), stop=(kc == KD - 1)),
                                    reads=["hA", "W"], writes=[("ps", pb)], inc=(kc == KD - 1))
                            S.op("act", lambda e, pb=pb, e0=e0, n=n: e.activation(out=KTt[:, e0:e0 + n], in_=psb[pb][:, 0:n], func=AF.Copy),
                                 reads=[("ps", pb)], writes=["KTt"])
                            S.op("pool", lambda e, e0=e0, n=n, i=(e0 // 512) % 2: e.tensor_tensor(
                                out=qsq[i][:, 0:n], in0=KTt[:, e0:e0 + n], in1=KTt[:, e0:e0 + n], op=ALU.mult),
                                reads=["KTt"], writes=[("qsq", (e0 // 512) % 2)])
                            norm_tile(1, (e0 // 512) % 2, n, e0 // 512)
                        for ti in range(NQT):
                            t0 = ti * QT
                            for qk in range(2):
                                pm, psw = bank(), bank()
                                for kc in range(KD):
                                    S.op("pe", lambda e, kc=kc, pm=pm, qk=qk, t0=t0: e.matmul(
                                        psb[pm][:, 0:QT], wqk[:, kc, qk * 512 + h * 128:qk * 512 + (h + 1) * 128],
                                        hA[:, kc, CTX + t0:CTX + t0 + QT], start=(kc == 0), stop=(kc == KD - 1)),
                                        reads=["hA", "W"], writes=[("ps", pm)], inc=(kc == KD - 1))
                                for kc in range(KD):
                                    S.op("pe", lambda e, kc=kc, psw=psw, qk=qk, t0=t0: e.matmul(
                                        psb[psw][:, 0:QT], wqs[:, kc, qk * 512 + h * 128:qk * 512 + (h + 1) * 128],
                                        hA[:, kc, CTX + t0:CTX + t0 + QT], start=(kc == 0), stop=(kc == KD - 1)),
                                        reads=["hA", "W"], writes=[("ps", psw)], inc=(kc == KD - 1))
                                j = (ti * 2 + qk) % 2
                                S.op("dve", lambda e, pm=pm, j=j, t0=t0: e.tensor_tensor(
                                    out=r1[j][:], in0=psb[pm][:, 0:QT], in1=cosT[:, t0:t0 + QT], op=ALU.mult),
                                    reads=[("ps", pm), "cosT"], writes=[("r1", j)])
                                S.op("dve", lambda e, psw=psw, j=j, t0=t0: e.tensor_tensor(
                                    out=r2[j][:], in0=psb[psw][:, 0:QT], in1=sinT[:, t0:t0 + QT], op=ALU.mult),
                                    reads=[("ps", psw), "sinT"], writes=[("r2", j)])
                                dst = QTt[:, t0:t0 + QT] if qk == 0 else KTt[:, CTX + t0:CTX + t0 + QT]
                                dres = "QTt" if qk == 0 else "KTt"
                                S.op("pool", lambda e, j=j, dst=dst: e.tensor_tensor(out=dst, in0=r1[j][:], in1=r2[j][:], op=ALU.add),
                                     reads=[("r1", j), ("r2", j)], writes=[dres])
                                S.op("pool", lambda e, j=j, dst=dst: e.tensor_tensor(out=qsq[j][:], in0=dst, in1=dst, op=ALU.mult),
                                     reads=[dres], writes=[("qsq", j)])
                                if qk == 0:
                                    S.op("pool", lambda e: e.tensor_copy(QTc[0][0:64, t0:t0 + QT], QTt[0:64, t0:t0 + QT]), reads=["QTt"], writes=["QTc"])
                                    S.op("pool", lambda e: e.tensor_copy(QTc[1][64:128, t0:t0 + QT], QTt[64:128, t0:t0 + QT]), reads=["QTt"], writes=["QTc"])
                                norm_tile(qk, j, QT, ti + (CTX + 511) // 512 if qk == 1 else ti)
                        nkt = NQT + (CTX + 511) // 512
                        S.op("dve", lambda e: e.tensor_reduce(out=mxr[:, 0, :], in_=mx[:, 0, :, 0:NQT], axis=AX.X, op=ALU.max),
                             reads=["mx"], writes=["mxr"])
                        S.op("dve", lambda e, nkt=nkt: e.tensor_reduce(out=mxr[:, 1, :], in_=mx[:, 1, :, 0:nkt], axis=AX.X, op=ALU.max),
                             reads=["mx"], writes=["mxr"])
                        S.op("dve", lambda e: e.tensor_tensor(out=negc[:], in0=mxr[:, 0, :], in1=mxr[:, 1, :], op=ALU.mult),
                             reads=["mxr"], writes=["negc"])
                        S.op("act", lambda e: e.activation(out=negc[:], in_=negc[:], func=AF.Sqrt, scale=1.0 / 64.0),
                             reads=["negc"], writes=["negc"])
                        S.op("dve", lambda e: e.tensor_scalar(negc[:], negc[:], -1.0, None, ALU.mult), reads=["negc"], writes=["negc"])
                        pst["set"] = [2, 3, 4, 5, 6, 7]
                        for ti in range(NQT):
                            t0 = ti * QT
                            for c in range(2):
                                accb = c
                                cs = slice(c * 64, (c + 1) * 64)

                                def score(kb, c=c, cs=cs, t0=t0):
                                    pb = bank()
                                    S.op("pe", lambda e: e.matmul(psb[pb][:, 0:QT], KTt[:, kb * 128:(kb + 1) * 128], QTc[c][:, t0:t0 + QT],
                                                                  start=True, stop=True),
                                         reads=["KTt", "QTc"], writes=[("ps", pb)])
                                    return pb

                                pbs = score(0)
                                for kb in range(NKB):
                                    pbn = score(kb + 1) if kb + 1 < NKB else None
                                    pi = kb % 3
                                    S.op("act", lambda e, pbs=pbs, pi=pi, c=c: e.activation(
                                        out=PT[pi][:], in_=psb[pbs][:, 0:QT], func=AF.Exp, bias=negc[:, c:c + 1], scale=0.125),
                                        reads=[("ps", pbs), "negc"], writes=[("PT", pi)])
                                    S.op("pe", lambda e, pi=pi, kb=kb: e.matmul(
                                        psb[accb][:, 0:QT], Vaug[:, kb, h, 0:128], PT[pi][:], start=(kb == 0), stop=(kb == NKB - 1)),
                                        reads=[("PT", pi), "Vaug"], writes=[("ps", accb)])
                                    if kb == 0:
                                        S.op("dve", lambda e, pi=pi: e.tensor_copy(Pacc[c][:], PT[pi][:]), reads=[("PT", pi)], writes=[("Pacc", c)])
                                    else:
                                        S.op("dve", lambda e, pi=pi: e.tensor_tensor(out=Pacc[c][:], in0=Pacc[c][:], in1=PT[pi][:], op=ALU.add),
                                             reads=[("PT", pi), ("Pacc", c)], writes=[("Pacc", c)])
                                    pbs = pbn
                                pr = bank()
                                S.op("pe", lambda e: e.matmul(psb[pr][:, 0:QT], ones_f[:], Pacc[c][:], start=True, stop=True),
                                     reads=[("Pacc", c), "ones_f"], writes=[("ps", pr)])
                                S.op("act", lambda e: e.activation(out=rcp[c][:], in_=psb[pr][:, 0:QT], func=AF.Ln), reads=[("ps", pr)], writes=[("rcp", c)])
                                S.op("act", lambda e: e.activation(out=rcp[c][:], in_=rcp[c][:], func=AF.Exp, scale=-1.0), reads=[("rcp", c)], writes=[("rcp", c)])
                                if c == 0:
                                    S.op("dve", lambda e: e.tensor_tensor(out=on1[:], in0=psb[accb][:, 0:QT], in1=rcp[0][:], op=ALU.mult),
                                         reads=[("ps", accb), ("rcp", 0)], writes=["on1"])
                                else:
                                    S.op("dve", lambda e: e.tensor_scalar(rcp[1][:], rcp[1][:], lam[:, 1:2], None, ALU.mult),
                                         reads=[("rcp", 1), "lam"], writes=[("rcp", 1)])
                                    S.op("dve", lambda e: e.tensor_tensor(out=oo[:], in0=psb[accb][:, 0:QT], in1=rcp[1][:], op=ALU.mult),
                                         reads=[("ps", accb), ("rcp", 1)], writes=["oo"])
                                    S.op("pool", lambda e: e.tensor_tensor(out=oo[:], in0=oo[:], in1=on1[:], op=ALU.add),
                                         reads=["oo", "on1"], writes=["oo"])
                            S.op("pool", lambda e: e.tensor_tensor(out=osq[:], in0=oo[:], in1=oo[:], op=ALU.mult), reads=["oo"], writes=["osq"])
                            pm_ = bank()
                            S.op("pe", lambda e: e.matmul(psb[pm_][:, 0:QT], ones_f[:], osq[:], start=True, stop=True),
                                 reads=["osq", "ones_f"], writes=[("ps", pm_)])
                            S.op("act", lambda e: e.activation(out=osq[:], in_=psb[pm_][:, 0:QT], func=AF.Ln, bias=epsc[:, 0:1], scale=1.0 / 128.0),
                                 reads=[("ps", pm_), "epsc"], writes=["osq"])
                            S.op("act", lambda e: e.activation(out=osq[:], in_=osq[:], func=AF.Exp, scale=-0.5), reads=["osq"], writes=["osq"])
                            yi = ti % 2
                            S.op("dve", lambda e: e.scalar_tensor_tensor(out=ybT[yi][:], in0=oo[:], scalar=subgp[:, 0:1], in1=osq[:],
                                                                         op0=ALU.mult, op1=ALU.mult),
                                 reads=["oo", "osq", "subgp"], writes=[("ybT", yi)])
                            S.op("sp", lambda e, yi=yi, t0=t0, b=b: e.dma_start(out=ybT_d[b, h, :, t0:t0 + QT], in_=ybT[yi][:]),
                                 reads=[("ybT", yi)], writes=[("ybT", b)], dsem="d_yb%d" % yi)
                pst["set"] = list(range(8))
                S.barrier()
                S.emit()


        def m2_phase():
            NCH = E // 128
            NCC = CTX // 128
            NXC = T // 128
            NSL = 2 * NCH
            G32 = cfg.get('m2f32', 'BC')
            gdt = lambda g: F32 if g in G32 else BF16
            gid = lambda g: ident32 if g in G32 else ident_bf[:]
            gbc = lambda g, ap: (ap if g in G32 else ap.bitcast(BF16))
            TPB = 4 if 'R' in G32 else 8
            with ExitStack() as ph:
                wsub = [sb("wsub%d" % i, [128, KD, 384], BF16, st=ph) for i in range(2)]
                wcnt = [0]
                w2t = sb("w2t", [128, 512], gdt('E'), st=ph)
                a2t = sb("a2t", [128, 512], gdt('E'), st=ph)
                g2t = sb("g2t", [128, 512], gdt('R'), st=ph)
                bones = sb("bones", [128, 128], gdt('A'), st=ph)
                ind2 = sb("ind2", [128, 2], gdt('R'), st=ph)
                omka = sb("omka", [128, 4], st=ph)
                hTt = [sb("hTt%d" % i, [128, KD, TT + 2], BF16, st=ph) for i in range(2)]
                lw = sb("lw", [128, E], gdt('E'), st=ph)
                la = sb("la", [128, E], gdt('E'), st=ph)
                lg = sb("lg", [128, E], gdt('R'), st=ph)
                MTs = sb("MTs", [128, NSL, 128], gdt('D'), st=ph)
                GyTs = sb("GyTs", [128, NSL, 128], gdt('D'), st=ph)
                Sadds = sb("Sadds", [128, NSL, 64], st=ph)
                WCs = sb("WCs", [128, NSL], st=ph)
                Yacc = sb("Yacc", [128, NXC, 128], st=ph)
                Vtm = sb("Vtm", [128, NCH, 128], gdt('C'), st=ph)
                BS = sb("BS", [128, NXC, 2], st=ph)
                S32 = sb("S32", [128, 2, 64], st=ph)
                Sbf = sb("Sbf", [128, 2, 64], gdt('D'), st=ph)
                Sbd = sb("Sbd", [128, 2, 128], gdt('D'), st=ph)
                ub = [sb("ub%d" % i, [128, TT + 2], st=ph) for i in range(2)]
                NF = 22
                ft = [sb("ft%d" % i, [128, TT], st=ph) for i in range(NF)]
                (rT, kT, vT, kk, t0_, t1_, Lin, Lex, E1, E2, E3, E4) = ft[0:12]
                sgw = ft[12:14]
                a_d = ft[14:16]
                kdir = ft[16:18]
                b_d = ft[18:20]
                Pp = ft[20]
                WCt = sb("WCt", [128, 2], st=ph)
                QRl = [[sb("QR%d_%d" % (t_, i), [128, TT // 128, 256], gdt('A'), st=ph) for i in range(2)] for t_ in range(2)]
                Khl = [[sb("Kh%d_%d" % (t_, i), [128, TT], gdt('A'), st=ph) for i in range(2)] for t_ in range(2)]
                Bhl = [[sb("Bh%d_%d" % (t_, i), [128, TT], gdt('A'), st=ph) for i in range(2)] for t_ in range(2)]
                Kdl = [[sb("Kd%d_%d" % (t_, i), [128, TT], gdt('C'), st=ph) for i in range(2)] for t_ in range(2)]
                Bdl = [[sb("Bd%d_%d" % (t_, i), [128, TT], gdt('C'), st=ph) for i in range(2)] for t_ in range(2)]
                vb = sb("vb", [128, TT], gdt('C'), st=ph)
                prb = sb("prb", [128, TT], gdt('R'), st=ph)
                ksq = sb("ksq", [128, TT], gdt('A'), st=ph)
                NCHN = cfg.get("nchain", 4)
                SAl = [sb("SA%d" % i, [128, 2, 256], gdt('B'), st=ph) for i in range(NCHN)]
                SBl = [sb("SB%d" % i, [128, 2, 256], gdt('B'), st=ph) for i in range(NCHN)]
                XTal = [None] * NCHN
                Xal = [[sb("Xa%d_%d" % (i, k), [128, 2, 128], gdt('B'), st=ph) for k in range(2)] for i in range(NCHN)]
                Zal = [[sb("Za%d_%d" % (i, k), [128, 2, 256], gdt('B'), st=ph) for k in range(2)] for i in range(NCHN)]
                KdBdl = [sb("KdBd%d" % i, [128, 2, 2, 128], gdt('C'), st=ph) for i in range(NCHN)]
                Gpadl = [sb("Gpad%d" % i, [128, 2, 128], gdt('B'), st=ph) for i in range(NCHN)]
                yc = sb("yc", [128, (NXC + 1) // 2, 128], st=ph)
                ysq = sb("ysq", [128, (NXC + 1) // 2, 128], st=ph)
                st1 = sb("st1", [128, NXC * 2], st=ph)
                st2 = sb("st2", [128, NXC * 2], st=ph)
                yab = sb("yab", [128, NXC, 128], gdt('R'), st=ph)
                yaT = sb("yaT", [128, T], BF16, st=ph)
                mi = mix_in_d.rearrange("(k p) n -> p k n", p=128)
                def load_w(hp_):
                    k = wcnt[0] % 2
                    wcnt[0] += 1
                    if hp_ is None:
                        S.op("pool", lambda e: e.dma_start(out=wsub[k][:], in_=mi[:, :, 1536:1920]), writes=[("wsub", k, 0), ("wsub", k, 1), ("wsub", k, 2)], dsem="d_wsub%d_0" % k)
                    else:
                        for j in range(3):
                            c0_ = j * 512 + hp_ * 128
                            S.op("pool", lambda e: e.dma_start(out=wsub[k][:, :, j * 128:(j + 1) * 128], in_=mi[:, :, c0_:c0_ + 128]),
                                 writes=[("wsub", k, j)], dsem="d_wsub%d_%d" % (k, j))
                    return k

                S.op("pool", lambda e: e.dma_start(out=w2t[:], in_=w2_d.rearrange("d r c -> (d r) c")), dsem="d_w")
                S.op("pool", lambda e: e.dma_start(out=a2t[:], in_=a2_d.rearrange("d r c -> (d r) c")), dsem="d_w")
                S.op("pool", lambda e: e.dma_start(out=g2t[:], in_=g2_d), writes=["W"], dsem="d_w")
                S.op("dve", lambda e: e.tensor_copy(bones[:], cst[:, CS_BONES:CS_BONES + 128]), reads=["cst"], writes=["bones"])
                S.op("dve", lambda e: e.tensor_copy(ind2[:], cst[:, CS_IND:CS_IND + 2]), reads=["cst"], writes=["ind2"])
                for ci in range(NCHN):
                    S.op("pool", lambda e: e.memset(KdBdl[ci][:], 0.0), writes=[("KdBd", ci)])
                    S.op("pool", lambda e: e.memset(Gpadl[ci][:], 0.0), writes=[("Gpad", ci)])
                S.op("dve", lambda e: e.tensor_scalar(omka[:], pvec[:, PV_KA:PV_KA + 4], -1.0, 1.0, ALU.mult, ALU.add),
                     reads=["pvec"], writes=["omka"])
                fsm = cst[:, CS_FS:CS_FS + 256]
                bsm = cst[:, CS_BS:CS_BS + 256]
                hcnt = [0]

                def load_h(b, e0, n):
                    r0, r1 = (0, CTX) if e0 < CTX else (CTX, E)
                    lo, hi = max(e0 - 1, r0), min(e0 + n + 1, r1)
                    off = lo - (e0 - 1)
                    buf = hcnt[0] % 2
                    hcnt[0] += 1
                    S.op("sp", lambda e: e.dma_start(out=hTt[buf][:, :, off:off + hi - lo],
                                                     in_=hT_d[b, :, :, lo:hi].rearrange("k p t -> p k t")),
                         reads=[("hTd", b)], writes=[("hTt", buf)], dsem="d_hTt%d" % buf)
                    return buf, lo, hi, off

                ucnt = [0]

                def proj_conv(hb, lo, hi, off, e0, n, chunk, dst, wk, wj):
                    N = hi - lo
                    pb = bank()
                    u = ub[ucnt[0] % 2]
                    ur = ("ub", ucnt[0] % 2)
                    ucnt[0] += 1
                    for kc in range(KD):
                        S.op("pe", lambda e, kc=kc: e.matmul(psb[pb][:, 0:N], wsub[wk][:, kc, wj * 128:(wj + 1) * 128],
                                                             hTt[hb][:, kc, off:off + N], start=(kc == 0), stop=(kc == KD - 1)),
                             reads=[("hTt", hb), ("wsub", wk, wj)], writes=[("ps", pb)], inc=(kc == KD - 1))
                    if off == 1:
                        S.op("pool", lambda e: e.memset(u[:, 0:1], 0.0), writes=[ur])
                    if hi < e0 + n + 1:
                        S.op("pool", lambda e: e.memset(u[:, n + 1:n + 2], 0.0), writes=[ur])
                    S.op("act", lambda e: e.activation(out=u[:, off:off + N], in_=psb[pb][:, 0:N], func=AF.Copy),
                         reads=[("ps", pb)], writes=[ur])
                    cw = lambda tap: pvec[:, PV_CONV + tap * 15 + chunk:PV_CONV + tap * 15 + chunk + 1]
                    S.op("act", lambda e: e.activation(out=dst, in_=u[:, 0:n], func=AF.Identity, scale=cw(0)),
                         reads=[ur, "pvec"], writes=[dst.tensor.name])
                    S.op("dve", lambda e: e.scalar_tensor_tensor(out=dst, in0=u[:, 1:n + 1], scalar=cw(1), in1=dst,
                                                                  op0=ALU.mult, op1=ALU.add),
                         reads=[ur, "pvec", dst.tensor.name], writes=[dst.tensor.name])
                    S.op("dve", lambda e: e.scalar_tensor_tensor(out=dst, in0=u[:, 2:n + 2], scalar=cw(2), in1=dst,
                                                                  op0=ALU.mult, op1=ALU.add),
                         reads=[ur, "pvec", dst.tensor.name], writes=[dst.tensor.name])

                def rn(t):
                    return t.tensor.name if hasattr(t, "tensor") else t.name

                def chunk_pre(b, hp, cidx, cl, d, ci, tp):
                    slot = d * NCH + cidx
                    QR, Kh, Bh, Kd, Bd = QRl[tp], Khl[tp], Bhl[tp], Kdl[tp], Bdl[tp]
                    SA, SB, XTa, Xa, Za, KdBd, Gpad = SAl[ci], SBl[ci], XTal[ci], Xal[ci], Zal[ci], KdBdl[ci], Gpadl[ci]
                    cs = slice(cl * 128, (cl + 1) * 128)
                    m2 = (fsm if d == 0 else bsm).unsqueeze(1).to_broadcast([128, 2, 256])
                    mT = (cst[:, CS_BS:CS_BS + 128] if d == 0 else cst[:, CS_FS:CS_FS + 128]).unsqueeze(1).to_broadcast([128, 2, 128])
                    pq = bank()
                    pqv = gbc('A', psb[pq][:])
                    S.op("pe", lambda e: e.transpose(pqv[:, 0:128], QR[d][:, cl, 0:128], gid('A')),
                         reads=[("QR", tp, d), "ident_bf", "cst"], writes=[("ps", pq)])
                    pt = bank()
                    ptv = gbc('C', psb[pt][:])
                    S.op("pe", lambda e: e.transpose(ptv[:, 128:256], Kd[d][:, cs], gid('C')),
                         reads=[("Kd", tp, d), "ident_bf", "cst"], writes=[("ps", pt)], inc=False)
                    S.op("pe", lambda e: e.transpose(ptv[:, 256:384], Bd[d][:, cs], gid('C')),
                         reads=[("Bd", tp, d), "ident_bf", "cst"], writes=[("ps", pt)])
                    S.op("act", lambda e: e.activation(out=Za[0][:, :, 64:128], in_=pqv[:, 0:128].rearrange("p (h k) -> p h k", h=2), func=AF.Copy),
                         reads=[("ps", pq)], writes=[("Z", ci, 0)])
                    for h in range(2):
                        S.op("act", lambda e, h=h: e.activation(out=KdBd[:, :, h, h * 64:(h + 1) * 64],
                                                                in_=ptv[:, 128:384].rearrange("p (a x) -> p a x", a=2)[:, :, h * 64:(h + 1) * 64], func=AF.Copy),
                             reads=[("ps", pt)], writes=[("KdBd", ci)])
                    if cfg.get('m2cut', 99) <= 3.01:
                        return
                    yield
                    for h in range(2):
                        hs = slice(h * 64, (h + 1) * 64)
                        pA, pB, pC = bank(), bank(), bank()
                        S.op("pe", lambda e: e.matmul(psb[pA][:, 0:256], Kh[d][hs, cs], QR[d][hs, cl, :], start=True, stop=True),
                             reads=[("Kh", tp, d), ("QR", tp, d)], writes=[("ps", pA)])
                        S.op("pe", lambda e: e.matmul(psb[pB][:, 0:256], Bh[d][hs, cs], QR[d][hs, cl, :], start=True, stop=True),
                             reads=[("Bh", tp, d), ("QR", tp, d)], writes=[("ps", pB)])
                        S.op("pe", lambda e: e.matmul(psb[pC][:, 0:128], QR[d][hs, cl, 0:128], Bh[d][hs, cs], start=True, stop=True),
                             reads=[("Bh", tp, d), ("QR", tp, d)], writes=[("ps", pC)])
                        if cfg.get('m2cut', 99) <= 3.02:
                            continue
                        S.op("dve", lambda e: e.tensor_tensor(out=SA[:, h, :], in0=psb[pA][:, 0:256], in1=(fsm if d == 0 else bsm), op=ALU.mult),
                             reads=[("ps", pA), "cst"], writes=[("SA", ci)])
                        S.op("dve", lambda e: e.tensor_tensor(out=SB[:, h, :], in0=psb[pB][:, 0:256], in1=(fsm if d == 0 else bsm), op=ALU.mult),
                             reads=[("ps", pB), "cst"], writes=[("SB", ci)])
                        S.op("dve", lambda e: e.tensor_tensor(out=Za[0][:, h, 128:256], in0=psb[pC][:, 0:128],
                                                              in1=(cst[:, CS_BS:CS_BS + 128] if d == 0 else cst[:, CS_FS:CS_FS + 128]), op=ALU.mult),
                             reads=[("ps", pC), "cst"], writes=[("Z", ci, 0)])
                    if cfg.get('m2cut', 99) <= 3.02:
                        return
                    if cfg.get('m2cut', 99) <= 3.03:
                        return
                    if cfg.get('m2cut', 99) <= 3.1:
                        return
                    yield
                    pP = bank()
                    for h in range(2):
                        S.op("pe", lambda e, h=h: e.matmul(psb[pP][:, h * 64:(h + 1) * 64], SA[:, h, 0:128], Vtm[:, cidx, h * 64:(h + 1) * 64],
                                                           start=(h == 0), stop=(h == 1)),
                             reads=[("SA", ci), "Vtm"], writes=[("ps", pP)], inc=(h == 1))
                    S.op("act", lambda e: e.activation(out=Za[0][:, :, 0:64], in_=psb[pP][:, 0:128].rearrange("p (h v) -> p h v", h=2),
                                                       func=AF.Identity, scale=-1.0),
                         reads=[("ps", pP)], writes=[("Z", ci, 0)])
                    if cfg.get('m2cut', 99) <= 3.2:
                        return
                    yield
                    for j in range(7):
                        zi, zo = j % 2, (j + 1) % 2
                        Xj = (lambda h: SB[:, h, 0:128]) if j == 0 else (lambda h, t=Xa[j % 2]: t[:, h, :])
                        xres = ("SB", ci) if j == 0 else ("X", ci, j % 2)
                        NW = 256 if j < 5 else 128
                        pZ = bank()
                        for h in range(2):
                            S.op("pe", lambda e, h=h: e.matmul(psb[pZ][:, h * 256:h * 256 + NW], Xj(h), Za[zi][:, h, 0:NW],
                                                               start=(h == 0), stop=(h == 1)),
                                 reads=[xres, ("Z", ci, zi)], writes=[("ps", pZ)], inc=(h == 1))
                        if j < 6:
                            pX = bank()
                            for h in range(2):
                                S.op("pe", lambda e, h=h: e.matmul(psb[pX][:, h * 128:(h + 1) * 128], Za[zi][:, h, 128:256], Xj(h),
                                                                   start=(h == 0), stop=(h == 1)),
                                     reads=[xres, ("Z", ci, zi)], writes=[("ps", pX)], inc=(h == 1))
                        yield
                        sign = -1.0 if j == 0 else 1.0
                        pz3 = psb[pZ][:].rearrange("p (h x) -> p h x", h=2)
                        S.op("dve", lambda e: e.scalar_tensor_tensor(
                            out=Za[zo][:, :, 0:128], in0=pz3[:, :, 0:128], scalar=sign, in1=Za[zi][:, :, 0:128],
                            op0=ALU.mult, op1=ALU.add),
                            reads=[("ps", pZ), ("Z", ci, zi)], writes=[("Z", ci, zo)])
                        if j < 5:
                            S.op("act", lambda e: e.activation(out=Za[zo][:, :, 128:256], in_=pz3[:, :, 128:256], func=AF.Copy),
                                 reads=[("ps", pZ)], writes=[("Z", ci, zo)])
                        if j < 6:
                            S.op("act", lambda e: e.activation(out=Xa[(j + 1) % 2][:], in_=psb[pX][:, 0:256].rearrange("p (h x) -> p h x", h=2), func=AF.Copy),
                                 reads=[("ps", pX)], writes=[("X", ci, (j + 1) % 2)])
                    if cfg.get('m2cut', 99) <= 3.3:
                        return
                    yield
                    Z7 = Za[1]
                    zr = ("Z", ci, 1)
                    for h in range(2):
                        hs = slice(h * 64, (h + 1) * 64)
                        S.op("dve", lambda e: e.tensor_copy(Gpad[:, h, hs], Z7[:, h, 64:128]), reads=[zr], writes=[("Gpad", ci)])
                    yield
                    p5 = bank()
                    for h in range(2):
                        hs = slice(h * 64, (h + 1) * 64)
                        S.op("pe", lambda e: e.matmul(psb[p5][:, 0:128], Gpad[:, h, :], SB[:, h, 128:256], start=(h == 0), stop=(h == 1)),
                             reads=[("Gpad", ci), ("SB", ci)], writes=[("ps", p5)], inc=(h == 1))
                    p5b = bank()
                    for h in range(2):
                        hs = slice(h * 64, (h + 1) * 64)
                        S.op("pe", lambda e: e.matmul(psb[p5b][:, h * 64:(h + 1) * 64], Gpad[:, h, :], KdBd[:, 1, h, hs],
                                                      start=(h == 0), stop=(h == 1)),
                             reads=[("Gpad", ci), ("KdBd", ci)], writes=[("ps", p5b)], inc=(h == 1))
                    yield
                    S.op("dve", lambda e: e.scalar_tensor_tensor(out=GyTs[:, slot, :], in0=psb[p5][:, 0:128], scalar=-1.0, in1=QR[d][:, cl, 128:256],
                                                                 op0=ALU.mult, op1=ALU.add),
                         reads=[("ps", p5), ("QR", tp, d)], writes=[("GyTs", slot)])
                    S.op("act", lambda e: e.activation(out=MTs[:, slot, :], in_=psb[p5b][:, 0:128], func=AF.Identity, scale=-1.0),
                         reads=[("ps", p5b)], writes=[("MTs", slot)])
                    if cfg.get('m2cut', 99) <= 3.4:
                        return
                    yield
                    if cidx >= NCC:
                        xc = cidx - NCC
                        p6 = bank()
                        for h in range(2):
                            S.op("pe", lambda e, h=h: e.matmul(psb[p6][:, h * 64:(h + 1) * 64], SA[:, h, 128:256], Vtm[:, cidx, h * 64:(h + 1) * 64],
                                                               start=(h == 0), stop=False),
                                 reads=[("SA", ci), "Vtm"], writes=[("ps", p6)], inc=False)
                            S.op("pe", lambda e, h=h: e.matmul(psb[p6][:, h * 64:(h + 1) * 64], SB[:, h, 128:256], Z7[:, h, 0:64],
                                                               start=False, stop=(h == 1)),
                                 reads=[("SB", ci), zr], writes=[("ps", p6)], inc=(h == 1))
                        if d == 0:
                            S.op("act", lambda e: e.activation(out=Yacc[:, xc, :], in_=psb[p6][:, 0:128], func=AF.Copy),
                                 reads=[("ps", p6)], writes=[("Yacc", xc)])
                        else:
                            S.op("dve", lambda e: e.tensor_tensor(out=Yacc[:, xc, :], in0=psb[p6][:, 0:128], in1=Yacc[:, xc, :], op=ALU.add),
                                 reads=[("ps", p6), ("Yacc", xc)], writes=[("Yacc", xc)])
                    if cfg.get('m2cut', 99) <= 3.5:
                        return
                    yield
                    p7 = bank()
                    for h in range(2):
                        hs = slice(h * 64, (h + 1) * 64)
                        S.op("pe", lambda e: e.matmul(psb[p7][:, 0:64], KdBd[:, 0, h, :], Vtm[:, cidx, hs], start=(h == 0), stop=False),
                             reads=[("KdBd", ci), "Vtm"], writes=[("ps", p7)], inc=False)
                        S.op("pe", lambda e: e.matmul(psb[p7][:, 0:64], KdBd[:, 1, h, :], Z7[:, h, 0:64], start=False, stop=(h == 1)),
                             reads=[("KdBd", ci), zr], writes=[("ps", p7)], inc=(h == 1))
                    S.op("act", lambda e: e.activation(out=Sadds[:, slot, :], in_=psb[p7][:, 0:64], func=AF.Copy),
                         reads=[("ps", p7)], writes=[("Sadds", slot)])

                def tile_prep(b, hp, hb, lo, hi, off, e0, n, wk, tp):
                    nch = n // 128
                    c0 = e0 // 128
                    QR, Kh, Bh, Kd, Bd = QRl[tp], Khl[tp], Bhl[tp], Kdl[tp], Bdl[tp]
                    proj_conv(hb, lo, hi, off, e0, n, hp, rT[:, 0:n], wk, 0)
                    proj_conv(hb, lo, hi, off, e0, n, 4 + hp, kT[:, 0:n], wk, 1)
                    proj_conv(hb, lo, hi, off, e0, n, 8 + hp, vT[:, 0:n], wk, 2)
                    hpc = slice(hp * 128, (hp + 1) * 128)
                    for d in range(2):
                        ds_ = slice(d * 64, (d + 1) * 64)
                        pb = bank()
                        S.op("pe", lambda e: e.matmul(psb[pb][:, 0:n], w2t[ds_, hpc], lw[ds_, e0:e0 + n], start=True, stop=True),
                             reads=["W", "lw"], writes=[("ps", pb)])
                        S.op("act", lambda e: e.activation(out=sgw[d][:, 0:n], in_=psb[pb][:, 0:n], func=AF.Sigmoid,
                                                           bias=pvec[:, PV_W0 + d * 4 + hp:PV_W0 + d * 4 + hp + 1], scale=1.0),
                             reads=[("ps", pb), "pvec"], writes=[rn(sgw[d])])
                        pb2 = bank()
                        S.op("pe", lambda e: e.matmul(psb[pb2][:, 0:n], a2t[ds_, hpc], la[ds_, e0:e0 + n], start=True, stop=True),
                             reads=["W", "la"], writes=[("ps", pb2)])
                        S.op("act", lambda e: e.activation(out=a_d[d][:, 0:n], in_=psb[pb2][:, 0:n], func=AF.Sigmoid,
                                                           bias=pvec[:, PV_A0 + d * 4 + hp:PV_A0 + d * 4 + hp + 1], scale=1.0),
                             reads=[("ps", pb2), "pvec"], writes=[rn(a_d[d])])
                    S.op("dve", lambda e: e.tensor_scalar(kk[:, 0:n], kT[:, 0:n], pvec[:, PV_KK + hp:PV_KK + hp + 1], None, ALU.mult),
                         reads=[rn(kT), "pvec"], writes=[rn(kk)])
                    S.op("dve", lambda e: e.tensor_tensor(out=ksq[:, 0:n], in0=kk[:, 0:n], in1=kk[:, 0:n], op=ALU.mult),
                         reads=[rn(kk)], writes=["ksq"])
                    pb = bank()
                    S.op("pe", lambda e: e.matmul(psb[pb][:, 0:n], bones[:], ksq[:, 0:n], start=True, stop=True),
                         reads=["ksq", "bones"], writes=[("ps", pb)])
                    S.op("act", lambda e: e.activation(out=t0_[:, 0:n], in_=psb[pb][:, 0:n], func=AF.Sqrt, bias=epsc[:, 2:3], scale=1.0),
                         reads=[("ps", pb), "epsc"], writes=[rn(t0_)])
                    S.op("dve", lambda e: e.reciprocal(t0_[:, 0:n], t0_[:, 0:n]), reads=[rn(t0_)], writes=[rn(t0_)])
                    S.op("dve", lambda e: e.tensor_tensor(out=kk[:, 0:n], in0=kk[:, 0:n], in1=t0_[:, 0:n], op=ALU.mult),
                         reads=[rn(kk), rn(t0_)], writes=[rn(kk)])
                    for d in range(2):
                        S.op("dve", lambda e: e.tensor_scalar(t1_[:, 0:n], a_d[d][:, 0:n], pvec[:, PV_KA + hp:PV_KA + hp + 1],
                                                              omka[:, hp:hp + 1], ALU.mult, ALU.add),
                             reads=[rn(a_d[d]), "pvec", "omka"], writes=[rn(t1_)])
                        S.op("dve", lambda e: e.tensor_tensor(out=kdir[d][:, 0:n], in0=kT[:, 0:n], in1=t1_[:, 0:n], op=ALU.mult),
                             reads=[rn(kT), rn(t1_)], writes=[rn(kdir[d])])
                        S.op("dve", lambda e: e.tensor_tensor(out=b_d[d][:, 0:n], in0=kk[:, 0:n], in1=a_d[d][:, 0:n], op=ALU.mult),
                             reads=[rn(kk), rn(a_d[d])], writes=[rn(b_d[d])])
                    if e0 >= CTX:
                        S.op("dve", lambda e: e.tensor_tensor(out=t1_[:, 0:n], in0=kdir[0][:, 0:n], in1=kdir[1][:, 0:n], op=ALU.add),
                             reads=[rn(kdir[0]), rn(kdir[1])], writes=[rn(t1_)])
                        S.op("dve", lambda e: e.scalar_tensor_tensor(out=prb[:, 0:n], in0=rT[:, 0:n], scalar=pvec[:, PV_RK + hp:PV_RK + hp + 1],
                                                                     in1=t1_[:, 0:n], op0=ALU.mult, op1=ALU.mult),
                             reads=[rn(rT), rn(t1_), "pvec"], writes=["prb"])
                        pb = bank()
                        for cl in range(nch):
                            S.op("pe", lambda e, cl=cl: e.matmul(psb[pb][:, cl * 2:cl * 2 + 2], prb[:, cl * 128:(cl + 1) * 128], ind2[:],
                                                                 start=(cl == 0), stop=(cl == nch - 1)),
                                 reads=["prb", "ind2"], writes=[("ps", pb)], inc=(cl == nch - 1))
                        xc0 = c0 - NCC
                        S.op("act", lambda e: e.activation(out=BS[:, xc0:xc0 + nch, :], in_=psb[pb][:, 0:nch * 2].rearrange("p (c h) -> p c h", h=2), func=AF.Copy),
                             reads=[("ps", pb)], writes=["BS"])
                    S.op("act", lambda e: e.activation(out=vb[:, 0:n], in_=vT[:, 0:n], func=AF.Copy), reads=[rn(vT)], writes=["vb"])
                    pv_ = bank()
                    pvv = gbc('C', psb[pv_][:])
                    for cl in range(nch):
                        S.op("pe", lambda e, cl=cl: e.transpose(pvv[:, cl * 128:(cl + 1) * 128], vb[:, cl * 128:(cl + 1) * 128], gid('C')),
                             reads=["vb", "ident_bf", "cst"], writes=[("ps", pv_)], inc=(cl == nch - 1))
                    S.op("act", lambda e: e.activation(out=Vtm[:, c0:c0 + nch, :], in_=pvv[:, 0:nch * 128].rearrange("p (c x) -> p c x", c=nch), func=AF.Copy),
                         reads=[("ps", pv_)], writes=["Vtm"])
                    for d in range(2):
                        for cl in range(nch):
                            S.op("dve", lambda e, cl=cl: e.tensor_tensor_scan(Pp[:, cl * 128:(cl + 1) * 128], ones_f[:, 0:128],
                                                                             sgw[d][:, cl * 128:(cl + 1) * 128], 0.0, ALU.mult, ALU.add),
                                 reads=[rn(sgw[d]), "ones_f"], writes=[rn(Pp)])
                        P3 = Pp[:, 0:n].rearrange("p (c t) -> p c t", c=nch)
                        tot = P3[:, :, 127:128].to_broadcast([128, nch, 128])
                        L3i = Lin[:, 0:n].rearrange("p (c t) -> p c t", c=nch)
                        L3e = Lex[:, 0:n].rearrange("p (c t) -> p c t", c=nch)
                        if d == 0:
                            S.op("dve", lambda e: e.tensor_copy(Lin[:, 0:n], Pp[:, 0:n]), reads=[rn(Pp)], writes=[rn(Lin)])
                            S.op("dve", lambda e: e.tensor_tensor(out=Lex[:, 0:n], in0=Pp[:, 0:n], in1=sgw[d][:, 0:n], op=ALU.subtract),
                                 reads=[rn(Pp), rn(sgw[d])], writes=[rn(Lex)])
                        else:
                            S.op("dve", lambda e: e.tensor_tensor(out=L3e, in0=tot, in1=P3, op=ALU.subtract),
                                 reads=[rn(Pp)], writes=[rn(Lex)])
                            S.op("dve", lambda e: e.tensor_tensor(out=Lin[:, 0:n], in0=Lex[:, 0:n], in1=sgw[d][:, 0:n], op=ALU.add),
                                 reads=[rn(Lex), rn(sgw[d])], writes=[rn(Lin)])
                        S.op("act", lambda e: e.activation(out=E2[:, 0:n], in_=Lin[:, 0:n], func=AF.Exp, scale=C0), reads=[rn(Lin)], writes=[rn(E2)])
                        S.op("act", lambda e: e.activation(out=E1[:, 0:n], in_=Lin[:, 0:n], func=AF.Exp, scale=-C0), reads=[rn(Lin)], writes=[rn(E1)])
                        S.op("act", lambda e: e.activation(out=E3[:, 0:n], in_=Lex[:, 0:n], func=AF.Exp, scale=-C0), reads=[rn(Lex)], writes=[rn(E3)])
                        S.op("act", lambda e: e.activation(out=WCt[:, 0:nch], in_=P3[:, :, 127], func=AF.Exp, scale=-C0), reads=[rn(Pp)], writes=["WCt"])
                        S.op("dve", lambda e: e.tensor_copy(WCs[:, d * NCH + c0:d * NCH + c0 + nch], WCt[:, 0:nch]),
                             reads=["WCt"], writes=["WCs"])
                        S.op("dve", lambda e: e.tensor_tensor(out=E4[:, 0:n].rearrange("p (c t) -> p c t", c=nch),
                                                              in0=E2[:, 0:n].rearrange("p (c t) -> p c t", c=nch),
                                                              in1=WCt[:, 0:nch].unsqueeze(2).to_broadcast([128, nch, 128]), op=ALU.mult),
                             reads=[rn(E2), "WCt"], writes=[rn(E4)])
                        S.op("dve", lambda e: e.tensor_tensor(out=QR[d][:, 0:nch, 0:128], in0=kk[:, 0:n].rearrange("p (c t) -> p c t", c=nch),
                                                              in1=E3[:, 0:n].rearrange("p (c t) -> p c t", c=nch), op=ALU.mult),
                             reads=[rn(kk), rn(E3)], writes=[("QR", tp, d)])
                        S.op("dve", lambda e: e.tensor_tensor(out=QR[d][:, 0:nch, 128:256], in0=rT[:, 0:n].rearrange("p (c t) -> p c t", c=nch),
                                                               in1=E1[:, 0:n].rearrange("p (c t) -> p c t", c=nch), op=ALU.mult),
                             reads=[rn(rT), rn(E1)], writes=[("QR", tp, d)])
                        S.op("dve", lambda e: e.tensor_tensor(out=Kh[d][:, 0:n], in0=kdir[d][:, 0:n], in1=E2[:, 0:n], op=ALU.mult),
                             reads=[rn(kdir[d]), rn(E2)], writes=[("Kh", tp, d)])
                        S.op("dve", lambda e: e.tensor_tensor(out=Bh[d][:, 0:n], in0=b_d[d][:, 0:n], in1=E2[:, 0:n], op=ALU.mult),
                             reads=[rn(b_d[d]), rn(E2)], writes=[("Bh", tp, d)])
                        S.op("dve", lambda e: e.tensor_tensor(out=Kd[d][:, 0:n], in0=kdir[d][:, 0:n], in1=E4[:, 0:n], op=ALU.mult),
                             reads=[rn(kdir[d]), rn(E4)], writes=[("Kd", tp, d)])
                        S.op("dve", lambda e: e.tensor_tensor(out=Bd[d][:, 0:n], in0=b_d[d][:, 0:n], in1=E4[:, 0:n], op=ALU.mult),
                             reads=[rn(b_d[d]), rn(E4)], writes=[("Bd", tp, d)])

                def run_chains(b, hp, e0, n, tp):
                    nch = n // 128
                    c0 = e0 // 128
                    work = [(cl, d) for cl in range(nch) for d in range(2)]
                    for w0 in range(0, len(work), NCHN):
                        gens = [chunk_pre(b, hp, c0 + cl, cl, d, k, tp) for k, (cl, d) in enumerate(work[w0:w0 + NCHN])]
                        while gens:
                            for g in list(gens):
                                try:
                                    next(g)
                                except StopIteration:
                                    gens.remove(g)

                def seq_step(d, cidx, last):
                    slot = d * NCH + cidx
                    if cidx >= NCC:
                        xc = cidx - NCC
                        pY = bank()
                        S.op("pe", lambda e: e.matmul(psb[pY][:, 0:128], GyTs[:, slot, :], Sbd[:, d, :], start=True, stop=True),
                             reads=[("GyTs", slot), ("Sbd", d)], writes=[("ps", pY)])
                        S.op("dve", lambda e: e.tensor_tensor(out=Yacc[:, xc, :], in0=psb[pY][:, 0:128], in1=Yacc[:, xc, :], op=ALU.add),
                             reads=[("ps", pY), ("Yacc", xc)], writes=[("Yacc", xc)])
                    if last:
                        return
                    pS = bank()
                    S.op("pe", lambda e: e.matmul(psb[pS][:, 0:64], MTs[:, slot, :], Sbf[:, d, :], start=True, stop=True),
                         reads=[("MTs", slot), ("Sbf", d)], writes=[("ps", pS)])
                    S.op("dve", lambda e: e.scalar_tensor_tensor(out=S32[:, d, :], in0=S32[:, d, :], scalar=WCs[:, slot:slot + 1],
                                                                 in1=Sadds[:, slot, :], op0=ALU.mult, op1=ALU.add),
                         reads=[("S32", d), "WCs", ("Sadds", slot)], writes=[("S32", d)])
                    S.op("dve", lambda e: e.tensor_tensor(out=S32[:, d, :], in0=psb[pS][:, 0:64], in1=S32[:, d, :], op=ALU.add),
                         reads=[("ps", pS), ("S32", d)], writes=[("S32", d)])
                    S.op("act", lambda e: e.activation(out=Sbf[:, d, :], in_=S32[:, d, :], func=AF.Copy),
                         reads=[("S32", d)], writes=[("Sbf", d)])
                    for h in range(2):
                        hs = slice(h * 64, (h + 1) * 64)
                        S.op("act", lambda e: e.activation(out=Sbd[hs, d, hs], in_=S32[hs, d, :], func=AF.Copy),
                             reads=[("S32", d)], writes=[("Sbd", d)])

                def readout(b, hp):
                    NH = (NXC + 1) // 2
                    for xh in range(0, NXC, NH):
                        readout_half(b, hp, xh, min(NH, NXC - xh))
                    S.op("sp", lambda e: e.dma_start(out=yaT_d[b, hp, :, :], in_=yaT[:]), reads=["yaT"], writes=[("yaT", b)], dsem="d_yaT")

                def readout_half(b, hp, xh, NXH):
                    n2 = NXH * 2
                    Y3 = Yacc[:, xh:xh + NXH, :].rearrange("p c (h v) -> p (c h) v", h=2)
                    yc3 = yc[:, 0:NXH, :].rearrange("p c (h v) -> p (c h) v", h=2)
                    sq3 = ysq[:, 0:NXH, :].rearrange("p c (h v) -> p (c h) v", h=2)
                    S.op("dve", lambda e: e.tensor_reduce(out=st1[:, 0:n2], in_=Y3, axis=AX.X, op=ALU.add),
                         reads=[("Yacc", x) for x in range(xh, xh + NXH)], writes=["st1"])
                    S.op("dve", lambda e: e.tensor_scalar(st1[:, 0:n2], st1[:, 0:n2], 1.0 / 64.0, None, ALU.mult), reads=["st1"], writes=["st1"])
                    S.op("dve", lambda e: e.tensor_tensor(out=yc3, in0=Y3, in1=st1[:, 0:n2].unsqueeze(2).to_broadcast([128, n2, 64]), op=ALU.subtract),
                         reads=[("Yacc", x) for x in range(xh, xh + NXH)] + ["st1"], writes=["yc"])
                    S.op("pool", lambda e: e.tensor_tensor(out=ysq[:, 0:NXH, :], in0=yc[:, 0:NXH, :], in1=yc[:, 0:NXH, :], op=ALU.mult), reads=["yc"], writes=["ysq"])
                    S.op("dve", lambda e: e.tensor_reduce(out=st2[:, 0:n2], in_=sq3, axis=AX.X, op=ALU.add), reads=["ysq"], writes=["st2"])
                    S.op("act", lambda e: e.activation(out=st2[:, 0:n2], in_=st2[:, 0:n2], func=AF.Sqrt, bias=epsc[:, 1:2], scale=1.0 / 64.0),
                         reads=["st2", "epsc"], writes=["st2"])
                    S.op("dve", lambda e: e.reciprocal(st2[:, 0:n2], st2[:, 0:n2]), reads=["st2"], writes=["st2"])
                    S.op("dve", lambda e: e.tensor_tensor(out=yc3, in0=yc3, in1=st2[:, 0:n2].unsqueeze(2).to_broadcast([128, n2, 64]), op=ALU.mult),
                         reads=["yc", "st2"], writes=["yc"])
                    lg_b = brow[:, BR_LNG + hp * 128:BR_LNG + (hp + 1) * 128].unsqueeze(1).to_broadcast([128, NXH, 128])
                    lb_b = brow[:, BR_LNB + hp * 128:BR_LNB + (hp + 1) * 128].unsqueeze(1).to_broadcast([128, NXH, 128])
                    S.op("pool", lambda e: e.tensor_tensor(out=yc[:, 0:NXH, :], in0=yc[:, 0:NXH, :], in1=lg_b, op=ALU.mult), reads=["yc", "brow"], writes=["yc"])
                    S.op("pool", lambda e: e.tensor_tensor(out=yc[:, 0:NXH, :], in0=yc[:, 0:NXH, :], in1=lb_b, op=ALU.add), reads=["yc", "brow"], writes=["yc"])
                    V3 = Vtm[:, NCC + xh:NCC + xh + NXH, :].rearrange("p c (h v) -> p (c h) v", h=2)
                    S.op("dve", lambda e: e.tensor_tensor(out=sq3, in0=V3,
                                                          in1=BS[:, xh:xh + NXH, :].rearrange("p c h -> p (c h)").unsqueeze(2).to_broadcast([128, n2, 64]), op=ALU.mult),
                         reads=["Vtm", "BS"], writes=["ysq"])
                    S.op("pool", lambda e: e.tensor_tensor(out=yc[:, 0:NXH, :], in0=yc[:, 0:NXH, :], in1=ysq[:, 0:NXH, :], op=ALU.add), reads=["yc", "ysq"], writes=["yc"])
                    for x0 in range(xh, xh + NXH, 4):
                        pg = bank()
                        nx = min(4, xh + NXH - x0)
                        for xi in range(nx):
                            xc = x0 + xi
                            S.op("pe", lambda e, xi=xi, xc=xc: e.matmul(psb[pg][:, xi * 128:(xi + 1) * 128],
                                                                        lg[:, CTX + xc * 128:CTX + (xc + 1) * 128], g2t[:, hp * 128:(hp + 1) * 128],
                                                                        start=(xi == 0), stop=(xi == nx - 1)),
                                 reads=["lg", "W"], writes=[("ps", pg)], inc=(xi == nx - 1))
                        S.op("dve", lambda e: e.tensor_tensor(out=yab[:, x0:x0 + nx, :], in0=psb[pg][:, 0:nx * 128].rearrange("p (c x) -> p c x", c=nx),
                                                              in1=yc[:, x0 - xh:x0 - xh + nx, :], op=ALU.mult),
                             reads=[("ps", pg), "yc"], writes=["yab"])
                    for x0 in range(xh, xh + NXH, TPB):
                        pt = bank()
                        ptv = gbc('R', psb[pt][:])
                        nx = min(TPB, xh + NXH - x0)
                        for xi in range(nx):
                            S.op("pe", lambda e, xi=xi: e.transpose(ptv[:, xi * 128:(xi + 1) * 128], yab[:, x0 + xi, :], gid('R')),
                                 reads=["yab", "ident_bf", "cst"], writes=[("ps", pt)], inc=(xi == nx - 1))
                        S.op("act", lambda e: e.activation(out=yaT[:, x0 * 128:(x0 + nx) * 128], in_=ptv[:, 0:nx * 128], func=AF.Copy),
                             reads=[("ps", pt)], writes=["yaT"])

                tl = tile_list(True)
                wk_next = load_w(None)
                for b in range(NB):
                    btiles = [t for t in tl if t[0] == b]
                    wk_l = wk_next
                    wk_next = load_w(0)
                    for (_, r, e0, n) in btiles:
                        hb, lo, hi, off = load_h(b, e0, n)
                        proj_conv(hb, lo, hi, off, e0, n, 12, t0_[:, 0:n], wk_l, 0)
                        S.op("act", lambda e: e.activation(out=lw[:, e0:e0 + n], in_=t0_[:, 0:n], func=AF.Tanh), reads=[rn(t0_)], writes=["lw"])
                        proj_conv(hb, lo, hi, off, e0, n, 13, t1_[:, 0:n], wk_l, 1)
                        S.op("act", lambda e: e.activation(out=la[:, e0:e0 + n], in_=t1_[:, 0:n], func=AF.Copy), reads=[rn(t1_)], writes=["la"])
                        proj_conv(hb, lo, hi, off, e0, n, 14, t0_[:, 0:n], wk_l, 2)
                        S.op("act", lambda e: e.activation(out=lg[:, e0:e0 + n], in_=t0_[:, 0:n], func=AF.Sigmoid), reads=[rn(t0_)], writes=["lg"])
                    cut = cfg.get("m2cut", 99)
                    for hp in range(4):
                        if cut <= 1:
                            break
                        wk_h = wk_next
                        if hp < 3:
                            wk_next = load_w(hp + 1)
                        elif b + 1 < NB:
                            wk_next = load_w(None)
                        def prep_t(ti_):
                            (_, r_, e0_, n_) = btiles[ti_]
                            hb, lo, hi, off = load_h(b, e0_, n_)
                            tile_prep(b, hp, hb, lo, hi, off, e0_, n_, wk_h, ti_ % 2)

                        prep_t(0)
                        for ti_ in range(len(btiles)):
                            if ti_ + 1 < len(btiles):
                                prep_t(ti_ + 1)
                            if cut > 2:
                                run_chains(b, hp, btiles[ti_][2], btiles[ti_][3], ti_ % 2)
                        if cut < 4:
                            continue
                        S.op("dve", lambda e: e.memset(S32[:], 0.0), writes=[("S32", 0), ("S32", 1)])
                        S.op("dve", lambda e: e.memset(Sbf[:], 0.0), writes=[("Sbf", 0), ("Sbf", 1)])
                        S.op("pool", lambda e: e.memset(Sbd[:], 0.0), writes=[("Sbd", 0), ("Sbd", 1)])
                        fwd = list(range(NCH))
                        bwd = list(range(NCC - 1, -1, -1)) + list(range(NCH - 1, NCC - 1, -1))
                        for i in range(NCH):
                            seq_step(0, fwd[i], i == NCH - 1)
                            seq_step(1, bwd[i], i == NCH - 1)
                        if cut <= 4:
                            continue
                        readout(b, hp)
                S.barrier()
                S.emit()

        def finish():
            S.wait_all("sp")
            S.emit()
            return nc

        ffn_phase(0, f1_in_d, f1_out_d, True)
        if stop == "ffn1":
            return finish()
        if stop == "m1":
            return finish()
        if cfg.get("skip_m2") is None:
            m2_phase()
        if stop == "m2":
            return finish()
        m3_phase()
        if stop == "m3":
            return finish()
        m4_phase()
        if stop == "m4":
            return finish()
        ffn_phase(2, f2_in_d, f2_out_d, False)
        return finish()
    return nc


def _consts():
    c = np.zeros((128, NCONST), np.float32)
    i = np.arange(128)[:, None]
    t = np.arange(128)[None, :]
    c[:, CS_ID:CS_ID + 128] = (i == t)
    c[:, CS_FS:CS_FS + 128] = (i < t)
    c[:, CS_FI:CS_FI + 128] = (i <= t)
    c[:, CS_BS:CS_BS + 128] = (i > t)
    c[:, CS_BI:CS_BI + 128] = (i >= t)
    c[:, CS_BONES:CS_BONES + 128] = ((i // 64) == (t // 64))
    c[:, CS_IND + 0] = (np.arange(128) < 64)
    c[:, CS_IND + 1] = (np.arange(128) >= 64)
    return c


def _rope_tables(T):
    pos = np.arange(T)
    row = (pos // 64).astype(np.float32)
    col = (pos % 64).astype(np.float32)
    freqs = (np.float32(10000.0) ** (-np.arange(16, dtype=np.float32) / np.float32(16))).astype(np.float32)
    cosT = np.zeros((128, T), np.float32)
    sinT = np.zeros((128, T), np.float32)
    for p in range(128):
        d = p % 64
        half, j = d // 32, d % 32
        f = j % 16
        ang = ((row if half == 0 else col) * freqs[f]).astype(np.float32)
        cosT[p] = np.cos(ang)
        sinT[p] = (-np.sin(ang) if j < 16 else np.sin(ang))
    return cosT, sinT


def host_prep(inputs, cfg, core):
    T, CTX, NB = cfg["T"], cfg["CTX"], cfg["NB"]
    f = lambda a: np.ascontiguousarray(np.asarray(a, dtype=np.float32))
    b0 = core * NB
    m = {}
    m["x"] = f(inputs["x"][b0:b0 + NB])
    m["ctx"] = f(inputs["ctx"][b0:b0 + NB])
    cond = np.concatenate([np.asarray(inputs["c"])[b0:b0 + NB], np.asarray(inputs["c_ctx"])[None, :]], 0)
    m["condT"] = f(cond.reshape(NB + 1, KD, 128).transpose(2, 1, 0))
    pv = np.zeros((128, NPV), np.float32)
    pv[:, PV_ADAB:PV_ADAB + 72] = np.asarray(inputs["ada_b"])[0].reshape(72, 128).T
    pv[:, PV_PRE:PV_PRE + 24] = np.asarray(inputs["pre_norm_g"])[0].reshape(3, 8, 128).transpose(2, 0, 1).reshape(128, 24)
    pv[:, PV_POST:PV_POST + 24] = np.asarray(inputs["post_norm_g"])[0].reshape(3, 8, 128).transpose(2, 0, 1).reshape(128, 24)
    pv[:, PV_CONV:PV_CONV + 45] = np.asarray(inputs["rwkv_shift_w"])[0].reshape(3, 15, 128).transpose(2, 0, 1).reshape(128, 45)
    pv[:, PV_W0:PV_W0 + 8] = np.asarray(inputs["rwkv_w0"])[0].reshape(2, 4, 128).transpose(2, 0, 1).reshape(128, 8)
    pv[:, PV_A0:PV_A0 + 8] = np.asarray(inputs["rwkv_a0"])[0].reshape(2, 4, 128).transpose(2, 0, 1).reshape(128, 8)
    pv[:, PV_KK:PV_KK + 4] = np.asarray(inputs["rwkv_k_k"])[0].reshape(4, 128).T
    pv[:, PV_KA:PV_KA + 4] = np.asarray(inputs["rwkv_k_a"])[0].reshape(4, 128).T
    pv[:, PV_RK:PV_RK + 4] = np.asarray(inputs["rwkv_r_k"])[0].reshape(4, 128).T
    pv[:, PV_SUBG] = np.asarray(inputs["diff_subln_g"])[0]
    m["pvec"] = pv
    br = np.zeros((1, NBR), np.float32)
    br[0, BR_LNG:BR_LNG + 512] = np.asarray(inputs["rwkv_ln_g"])[0]
    br[0, BR_LNB:BR_LNB + 512] = np.asarray(inputs["rwkv_ln_b"])[0]
    br[0, BR_SUB:BR_SUB + 128] = np.asarray(inputs["diff_subln_g"])[0]
    br[0, BR_LAM:BR_LAM + 256] = np.asarray(inputs["diff_lambda"])[0].reshape(256)
    m["brow"] = br
    m["consts"] = _consts()
    m["cosT"], m["sinT"] = _rope_tables(T)
    for k in ("ada_w", "ffn1_w_in", "ffn1_w_out", "ffn2_w_in", "ffn2_w_out", "mix_w_in", "rwkv_w2", "rwkv_a2",
              "rwkv_g2", "branch_up_a", "branch_up_b", "mix_w_out"):
        m[k] = f(np.asarray(inputs[k])[0])
    perm = np.arange(1024)
    j = perm % 32
    perm = perm - j + np.where(j < 16, j + 16, j - 16)
    m["w_qk_swap"] = f(m["mix_w_in"][:, 1920 + perm])
    return m


FULL = {"T": 2048, "CTX": 256, "NB": 2}
_NC_CACHE = {}


def kernel(**inputs):
    cfg = FULL
    if "nc" not in _NC_CACHE:
        _NC_CACHE["nc"] = build(cfg)
    nc = _NC_CACHE["nc"]
    ncores = 8
    in_maps = [host_prep(inputs, cfg, c) for c in range(ncores)]
    res = run_bass_kernel_spmd(nc, in_maps, core_ids=list(range(ncores)))
    return np.concatenate([np.asarray(r["out"]) for r in res.results], axis=0).astype(np.float32)
```
